# Optimizing a Trainium2 kernel written in Bass

```python
import math
import jax, jax.numpy as jnp
from jax import lax
import numpy as np

D_MODEL = 2048
BATCH = 4
SEQ = 2048
DEPTH = 2

HEAD_DIM = 64
N_HEADS = D_MODEL // 128
N_KV = 4
HPG = N_HEADS // N_KV
NSA_W = N_HEADS * HEAD_DIM
KV_W = N_KV * HEAD_DIM
CMP_BLOCK = 32
CMP_STRIDE = 16
CMP_HIDDEN = 256
SEL_BLOCK = 64
N_SEL = 16
WINDOW = 512
Q_BLOCK = 128
SEL_Q_BLOCK = 64
GM_W = D_MODEL // 2
GM_CHUNK = 128
GM_GW = 128
GM_GROUPS = GM_W // GM_GW
N_BUCKETS = 32
MAX_DISTANCE = 128
D_FF = 5504
EPS = 1e-6
NEG = -1e30
FORCE = 1e4
OFF_KV = NSA_W
OFF_NG = OFF_KV + 6 * KV_W
OFF_UV = OFF_NG + 3 * N_HEADS
OFF_MG = OFF_UV + 2 * GM_W
IN_W = OFF_MG + 2 * D_MODEL

kernel_name = "hybrid_nsa_gmlp_macaron_block"


def rms_norm(x, g):
    x32 = x.astype(jnp.float32)
    y = x32 * lax.rsqrt(jnp.mean(x32 * x32, axis=-1, keepdims=True) + EPS)
    return (y * g.astype(jnp.float32)).astype(x.dtype)


def layer_norm(x, g, b):
    x32 = x.astype(jnp.float32)
    mu = jnp.mean(x32, axis=-1, keepdims=True)
    xc = x32 - mu
    y = xc * lax.rsqrt(jnp.mean(xc * xc, axis=-1, keepdims=True) + EPS)
    return (y * g.astype(jnp.float32) + b.astype(jnp.float32)).astype(x.dtype)


def swiglu(x, w_gate, w_up, w_down):
    return (jax.nn.silu(x @ w_gate) * (x @ w_up)) @ w_down


def t5_bucket(dist):
    n = jnp.maximum(dist, 0)
    max_exact = N_BUCKETS // 2
    nf = jnp.maximum(n, 1).astype(jnp.float32)
    large = max_exact + (jnp.log(nf / max_exact) / math.log(MAX_DISTANCE / max_exact)
                         * (N_BUCKETS - max_exact)).astype(jnp.int32)
    large = jnp.minimum(large, N_BUCKETS - 1)
    return jnp.where(n < max_exact, n, large)


def masked_softmax(logits, bias, mask):
    s = jnp.where(mask, logits.astype(jnp.float32) + bias.astype(jnp.float32), NEG)
    return jax.nn.softmax(s, axis=-1) * mask


def compress(kv, pos, w1, w2):
    B, S, G, D = kv.shape
    n_c = (S - CMP_BLOCK) // CMP_STRIDE + 1
    idx = jnp.arange(n_c)[:, None] * CMP_STRIDE + jnp.arange(CMP_BLOCK)[None, :]
    blocks = kv[:, idx] + pos[:, None, :]
    flat = blocks.transpose(0, 1, 3, 2, 4).reshape(B, n_c, G, CMP_BLOCK * D)
    return jax.nn.silu(flat @ w1) @ w2


def nsa_attention(q, k_c, v_c, k_s, v_s, k_w, v_w, gates, k_norm,
                  cmp_pos_k, cmp_pos_v, cmp_k_w1, cmp_k_w2, cmp_v_w1, cmp_v_w2, rel_bias):
    B, S = q.shape[:2]
    scale = HEAD_DIM ** -0.5
    t = jnp.arange(S)

    kc = rms_norm(compress(k_c, cmp_pos_k, cmp_k_w1, cmp_k_w2), k_norm[0])
    vc = compress(v_c, cmp_pos_v, cmp_v_w1, cmp_v_w2)
    n_c = kc.shape[1]
    c_end = jnp.arange(n_c) * CMP_STRIDE + CMP_BLOCK - 1
    dist_c = t[:, None] - c_end[None, :]
    bias_c = rel_bias[t5_bucket(dist_c)].reshape(S, n_c, N_KV, HPG).transpose(2, 3, 0, 1)
    logits_c = jnp.einsum('bsghd,bcgd->bghsc', q, kc) * scale
    p_c = masked_softmax(logits_c, bias_c, dist_c >= 0)
    o_c = jnp.einsum('bghsc,bcgd->bsghd', p_c.astype(vc.dtype), vc)

    n_s = S // SEL_BLOCK
    ci = jnp.arange(n_c)[:, None] * CMP_STRIDE
    sj = jnp.arange(n_s)[None, :] * SEL_BLOCK
    overlap = jnp.clip(jnp.minimum(ci + CMP_BLOCK, sj + SEL_BLOCK) - jnp.maximum(ci, sj), 0, None)
    overlap = overlap.astype(jnp.float32) / CMP_BLOCK
    imp = jnp.einsum('bghsc,cj->bgsj', p_c, overlap)
    blk = jnp.arange(n_s)[None, :]
    cur = (t // SEL_BLOCK)[:, None]
    forced = (blk == 0) | (blk == cur) | (blk == cur - 1)
    score = jnp.where(blk <= cur, jnp.where(forced, FORCE, imp), NEG)
    n_sel = min(N_SEL, n_s)
    _, sel_idx = lax.top_k(score, n_sel)
    sel_ok = sel_idx <= cur

    k_s_t = k_s.transpose(0, 2, 1, 3)
    v_s_t = v_s.transpose(0, 2, 1, 3)
    tbl = rel_bias.reshape(N_BUCKETS, N_KV, HPG).transpose(1, 0, 2)
    n_qb = S // SEL_Q_BLOCK
    bi = jnp.arange(B)[:, None, None, None]
    gi = jnp.arange(N_KV)[None, :, None, None]
    offs = jnp.arange(SEL_BLOCK)

    def sel_block(args):
        qb, ib, okb, tb = args
        tok = (ib[..., None] * SEL_BLOCK + offs).reshape(B, N_KV, SEL_Q_BLOCK, n_sel * SEL_BLOCK)
        kg = k_s_t[bi, gi, tok]
        vg = v_s_t[bi, gi, tok]
        dist = tb[None, None, :, None] - tok
        bias = tbl[gi, t5_bucket(dist)].transpose(0, 1, 4, 2, 3)
        mask = (dist >= 0) & jnp.repeat(okb, SEL_BLOCK, axis=-1)
        logits = jnp.einsum('bqghd,bgqkd->bghqk', qb, kg) * scale
        p = masked_softmax(logits, bias, mask[:, :, None])
        return jnp.einsum('bghqk,bgqkd->bqghd', p.astype(vg.dtype), vg)

    qs = q.reshape(B, n_qb, SEL_Q_BLOCK, N_KV, HPG, HEAD_DIM).swapaxes(0, 1)
    is_ = sel_idx.reshape(B, N_KV, n_qb, SEL_Q_BLOCK, n_sel).transpose(2, 0, 1, 3, 4)
    oks = sel_ok.reshape(B, N_KV, n_qb, SEL_Q_BLOCK, n_sel).transpose(2, 0, 1, 3, 4)
    ts = t.reshape(n_qb, SEL_Q_BLOCK)
    o_s = lax.map(sel_block, (qs, is_, oks, ts))
    o_s = o_s.swapaxes(0, 1).reshape(B, S, N_KV, HPG, HEAD_DIM)

    nb = S // Q_BLOCK
    span = WINDOW + Q_BLOCK
    pad = ((0, 0), (WINDOW, 0), (0, 0), (0, 0))
    widx = jnp.arange(nb)[:, None] * Q_BLOCK + jnp.arange(span)[None, :]
    kb = jnp.pad(k_w, pad)[:, widx]
    vb = jnp.pad(v_w, pad)[:, widx]
    qpos = t.reshape(nb, Q_BLOCK)
    kpos = widx - WINDOW
    dist_w = qpos[:, :, None] - kpos[:, None, :]
    mask_w = (dist_w >= 0) & (dist_w < WINDOW) & (kpos[:, None, :] >= 0)
    bias_w = rel_bias[t5_bucket(dist_w)].reshape(nb, Q_BLOCK, span, N_KV, HPG).transpose(0, 3, 4, 1, 2)
    logits_w = jnp.einsum('bnqghd,bnkgd->bnghqk',
                          q.reshape(B, nb, Q_BLOCK, N_KV, HPG, HEAD_DIM), kb) * scale
    p_w = masked_softmax(logits_w, bias_w, mask_w[:, None, None])
    o_w = jnp.einsum('bnghqk,bnkgd->bnqghd', p_w.astype(vb.dtype), vb).reshape(B, S, N_KV, HPG, HEAD_DIM)

    return gates[..., 0:1] * o_c + gates[..., 1:2] * o_s + gates[..., 2:3] * o_w


def spatial_gating(uv, g, b, w_s, b_s):
    B, S, _ = uv.shape
    u, v = uv[..., :GM_W], uv[..., GM_W:]
    v = layer_norm(v, g, b).reshape(B, S // GM_CHUNK, GM_CHUNK, GM_GROUPS, GM_GW)
    causal = jnp.tril(jnp.ones((GM_CHUNK, GM_CHUNK), w_s.dtype))
    mixed = jnp.einsum('gts,bcsgd->bctgd', w_s * causal, v) + b_s.T[:, :, None]
    return u * mixed.reshape(B, S, GM_W)


def mixer_block(x, mix_norm, w_in, q_norm, k_norm, cmp_pos_k, cmp_pos_v, cmp_k_w1, cmp_k_w2,
                cmp_v_w1, cmp_v_w2, sgu_norm_g, sgu_norm_b, sgu_w, sgu_b,
                w_proj_nsa, w_proj_sgu, w_out, rel_bias):
    B, S, _ = x.shape
    h = rms_norm(x, mix_norm)
    z = h @ w_in
    q = rms_norm(z[..., :OFF_KV].reshape(B, S, N_KV, HPG, HEAD_DIM), q_norm)
    kv = z[..., OFF_KV:OFF_NG].reshape(B, S, 6, N_KV, HEAD_DIM)
    k_s = rms_norm(kv[:, :, 2], k_norm[1])
    k_w = rms_norm(kv[:, :, 4], k_norm[2])
    gates = jax.nn.sigmoid(z[..., OFF_NG:OFF_UV]).reshape(B, S, N_KV, HPG, 3)
    a = nsa_attention(q, kv[:, :, 0], kv[:, :, 1], k_s, kv[:, :, 3], k_w, kv[:, :, 5], gates, k_norm,
                      cmp_pos_k, cmp_pos_v, cmp_k_w1, cmp_k_w2, cmp_v_w1, cmp_v_w2,
                      rel_bias).reshape(B, S, NSA_W)
    sg = spatial_gating(jax.nn.gelu(z[..., OFF_UV:OFF_MG]), sgu_norm_g, sgu_norm_b, sgu_w, sgu_b)
    mg = jax.nn.sigmoid(z[..., OFF_MG:])
    merged = mg[..., :D_MODEL] * (a @ w_proj_nsa) + mg[..., D_MODEL:] * (sg @ w_proj_sgu)
    return merged @ w_out


def setup_inputs(seed: int = 0) -> dict:
    key = jax.random.key(seed)
    ks = jax.random.split(key, 32)
    f32 = jnp.float32
    L = DEPTH

    def nrm(k, shape, fan_in):
        return jax.random.normal(k, shape, f32) * (fan_in ** -0.5)

    def gain(k, shape):
        return 1.0 + 0.05 * jax.random.normal(k, shape, f32)

    return {
        "x": jax.random.normal(ks[0], (BATCH, SEQ, D_MODEL), f32),
        "rel_bias": 0.5 * jax.random.normal(ks[1], (N_BUCKETS, N_HEADS), f32),
        "ffn1_norm": gain(ks[2], (L, D_MODEL)),
        "ffn1_w_gate": nrm(ks[3], (L, D_MODEL, D_FF), D_MODEL),
        "ffn1_w_up": nrm(ks[4], (L, D_MODEL, D_FF), D_MODEL),
        "ffn1_w_down": nrm(ks[5], (L, D_FF, D_MODEL), D_FF),
        "mix_norm": gain(ks[6], (L, D_MODEL)),
        "w_in": nrm(ks[7], (L, D_MODEL, IN_W), D_MODEL),
        "q_norm": gain(ks[8], (L, HEAD_DIM)),
        "k_norm": gain(ks[9], (L, 3, HEAD_DIM)),
        "cmp_pos_k": 0.5 * jax.random.normal(ks[10], (L, CMP_BLOCK, HEAD_DIM), f32),
        "cmp_pos_v": 0.5 * jax.random.normal(ks[11], (L, CMP_BLOCK, HEAD_DIM), f32),
        "cmp_k_w1": nrm(ks[12], (L, CMP_BLOCK * HEAD_DIM, CMP_HIDDEN), CMP_BLOCK * HEAD_DIM),
        "cmp_k_w2": nrm(ks[13], (L, CMP_HIDDEN, HEAD_DIM), CMP_HIDDEN),
        "cmp_v_w1": nrm(ks[14], (L, CMP_BLOCK * HEAD_DIM, CMP_HIDDEN), CMP_BLOCK * HEAD_DIM),
        "cmp_v_w2": nrm(ks[15], (L, CMP_HIDDEN, HEAD_DIM), CMP_HIDDEN),
        "sgu_norm_g": gain(ks[16], (L, GM_W)),
        "sgu_norm_b": 0.02 * jax.random.normal(ks[17], (L, GM_W), f32),
        "sgu_w": nrm(ks[18], (L, GM_GROUPS, GM_CHUNK, GM_CHUNK), GM_CHUNK),
        "sgu_b": 1.0 + 0.1 * jax.random.normal(ks[19], (L, GM_GROUPS, GM_CHUNK), f32),
        "w_proj_nsa": nrm(ks[20], (L, NSA_W, D_MODEL), NSA_W),
        "w_proj_sgu": nrm(ks[21], (L, GM_W, D_MODEL), GM_W),
        "w_out": nrm(ks[22], (L, D_MODEL, D_MODEL), D_MODEL),
        "ffn2_norm": gain(ks[23], (L, D_MODEL)),
        "ffn2_w_gate": nrm(ks[24], (L, D_MODEL, D_FF), D_MODEL),
        "ffn2_w_up": nrm(ks[25], (L, D_MODEL, D_FF), D_MODEL),
        "ffn2_w_down": nrm(ks[26], (L, D_FF, D_MODEL), D_FF),
    }


def reference(x, rel_bias, ffn1_norm, ffn1_w_gate, ffn1_w_up, ffn1_w_down, mix_norm, w_in,
              q_norm, k_norm, cmp_pos_k, cmp_pos_v, cmp_k_w1, cmp_k_w2, cmp_v_w1, cmp_v_w2,
              sgu_norm_g, sgu_norm_b, sgu_w, sgu_b, w_proj_nsa, w_proj_sgu, w_out,
              ffn2_norm, ffn2_w_gate, ffn2_w_up, ffn2_w_down):
    for l in range(DEPTH):
        x = x + 0.5 * swiglu(rms_norm(x, ffn1_norm[l]), ffn1_w_gate[l], ffn1_w_up[l], ffn1_w_down[l])
        x = x + mixer_block(x, mix_norm[l], w_in[l], q_norm[l], k_norm[l], cmp_pos_k[l], cmp_pos_v[l],
                            cmp_k_w1[l], cmp_k_w2[l], cmp_v_w1[l], cmp_v_w2[l],
                            sgu_norm_g[l], sgu_norm_b[l], sgu_w[l], sgu_b[l],
                            w_proj_nsa[l], w_proj_sgu[l], w_out[l], rel_bias)
        x = x + 0.5 * swiglu(rms_norm(x, ffn2_norm[l]), ffn2_w_gate[l], ffn2_w_up[l], ffn2_w_down[l])
    return x
```

```python
import math
from contextlib import ExitStack
import numpy as np
import concourse.bass as bass
import concourse.mybir as mybir
from concourse.bass_utils import run_bass_kernel_spmd

F32 = mybir.dt.float32
BF16 = mybir.dt.bfloat16
ALU = mybir.AluOpType
AF = mybir.ActivationFunctionType
AX = mybir.AxisListType

D = 2048
S = 2048
L = 2
DFF = 5504
NFC = 43
INW = 8752
OFF_KV, OFF_NG, OFF_UV, OFF_MG = 1024, 2560, 2608, 4656
EPS = 1e-6
NEGB = -30000.0
NCORES = 4

ENGS = ["sync", "gpsimd", "scalar", "vector", "tensor"]
NDSEM = 24


class Trk:
    __slots__ = ("w", "r", "x")

    def __init__(self, x=False):
        self.w = None
        self.r = {}
        self.x = x


class V:
    __slots__ = ("ap", "trks")

    def __init__(self, ap, trks):
        self.ap = ap
        self.trks = trks


class Buf:
    def __init__(self, h, excl=False):
        self.h = h
        self.t = {}
        self.excl = excl

    def __call__(self, ap, *keys):
        if not keys:
            keys = (0,)
        out = []
        for k in keys:
            t = self.t.get(k)
            if t is None:
                t = self.t[k] = Trk(self.excl)
            out.append(t)
        return V(ap, out)


class Prog:
    def __init__(self, nc):
        self.nc = nc
        self.ops = {e: [] for e in ENGS}
        self.seen = {e: {} for e in ENGS}
        self.seen_dma = {e: set() for e in ENGS}
        self.ndma = {e: 0 for e in ENGS}

    def _dep(self, eng, waits, d):
        if d[0] == "dma":
            if d in self.seen_dma[eng]:
                return
            self.seen_dma[eng].add(d)
            waits.append(d)
        else:
            e, i = d
            if e == eng and eng in ("tensor", "sync"):
                return
            if self.seen[eng].get(e, -1) >= i:
                return
            self.seen[eng][e] = i
            self.ops[e][i]["inc"] = True
            waits.append(d)

    def op(self, eng, fn, reads=(), writes=(), dma=False):
        idx = len(self.ops[eng])
        waits = []
        deps = []
        rd, wr = [], []
        for v in reads:
            for t in v.trks:
                (wr if t.x else rd).append(t)
        for v in writes:
            wr.extend(v.trks)
        for t in rd:
            if t.w is not None:
                deps.append(t.w)
        for t in wr:
            if t.w is not None:
                deps.append(t.w)
            deps.extend(t.r.values())
        if dma:
            n = self.ndma[eng]
            self.ndma[eng] += 1
            ev = ("dma", eng, n)
            if n >= NDSEM:
                deps.append(("dma", eng, n - NDSEM))
        else:
            ev = (eng, idx)
        for d in deps:
            self._dep(eng, waits, d)
        self.ops[eng].append({"fn": fn, "waits": waits, "inc": False, "dma": ev if dma else None})
        for t in rd:
            t.r[ev if dma else eng] = ev
        for t in wr:
            t.w = ev
            t.r = {}
        return ev

    def barrier(self):
        evs = []
        for e in ENGS:
            if self.ops[e]:
                last = len(self.ops[e]) - 1
                while last >= 0 and self.ops[e][last]["fn"] is None:
                    last -= 1
                if last >= 0 and self.ops[e][last]["dma"] is None:
                    evs.append((e, last))
            n = self.ndma[e]
            for k in range(max(0, n - NDSEM), n):
                evs.append(("dma", e, k))
        for e in ENGS:
            waits = []
            for d in evs:
                if d[0] != "dma" and d[0] == e:
                    continue
                self._dep(e, waits, d)
            self.ops[e].append({"fn": None, "waits": waits, "inc": False, "dma": None})

    def emit(self, stack):
        nc = self.nc
        esem = {e: stack.enter_context(nc.semaphore("es_" + e)) for e in ENGS}
        dsem = {e: [stack.enter_context(nc.semaphore("ds_%s_%d" % (e, i))) for i in range(NDSEM)]
                for e in ENGS if self.ndma[e] > 0}
        for e in ENGS:
            c = 0
            for o in self.ops[e]:
                if o["inc"]:
                    c += 1
                o["cnt"] = c
        block = stack.enter_context(nc.Block())
        ops = self.ops

        def run(engobj, e):
            for o in ops[e]:
                for d in o["waits"]:
                    if d[0] == "dma":
                        _, q, n = d
                        engobj.wait_ge(dsem[q][n % NDSEM], 16 * (n // NDSEM + 1))
                    else:
                        engobj.wait_ge(esem[d[0]], ops[d[0]][d[1]]["cnt"])
                if o["fn"] is None:
                    continue
                ins = o["fn"](engobj)
                if o["dma"] is not None:
                    _, q, n = o["dma"]
                    ins.then_inc(dsem[q][n % NDSEM], 16)
                elif o["inc"]:
                    ins.then_inc(esem[e], 1)

        @block.sync
        def _(x):
            run(x, "sync")

        @block.gpsimd
        def _(x):
            run(x, "gpsimd")

        @block.scalar
        def _(x):
            run(x, "scalar")

        @block.vector
        def _(x):
            run(x, "vector")

        @block.tensor
        def _(x):
            run(x, "tensor")

    def dma(self, q, out, in_):
        return self.op(q, lambda e: e.dma_start(out=out.ap, in_=in_.ap), [in_], [out], dma=True)

    def mm(self, out, lhsT, rhs, start=True, stop=True):
        return self.op("tensor", lambda e: e.matmul(out.ap, lhsT.ap, rhs.ap, start=start, stop=stop),
                       [lhsT, rhs], [out])

    def transpose(self, out, in_, ident):
        return self.op("tensor", lambda e: e.transpose(out.ap, in_.ap, ident.ap), [in_, ident], [out])

    def act(self, out, in_, func, bias=None, scale=None):
        reads = [in_]
        kw = {}
        if bias is not None:
            reads.append(bias)
            kw["bias"] = bias.ap
        if scale is not None:
            kw["scale"] = scale
        return self.op("scalar", lambda e: e.activation(out.ap, in_.ap, func, **kw), reads, [out])

    def tt(self, eng, out, in0, in1, op):
        return self.op(eng, lambda e: e.tensor_tensor(out.ap, in0.ap, in1.ap, op), [in0, in1], [out])

    def ts(self, eng, out, in0, s1, s2, op0, op1=None):
        reads = [in0]
        a1, a2 = s1, s2
        if isinstance(s1, V):
            reads.append(s1)
            a1 = s1.ap
        if isinstance(s2, V):
            reads.append(s2)
            a2 = s2.ap
        kw = {}
        if op1 is not None:
            kw["op1"] = op1
        return self.op(eng, lambda e: e.tensor_scalar(out.ap, in0.ap, a1, a2, op0, **kw), reads, [out])

    def stt(self, eng, out, in0, scalar, in1, op0, op1):
        reads = [in0, in1]
        a = scalar
        if isinstance(scalar, V):
            reads.append(scalar)
            a = scalar.ap
        return self.op(eng, lambda e: e.scalar_tensor_tensor(out.ap, in0.ap, a, in1.ap, op0, op1), reads, [out])

    def copy(self, eng, out, in_):
        if eng == "scalar":
            return self.op(eng, lambda e: e.copy(out.ap, in_.ap), [in_], [out])
        return self.op(eng, lambda e: e.tensor_copy(out.ap, in_.ap), [in_], [out])

    def memset(self, eng, out, val):
        return self.op(eng, lambda e: e.memset(out.ap, val), [], [out])

    def recip(self, out, in_):
        return self.op("vector", lambda e: e.reciprocal(out.ap, in_.ap), [in_], [out])


def _t5_bucket(dist):
    n = np.maximum(dist, 0)
    nf = np.maximum(n, 1).astype(np.float32)
    large = 16 + (np.log(nf / np.float32(16)) / np.float32(math.log(128 / 16)) * np.float32(16)).astype(np.int32)
    large = np.minimum(large, 31)
    return np.where(n < 16, n, large)


CST = {}


def _build_consts():
    cols = []
    off = [0]

    def add(name, arr):
        arr = np.asarray(arr, np.float32)
        a = np.zeros((128, arr.shape[1]), np.float32)
        a[:arr.shape[0]] = arr
        CST[name] = (off[0], arr.shape[1])
        off[0] += arr.shape[1]
        cols.append(a)

    idx = np.arange(4096)
    dist = idx - 2048
    oh = np.zeros((33, 4096), np.float32)
    bk = _t5_bucket(dist)
    for i in range(4096):
        if dist[i] < 0:
            oh[32, i] = 1
        else:
            oh[bk[i], i] = 1
    CST["_oh"] = oh
    add("ident", np.eye(128))
    add("j128", np.eye(128)[::-1])
    j127 = np.zeros((128, 128))
    j127[:127, :127] = np.eye(127)[::-1]
    add("j127", j127)
    s_ = np.arange(128)[:, None]
    t_ = np.arange(128)[None, :]
    add("triu", (s_ <= t_))
    expm = np.zeros((32, 16, 128))
    for jc in range(16):
        for k in range(128):
            expm[2 * jc + k // 64, jc, k] = 1
    CST["_expm"] = expm.reshape(32, -1).astype(np.float32)
    ci = np.arange(127)[:, None] * 16
    sj = np.arange(32)[None, :] * 64
    ov = np.clip(np.minimum(ci + 32, sj + 64) - np.maximum(ci, sj), 0, None).astype(np.float32) / 32
    add("ov", ov)
    notf = np.zeros((128, 16, 32))
    addv = np.zeros((128, 16, 32))
    vneg = np.zeros((128, 16, 32))
    for i in range(16):
        for p in range(128):
            cur = (128 * i + p) // 64
            for j in range(32):
                if j > cur:
                    addv[p, i, j] = -1e30
                    vneg[p, i, j] = NEGB
                elif j == 0:
                    addv[p, i, j] = 1e4
                elif j == cur:
                    addv[p, i, j] = 2e4
                elif j == cur - 1:
                    addv[p, i, j] = 3e4
                else:
                    notf[p, i, j] = 1
    add("notf", notf.reshape(128, -1))
    add("addv", addv.reshape(128, -1))
    add("vneg", vneg.reshape(128, -1))
    m6 = (t_ < s_).astype(np.float32)
    add("m6", m6)
    add("n6", (m6 - 1) * 30000.0)
    return np.concatenate(cols, axis=1)


CST_ARR = _build_consts()
NCST = CST_ARR.shape[1]
PRM = {"n1": 0, "nm": 16, "n2": 32, "qn": 48, "kn": 49, "pk": 52, "pv": 84}
NPRM = 116
OH_ARR = CST["_oh"]
EXPM_ARR = CST["_expm"]


def build_program(depth=L, dbg=None, phases="1m2", mstop=None, ntiles=16, tstop=8):
    nc = bass.Bass("TRN2", target_bir_lowering=False)
    dr = {}

    def din(name, shape, dt=F32):
        dr[name] = Buf(nc.dram_tensor(name, list(shape), dt, kind="ExternalInput"))
        return dr[name]

    x_in = din("x_fm", [16, 128, S])
    cst_d = din("cst", [128, NCST])
    prm_d = din("prm", [L, 128, NPRM])
    rb_d = din("rbp", [32, 16])
    oh_d = din("oh", [33, 4096])
    expm_d = din("expm", [32, 2048])
    sgub_d = din("sgub", [L, 1, 1024])
    sgb_d = din("sgb", [L, 2, 128, 1024])
    sgw_d = din("sgwT", [L, 128, 8, 128])
    wnames = {"ffn1_w_gate": [L, D, DFF], "ffn1_w_up": [L, D, DFF], "ffn1_w_down": [L, DFF, D],
              "ffn2_w_gate": [L, D, DFF], "ffn2_w_up": [L, D, DFF], "ffn2_w_down": [L, DFF, D],
              "w_in": [L, D, INW], "cmp_k_w1": [L, 2048, 256], "cmp_k_w2": [L, 256, 64],
              "cmp_v_w1": [L, 2048, 256], "cmp_v_w2": [L, 256, 64],
              "w_proj_nsa": [L, 1024, D], "w_proj_sgu": [L, 1024, D], "w_out": [L, D, D]}
    for k, shp in wnames.items():
        din(k, shp)
    out_d = Buf(nc.dram_tensor("out_fm", [16, 128, S], F32, kind="ExternalOutput"))
    xs_d = Buf(nc.dram_tensor("xs", [16, 128, S], F32))
    fd_d = Buf(nc.dram_tensor("fdt", [16, 4096], BF16))
    wc16_d = Buf(nc.dram_tensor("wc16", [76, 128, 16, 128], BF16))
    wc8_d = Buf(nc.dram_tensor("wc8", [32, 128, 8, 128], BF16))
    wcv_d = Buf(nc.dram_tensor("wcv", [128, 16, 560], BF16))
    wcv2_d = Buf(nc.dram_tensor("wcv2", [2, 128, 16, 512], BF16))
    dbg_d = {}
    if dbg:
        for name, shp in dbg.items():
            dbg_d[name] = Buf(nc.dram_tensor("dbg_" + name, list(shp), F32, kind="ExternalOutput"))

    P = Prog(nc)
    _NC_CACHE['P'] = P
    out_events = []

    with ExitStack() as top:
        top.enter_context(nc.allow_low_precision("bf16 matmul operands, fp32 accumulation"))

        uniq = [0]

        def sbt(st, name, shape, dt):
            uniq[0] += 1
            return Buf(st.enter_context(nc.sbuf_tensor("%s_%d" % (name, uniq[0]), list(shape), dt)))

        banks = [Buf(top.enter_context(nc.psum_tensor("pb%d" % i, [128, 512], F32)), excl=True) for i in range(8)]
        bank_i = [0]
        bank_n = [8]

        def nb():
            b = banks[bank_i[0] % bank_n[0]]
            bank_i[0] += 1
            return b

        cst = sbt(top, "cst_sb", [128, NCST], F32)
        P.dma("sync", cst(cst.h[:, :]), cst_d(cst_d.h.ap()))

        def cv(name, rows=128, lo=0, n=None):
            o, w = CST[name]
            if n is None:
                n = w - lo
            return cst(cst.h[0:rows, o + lo:o + lo + n])

        prm = sbt(top, "prm_sb", [128, L, NPRM], F32)
        P.dma("sync", prm(prm.h[:, :, :]), prm_d(prm_d.h.ap().rearrange("l p n -> p l n")))
        ones_bf = sbt(top, "ones_bf", [128, 128], BF16)
        P.memset("vector", ones_bf(ones_bf.h[:, :]), 1.0)
        blk_bf = sbt(top, "blk_bf", [128, 128], BF16)
        P.memset("vector", blk_bf(blk_bf.h[:, :]), 0.0)
        P.memset("vector", blk_bf(blk_bf.h[0:64, 0:64]), 1.0)
        P.memset("vector", blk_bf(blk_bf.h[64:128, 64:128]), 1.0)
        ident_bf = sbt(top, "ident_bf", [128, 128], BF16)
        P.copy("vector", ident_bf(ident_bf.h[:, :]), cv("ident"))
        epsc = sbt(top, "epsc", [128, 2], F32)
        P.memset("vector", epsc(epsc.h[:, 0:1]), EPS)
        P.memset("vector", epsc(epsc.h[:, 1:2]), 0.0)
        gsc = sbt(top, "gsc", [128, L, 52], F32)
        for l in range(L):
            P.copy("vector", gsc(gsc.h[:, l, 0:49]), prm(prm.h[:, l, 0:49]))
            P.ts("vector", gsc(gsc.h[:, l, 49:50]), prm(prm.h[:, l, 48:49]), 0.125, None, ALU.mult)

        def pcol(l, name, c=0):
            return gsc(gsc.h[:, l, PRM[name] + c:PRM[name] + c + 1])

        def make_h(st_x, st_h, xsrc, l, nname, tok0, hcol0, sq, sd):
            xt, xo = st_x
            ht, ho = st_h
            P.dma("sync", xt(xt.h[:, :, xo:xo + 512], ("x", xo)),
                  xsrc(xsrc.h.ap()[:, :, tok0:tok0 + 512].rearrange("c p t -> p c t"), tok0 // 512))
            psn = nb()
            for c in range(16):
                s = sq[c % 2]
                P.act(s(s.h[:, :]), xt(xt.h[:, c, xo:xo + 512], ("x", xo)), AF.Square)
                P.mm(psn(psn.h[:, :]), ones_bf(ones_bf.h[:, :]), s(s.h[:, :]), start=(c == 0), stop=(c == 15))
            P.act(sd(sd.h[:, :]), psn(psn.h[:, :]), AF.Sqrt, bias=epsc(epsc.h[:, 0:1]), scale=1.0 / D)
            P.recip(sd(sd.h[:, :]), sd(sd.h[:, :]))
            for c in range(16):
                P.stt("vector", ht(ht.h[:, c, ho:ho + 512], ("h", ho)), xt(xt.h[:, c, xo:xo + 512], ("x", xo)),
                      pcol(l, nname, c), sd(sd.h[:, :]), ALU.mult, ALU.mult)

        def wfetch(tb, wt_view, cbuf, cap, ckey, loaders):
            if tb == 0:
                for d_, s_ in loaders:
                    P.dma("gpsimd", d_, s_)
                P.dma("sync", cbuf(cap, ckey), wt_view)
            else:
                P.dma("sync", wt_view, cbuf(cap, ckey))

        def wload(wt, src_ap):
            P.dma("gpsimd", wt(wt.h[:, :, :]) if len(wt.h.shape) == 3 else wt(wt.h[:, :]), src_ap)

        def ffn(l, which, xsrc, xdst):
            wg_d, wu_d, wd_d = dr[which + "_w_gate"], dr[which + "_w_up"], dr[which + "_w_down"]
            nname = "n1" if which == "ffn1" else "n2"
            with ExitStack() as st:
                xt = sbt(st, "f_x", [128, 16, 1024], F32)
                ht = sbt(st, "f_h", [128, 16, 1024], BF16)
                act = sbt(st, "f_act", [128, 22, 1024], BF16)
                sq = [sbt(st, "f_sq%d" % i, [128, 512], BF16) for i in range(2)]
                sd = sbt(st, "f_sd", [128, 512], F32)
                wg = [sbt(st, "f_wg%d" % i, [128, 16, 128], BF16) for i in range(2)]
                wu = [sbt(st, "f_wu%d" % i, [128, 16, 128], BF16) for i in range(2)]
                wd = [sbt(st, "f_wd%d" % i, [128, 22, 256], BF16) for i in range(2)]
                sl = [sbt(st, "f_sl%d" % i, [128, 512], BF16) for i in range(2)]
                wi = 0
                di = 0
                for tt in range(2):
                    for tb in range(2):
                        make_h((xt, tb * 512), (ht, tb * 512), xsrc, l, nname, tt * 1024 + tb * 512, 0, sq, sd)
                    for fh in range(2):
                        fcs = list(range(0, 22)) if fh == 0 else list(range(22, 43))
                        for fi, fc in enumerate(fcs):
                            g, u = wg[wi % 2], wu[wi % 2]
                            wi += 1
                            P.dma("gpsimd", g(g.h[:, :, :]),
                                  wg_d(wg_d.h.ap()[l, :, fc * 128:(fc + 1) * 128].rearrange("(c p) f -> p c f", p=128)))
                            P.dma("gpsimd", u(u.h[:, :, :]),
                                  wu_d(wu_d.h.ap()[l, :, fc * 128:(fc + 1) * 128].rearrange("(c p) f -> p c f", p=128)))
                            for tb in range(2):
                                pg, pu = nb(), nb()
                                for c in range(16):
                                    P.mm(pg(pg.h[:, :]), g(g.h[:, c, :]), ht(ht.h[:, c, tb * 512:(tb + 1) * 512], ("h", tb * 512)),
                                         start=(c == 0), stop=(c == 15))
                                for c in range(16):
                                    P.mm(pu(pu.h[:, :]), u(u.h[:, c, :]), ht(ht.h[:, c, tb * 512:(tb + 1) * 512], ("h", tb * 512)),
                                         start=(c == 0), stop=(c == 15))
                                s = sl[(fi * 2 + tb) % 2]
                                P.act(s(s.h[:, :]), pg(pg.h[:, :]), AF.Silu)
                                P.tt("vector", act(act.h[:, fi, tb * 512:(tb + 1) * 512], (fi, tb)), s(s.h[:, :]), pu(pu.h[:, :]), ALU.mult)
                        nf = len(fcs)
                        for dcp in range(8):
                            w = wd[di % 2]
                            di += 1
                            P.dma("gpsimd", w(w.h[:, 0:nf, :]),
                                  wd_d(wd_d.h.ap()[l, fcs[0] * 128:(fcs[-1] + 1) * 128, dcp * 256:(dcp + 1) * 256]
                                       .rearrange("(f p) d -> p f d", p=128)))
                            for ds in range(2):
                                dc = dcp * 2 + ds
                                for tb in range(2):
                                    pd = nb()
                                    for fi in range(nf):
                                        P.mm(pd(pd.h[:, :]), w(w.h[:, fi, ds * 128:(ds + 1) * 128]),
                                             act(act.h[:, fi, tb * 512:(tb + 1) * 512], (fi, tb)), start=(fi == 0), stop=(fi == nf - 1))
                                    xv = xt(xt.h[:, dc, tb * 512:(tb + 1) * 512], ("x", tb * 512))
                                    P.stt("vector", xv, pd(pd.h[:, :]), 0.5, xv, ALU.mult, ALU.add)
                    ev = P.dma("sync", xdst(xdst.h.ap()[:, :, tt * 1024:(tt + 1) * 1024].rearrange("c p t -> p c t"), 2 * tt, 2 * tt + 1),
                               xt(xt.h[:, :, :], ("x", 0), ("x", 512)))
                    if xdst is out_d:
                        out_events.append(ev)
            P.barrier()

        def build_tables(st):
            dall = sbt(st, "dall", [128, 7, 16, 128], BF16)
            with ExitStack() as s2:
                rbx = sbt(s2, "rbx", [33, 16], F32)
                rbb = sbt(s2, "rbb", [33, 16], BF16)
                ohb = sbt(s2, "ohb", [33, 4096], BF16)
                fsb = sbt(s2, "fsb", [16, 4096], BF16)
                hk = sbt(s2, "hk", [128, 16, 128], BF16)
                P.memset("vector", rbx(rbx.h[:, :]), NEGB)
                P.dma("sync", rbx(rbx.h[0:32, :]), rb_d(rb_d.h.ap()))
                P.copy("vector", rbb(rbb.h[:, :]), rbx(rbx.h[:, :]))
                P.dma("gpsimd", ohb(ohb.h[:, :]), oh_d(oh_d.h.ap()))
                for n in range(8):
                    pb = nb()
                    P.mm(pb(pb.h[0:16, :]), rbb(rbb.h[:, :]), ohb(ohb.h[:, n * 512:(n + 1) * 512]))
                    P.copy("vector", fsb(fsb.h[:, n * 512:(n + 1) * 512]), pb(pb.h[0:16, :]))
                P.dma("sync", fd_d(fd_d.h.ap()), fsb(fsb.h[:, :]))
                j128 = sbt(s2, "j128b", [128, 128], BF16)
                P.copy("vector", j128(j128.h[:, :]), cv("j128"))
                for dl in range(6):
                    src = bass.AP(fd_d.h, 2048 + dl * 128 - 127, [[1, 128], [4096, 16], [1, 128]])
                    P.dma("sync", hk(hk.h[:, :, :]), fd_d(src))
                    for n in range(4):
                        pb = nb()
                        P.mm(pb(pb.h[:, :]), j128(j128.h[:, :]), hk(hk.h[:, n * 4:(n + 1) * 4, :]))
                        P.copy("vector", dall(dall.h[:, dl, n * 4:(n + 1) * 4, :]), pb(pb.h[:, :]))
                for h in range(16):
                    P.tt("vector", dall(dall.h[:, 6, h, :]), dall(dall.h[:, 5, h, :]), cv("m6"), ALU.mult)
                    P.tt("vector", dall(dall.h[:, 6, h, :]), dall(dall.h[:, 6, h, :]), cv("n6"), ALU.add)
            P.barrier()
            return dall

        def mixer(l, xsrc, xdst):
            w_in = dr["w_in"]

            def wcols(c0, n):
                return w_in(w_in.h.ap()[l, :, c0:c0 + n].rearrange("(c p) f -> p c f", p=128))

            with ExitStack() as sq_:
                qT = sbt(sq_, "m_qT", [128, 8, S], BF16)
                with ExitStack() as st:
                    ksd = sbt(st, "m_ksd", [128, 4, S], BF16)
                    kwd = sbt(st, "m_kwd", [128, 4, S], BF16)
                    kcT = sbt(st, "m_kcT", [128, 2, S], BF16)
                    vcT = sbt(st, "m_vcT", [128, 2, S], BF16)
                    vsa = sbt(st, "m_vsa", [128, 16, 4, 65], BF16)
                    vwa = sbt(st, "m_vwa", [128, 16, 4, 65], BF16)
                    gat = sbt(st, "m_gat", [128, 16, 48], F32)
                    P.memset("vector", vsa(vsa.h[:, :, :, 64:65]), 1.0)
                    P.memset("vector", vwa(vwa.h[:, :, :, 64:65]), 1.0)
                    with ExitStack() as sa:
                        xt = sbt(sa, "a_x", [128, 16, 512], F32)
                        ht = sbt(sa, "a_h", [128, 16, 512], BF16)
                        sq = [sbt(sa, "a_sq%d" % i, [128, 512], BF16) for i in range(2)]
                        sd = sbt(sa, "a_sd", [128, 512], F32)
                        rs = sbt(sa, "a_rs", [128, 512], F32)
                        wq = [sbt(sa, "a_wq%d" % i, [128, 16, 128], BF16) for i in range(3)]
                        wv = sbt(sa, "a_wv", [128, 16, 560], BF16)
                        wi = 0
                        for tb in range(4):
                            t0 = tb * 512
                            make_h((xt, 0), (ht, 0), xsrc, l, "nm", t0, 0, sq, sd)
                            hv = lambda c: ht(ht.h[:, c, :], ("h", 0))
                            jobs = [("q", c) for c in range(8)] + [("ks", g) for g in range(4)] + \
                                   [("kw", g) for g in range(4)] + [("kc", c) for c in range(2)] + [("vc", c) for c in range(2)]
                            for jn, (kind, ix) in enumerate(jobs):
                                w = wq[wi % 3]
                                wi += 1
                                wall = w(w.h[:, :, :])
                                if kind == "q":
                                    lds = [(wall, wcols(ix * 128, 128))]
                                elif kind in ("ks", "kw"):
                                    c0 = OFF_KV + (2 if kind == "ks" else 4) * 256 + ix * 64
                                    lds = [(w(w.h[:, :, 0:64]), wcols(c0, 64)), (w(w.h[:, :, 64:128]), wcols(c0, 64))]
                                else:
                                    c0 = OFF_KV + (0 if kind == "kc" else 1) * 256 + ix * 128
                                    lds = [(wall, wcols(c0, 128))]
                                wfetch(tb, wall, wc16_d, wc16_d.h.ap()[jn], jn, lds)
                                wvw = wall
                                pz = nb()
                                for c in range(16):
                                    P.mm(pz(pz.h[:, :]), V(w.h[:, c, :], wvw.trks), hv(c), start=(c == 0), stop=(c == 15))
                                if kind in ("kc", "vc"):
                                    dst = kcT if kind == "kc" else vcT
                                    P.copy("scalar", dst(dst.h[:, ix, t0:t0 + 512], tb), pz(pz.h[:, :]))
                                    continue
                                s = sq[wi % 2]
                                P.act(s(s.h[:, :]), pz(pz.h[:, :]), AF.Square)
                                pm = nb()
                                P.mm(pm(pm.h[:, :]), blk_bf(blk_bf.h[:, :]), s(s.h[:, :]))
                                P.act(rs(rs.h[:, :]), pm(pm.h[:, :]), AF.Sqrt, bias=epsc(epsc.h[:, 0:1]), scale=1.0 / 64)
                                P.recip(rs(rs.h[:, :]), rs(rs.h[:, :]))
                                if kind == "q":
                                    dstv = qT(qT.h[:, ix, t0:t0 + 512], tb)
                                    gc = gsc(gsc.h[:, l, 49:50])
                                    P.stt("vector", dstv, pz(pz.h[:, :]), gc, rs(rs.h[:, :]), ALU.mult, ALU.mult)
                                else:
                                    dst = ksd if kind == "ks" else kwd
                                    kn = 1 if kind == "ks" else 2
                                    gc = prm(prm.h[:, l, PRM["kn"] + kn:PRM["kn"] + kn + 1])
                                    P.stt("vector", dst(dst.h[:, ix, t0:t0 + 512], tb), pz(pz.h[:, :]), gc, rs(rs.h[:, :]), ALU.mult, ALU.mult)
                            wfetch(tb, wv(wv.h[:, :, :]), wcv_d, wcv_d.h.ap(), 0,
                                   [(wv(wv.h[:, :, 0:256]), wcols(OFF_KV + 3 * 256, 256)),
                                    (wv(wv.h[:, :, 256:512]), wcols(OFF_KV + 5 * 256, 256)),
                                    (wv(wv.h[:, :, 512:560]), wcols(OFF_NG, 48))])
                            wvt = wv(wv.h[:, :, :]).trks
                            for sub in range(4):
                                ti = tb * 4 + sub
                                p1, p2 = nb(), nb()
                                for c in range(16):
                                    P.mm(p1(p1.h[:, :]), ht(ht.h[:, c, sub * 128:(sub + 1) * 128], ("h", 0)), V(wv.h[:, c, 0:512], wvt),
                                         start=(c == 0), stop=(c == 15))
                                for c in range(16):
                                    P.mm(p2(p2.h[:, 0:48]), ht(ht.h[:, c, sub * 128:(sub + 1) * 128], ("h", 0)), V(wv.h[:, c, 512:560], wvt),
                                         start=(c == 0), stop=(c == 15))
                                P.copy("vector", vsa(vsa.h[:, ti, :, 0:64], ti), p1(p1.h[:, 0:256].rearrange("p (g d) -> p g d", g=4)))
                                P.copy("vector", vwa(vwa.h[:, ti, :, 0:64], ti), p1(p1.h[:, 256:512].rearrange("p (g d) -> p g d", g=4)))
                                P.act(gat(gat.h[:, ti, :], ti), p2(p2.h[:, 0:48]), AF.Sigmoid)
                    P.barrier()
                    if mstop != "A":
                        dall = build_tables(st)
                    if mstop in ("A", "T"):
                        with ExitStack() as sx:
                            xc = sbt(sx, "passx", [128, 16, 512], F32)
                            for tb in range(4):
                                P.dma("sync", xc(xc.h[:, :, :]), xsrc(xsrc.h.ap()[:, :, tb * 512:(tb + 1) * 512].rearrange("c p t -> p c t"), tb))
                                P.dma("sync", xdst(xdst.h.ap()[:, :, tb * 512:(tb + 1) * 512].rearrange("c p t -> p c t"), tb), xc(xc.h[:, :, :]))
                            P.barrier()
                        return
                    attention(l, st, qT, ksd, kwd, kcT, vcT, vsa, vwa, gat, dall)
                    bank_n[0] = 8
                P.barrier()
                if mstop == "X":
                    with ExitStack() as sx:
                        xc = sbt(sx, "passx2", [128, 16, 512], F32)
                        for tb in range(4):
                            P.dma("sync", xc(xc.h[:, :, :]), xsrc(xsrc.h.ap()[:, :, tb * 512:(tb + 1) * 512].rearrange("c p t -> p c t"), tb))
                            P.dma("sync", xdst(xdst.h.ap()[:, :, tb * 512:(tb + 1) * 512].rearrange("c p t -> p c t"), tb), xc(xc.h[:, :, :]))
                        P.barrier()
                    return
                stage_d(l, qT, xsrc, xdst, wcols)
            P.barrier()

        def dump(name, view_fn_list):
            pass

        def dump_attn_inputs(qT, ksd, kwd, kcT, vcT, vsa, vwa, gat, dall):
            with ExitStack() as sd_:
                tmp = sbt(sd_, "dbg_tmp", [128, 8 * 512], F32)
                if "qT" in dbg_d:
                    o = dbg_d["qT"]
                    P.copy("vector", tmp(tmp.h[:, :].rearrange("p (c t) -> p c t", c=8)), qT(qT.h[:, :, 0:512], 0))
                    P.dma("sync", o(o.h.ap()), tmp(tmp.h[:, :]))
                if "ksd" in dbg_d:
                    o = dbg_d["ksd"]
                    t2 = sbt(sd_, "dbg_t2", [128, 4 * 512], F32)
                    P.copy("vector", t2(t2.h[:, :].rearrange("p (c t) -> p c t", c=4)), ksd(ksd.h[:, :, 0:512], 0))
                    P.dma("sync", o(o.h.ap()), t2(t2.h[:, :]))
                if "gat" in dbg_d:
                    o = dbg_d["gat"]
                    P.dma("sync", o(o.h.ap()), gat(gat.h[:, :, :].rearrange("p a b -> p (a b)") if False else gat.h[:, 0, :], 0))
                if "dall" in dbg_d:
                    o = dbg_d["dall"]
                    t3 = sbt(sd_, "dbg_t3", [128, 7 * 128], F32)
                    P.copy("vector", t3(t3.h[:, :].rearrange("p (c t) -> p c t", c=7)), dall(dall.h[:, :, 5, :]))
                    P.dma("sync", o(o.h.ap()), t3(t3.h[:, :]))

        def attention(l, st, qT, ksd, kwd, kcT, vcT, vsa, vwa, gat, dall):
            with ExitStack() as sc:
                kcn = sbt(sc, "c_kcn", [128, 4, 128], BF16)
                vcx = sbt(sc, "c_vcx", [128, 4, 97], BF16)
                P.memset("vector", vcx(vcx.h[:, :, 64:65]), 1.0)
                for g in range(4):
                    P.copy("vector", vcx(vcx.h[0:127, g, 65:97], "ov"), cv("ov", rows=127))
                expb = sbt(sc, "c_expb", [32, 16, 128], BF16)
                P.dma("gpsimd", expb(expb.h[:, :, :]), expm_d(expm_d.h.ap().rearrange("p (a b) -> p a b", a=16)))
                j127 = sbt(sc, "c_j127", [128, 128], BF16)
                P.copy("vector", j127(j127.h[:, :]), cv("j127"))
                with ExitStack() as s2:
                    w1 = sbt(s2, "c_w1", [128, 32, 256], BF16)
                    w2 = sbt(s2, "c_w2", [128, 2, 128], BF16)
                    posb = sbt(s2, "c_posb", [128, 32], BF16)
                    pbias = sbt(s2, "c_pbias", [128, 2], F32)
                    hid = sbt(s2, "c_hid", [128, 2, 128], BF16)
                    sqc = sbt(s2, "c_sq", [128, 128], BF16)
                    rsc = sbt(s2, "c_rs", [128, 128], F32)
                    for kind in ("k", "v"):
                        w1d = dr["cmp_%s_w1" % kind]
                        w2d = dr["cmp_%s_w2" % kind]
                        src1 = w1d.h.ap()[l].rearrange("(l d) f -> d l f", d=64)
                        P.dma("gpsimd", w1(w1.h[0:64, :, :], "a"), w1d(src1))
                        P.dma("gpsimd", w1(w1.h[64:128, :, :], "b"), w1d(src1))
                        src2 = w2d.h.ap()[l].rearrange("(fh p) d -> p fh d", p=128)
                        P.dma("gpsimd", w2(w2.h[:, :, 0:64], "a"), w2d(src2))
                        P.dma("gpsimd", w2(w2.h[:, :, 64:128], "b"), w2d(src2))
                        w1t = w1(w1.h[:, :, :], "a", "b").trks
                        w2t = w2(w2.h[:, :, :], "a", "b").trks
                        pn = "pk" if kind == "k" else "pv"
                        P.copy("vector", posb(posb.h[:, :]), prm(prm.h[:, l, PRM[pn]:PRM[pn] + 32]))
                        for fh in range(2):
                            pb = nb()
                            for ll in range(32):
                                P.mm(pb(pb.h[:, 0:1]), V(w1.h[0:64, ll, fh * 128:(fh + 1) * 128], w1t), posb(posb.h[0:64, ll:ll + 1]),
                                     start=(ll == 0), stop=(ll == 31))
                            P.copy("vector", pbias(pbias.h[:, fh:fh + 1]), pb(pb.h[:, 0:1]))
                        srcT = kcT if kind == "k" else vcT
                        for g in range(4):
                            hf, ch = g % 2, g // 2
                            for fh in range(2):
                                pb = nb()
                                for ll in range(32):
                                    r16 = srcT.h[hf * 64:(hf + 1) * 64, ch, :].rearrange("p (c s) -> p c s", s=16)
                                    rhs = r16[:, 0:127, ll] if ll < 16 else r16[:, 1:128, ll - 16]
                                    P.mm(pb(pb.h[:, 0:127]), V(w1.h[hf * 64:(hf + 1) * 64, ll, fh * 128:(fh + 1) * 128], w1t),
                                         srcT(rhs, 0, 1, 2, 3), start=(ll == 0), stop=(ll == 31))
                                P.act(hid(hid.h[:, fh, 0:127], fh), pb(pb.h[:, 0:127]), AF.Silu, bias=pbias(pbias.h[:, fh:fh + 1]))
                            if kind == "k":
                                pb = nb()
                                for fh in range(2):
                                    P.mm(pb(pb.h[:, 0:127]), V(w2.h[:, fh, :], w2t), hid(hid.h[:, fh, 0:127], fh), start=(fh == 0), stop=(fh == 1))
                                P.act(sqc(sqc.h[:, 0:127]), pb(pb.h[:, 0:127]), AF.Square)
                                pm = nb()
                                P.mm(pm(pm.h[:, 0:127]), blk_bf(blk_bf.h[:, :]), sqc(sqc.h[:, 0:127]))
                                P.act(rsc(rsc.h[:, 0:127]), pm(pm.h[:, 0:127]), AF.Sqrt, bias=epsc(epsc.h[:, 0:1]), scale=1.0 / 64)
                                P.recip(rsc(rsc.h[:, 0:127]), rsc(rsc.h[:, 0:127]))
                                gc = prm(prm.h[:, l, PRM["kn"]:PRM["kn"] + 1])
                                P.stt("vector", kcn(kcn.h[:, g, 0:127], g), pb(pb.h[:, 0:127]), gc, rsc(rsc.h[:, 0:127]), ALU.mult, ALU.mult)
                            else:
                                pb = nb()
                                for fh in range(2):
                                    P.mm(pb(pb.h[0:127, 0:64]), hid(hid.h[:, fh, 0:127], fh), V(w2.h[:, fh, 0:64], w2t), start=(fh == 0), stop=(fh == 1))
                                P.copy("vector", vcx(vcx.h[0:127, g, 0:64], ("v", g)), pb(pb.h[0:127, 0:64]))
                P.barrier()
                bct = [sbt(sc, "t_bch%d" % i, [128, 4, 128], BF16) for i in range(4)]
                bcf = [sbt(sc, "t_bcf%d" % i, [128, 4, 128], BF16) for i in range(4)]
                pT = [sbt(sc, "t_pT%d" % i, [128, 512], BF16) for i in range(5)]
                aacc = sbt(sc, "t_aacc", [128, 16, 64], F32)
                abf = sbt(sc, "t_abf", [128, 1024], BF16)
                rden = sbt(sc, "t_rden", [128, 4], F32)
                coef = sbt(sc, "t_coef", [128, 4], F32)
                imp = sbt(sc, "t_imp", [128, 32], F32)
                scr = sbt(sc, "t_scr", [128, 32], F32)
                wrk = sbt(sc, "t_wrk", [128, 32], F32)
                m8 = sbt(sc, "t_m8", [128, 8], F32)
                selns = [sbt(sc, "t_seln%d" % g, [128, 32], F32) for g in range(4)]
                selTs = [sbt(sc, "t_selT%d" % g, [32, 2, 128], BF16) for g in range(4)]
                bank_n[0] = 6
                cnt = {"p": 0, "b": 0, "a": 0}
                accb = [banks[6], banks[7]]

                def nacc():
                    b = accb[cnt["a"] % 2]
                    cnt["a"] += 1
                    return b

                o1, o2, o3 = CST["notf"][0], CST["addv"][0], CST["vneg"][0]

                def hd(g, s):
                    return 4 * g + 2 * (s % 2) + s // 2

                for i in range(ntiles):
                    q0 = i * 128

                    def score(g, klhs, rows, bias_fn, mask_j):
                        p = pT[cnt["p"] % 5]
                        cnt["p"] += 1
                        pss = (nb(), nb())
                        idv = ident_bf(ident_bf.h[0:rows, 0:rows])
                        for hf in range(2):
                            rhs = qT(qT.h[hf * 64:(hf + 1) * 64, 2 * g:2 * g + 2, q0:q0 + 128], i // 4)
                            P.mm(pss[hf](pss[hf].h[0:rows, 0:256]), klhs(hf), rhs, start=True, stop=False)
                        for hf in range(2):
                            P.mm(pss[hf](pss[hf].h[0:rows, 0:256]), idv, bias_fn(hf), start=False, stop=(mask_j is None))
                        if mask_j is not None:
                            sT = selTs[g]
                            for hf in range(2):
                                P.mm(pss[hf](pss[hf].h[0:rows, 0:256]), expb(expb.h[:, mask_j, :]), sT(sT.h[:, :, :]), start=False, stop=True)
                        for hf in range(2):
                            P.act(p(p.h[0:rows, hf * 256:(hf + 1) * 256], hf), pss[hf](pss[hf].h[0:rows, 0:256]), AF.Exp)
                        return p

                    def combine(g, br, pbk, first):
                        w = 97 if br == 0 else 65
                        den = V(pbk.h[:, 0:4 * w].rearrange("p (s c) -> p s c", s=4)[:, :, 64], pbk(pbk.h[:, :]).trks)
                        P.ts("vector", rden(rden.h[:, :]), den, 1e-30, None, ALU.max)
                        P.recip(rden(rden.h[:, :]), rden(rden.h[:, :]))
                        for s in range(4):
                            h = hd(g, s)
                            if br == 0:
                                if s == 0:
                                    P.ts("vector", imp(imp.h[:, :]), pbk(pbk.h[:, s * 97 + 65:s * 97 + 97]), rden(rden.h[:, s:s + 1]), None, ALU.mult)
                                else:
                                    P.stt("vector", imp(imp.h[:, :]), pbk(pbk.h[:, s * 97 + 65:s * 97 + 97]), rden(rden.h[:, s:s + 1]),
                                          imp(imp.h[:, :]), ALU.mult, ALU.add)
                            P.tt("vector", coef(coef.h[:, s:s + 1]), rden(rden.h[:, s:s + 1]), gat(gat.h[:, i, 3 * h + br:3 * h + br + 1], i), ALU.mult)
                            if first:
                                P.ts("vector", aacc(aacc.h[:, h, :], h), pbk(pbk.h[:, s * w:s * w + 64]), coef(coef.h[:, s:s + 1]), None, ALU.mult)
                            else:
                                P.stt("vector", aacc(aacc.h[:, h, :], h), pbk(pbk.h[:, s * w:s * w + 64]), coef(coef.h[:, s:s + 1]),
                                      aacc(aacc.h[:, h, :], h), ALU.mult, ALU.add)

                    def branch(g, steps, pbk, vbuf, mask):
                        LA = 2
                        ps_ = [None] * len(steps)
                        for n in range(min(LA, len(steps))):
                            ps_[n] = steps[n][1]()
                        for n, (j, _) in enumerate(steps):
                            if n + LA < len(steps):
                                ps_[n + LA] = steps[n + LA][1]()
                            p = ps_[n]
                            for s in range(4):
                                P.mm(pbk(pbk.h[:, s * 65:(s + 1) * 65]), p(p.h[:, s * 128:(s + 1) * 128], s // 2), vbuf(vbuf.h[:, j, g, :], j),
                                     start=(n == 0 and s == 0), stop=(n == len(steps) - 1 and s == 3))

                    for g in range(4):
                        bh, bf_ = bct[cnt["b"] % 4], bcf[cnt["b"] % 4]
                        cnt["b"] += 1
                        src = bass.AP(fd_d.h, 4 * g * 4096 + 2048 + q0 - 16 * 126 - 31, [[16, 128], [4096, 4], [1, 128]])
                        P.dma("sync", bh(bh.h[:, :, :]), fd_d(src))
                        pb = nb()
                        P.mm(pb(pb.h[0:127, :]), j127(j127.h[0:127, 0:127]), bh(bh.h[0:127, :, :]))
                        P.copy("scalar", bf_(bf_.h[0:127, :, :]), pb(pb.h[0:127, :]))
                        p = score(g, lambda hf: kcn(kcn.h[hf * 64:(hf + 1) * 64, g, 0:127], g), 127,
                                  lambda hf: bf_(bf_.h[0:127, 2 * hf:2 * hf + 2, :]), None)
                        po = nacc()
                        for s in range(4):
                            P.mm(po(po.h[:, s * 97:(s + 1) * 97]), p(p.h[0:127, s * 128:(s + 1) * 128], s // 2),
                                 vcx(vcx.h[0:127, g, :], "ov", ("v", g), 0), start=(s == 0), stop=(s == 3))
                        combine(g, 0, po, True)
                        P.tt("vector", scr(scr.h[:, :]), imp(imp.h[:, :]), cst(cst.h[:, o1 + i * 32:o1 + (i + 1) * 32]), ALU.mult)
                        P.tt("vector", scr(scr.h[:, :]), scr(scr.h[:, :]), cst(cst.h[:, o2 + i * 32:o2 + (i + 1) * 32]), ALU.add)
                        P.op("vector", lambda e: e.max(m8.h[:, :], scr.h[:, :]), [scr(scr.h[:, :])], [m8(m8.h[:, :])])
                        P.op("vector", lambda e: e.match_replace(wrk.h[:, :], m8.h[:, :], scr.h[:, :], -3e30),
                             [scr(scr.h[:, :]), m8(m8.h[:, :])], [wrk(wrk.h[:, :])])
                        P.op("vector", lambda e: e.max(m8.h[:, :], wrk.h[:, :]), [wrk(wrk.h[:, :])], [m8(m8.h[:, :])])
                        P.op("vector", lambda e: e.match_replace(wrk.h[:, :], m8.h[:, :], wrk.h[:, :], -3e30),
                             [wrk(wrk.h[:, :]), m8(m8.h[:, :])], [wrk(wrk.h[:, :])])
                        seln = selns[g]
                        P.tt("vector", seln(seln.h[:, :]), scr(scr.h[:, :]), wrk(wrk.h[:, :]), ALU.subtract)
                        P.ts("vector", seln(seln.h[:, :]), seln(seln.h[:, :]), 1.0, -1.0, ALU.min, ALU.add)
                        P.stt("vector", seln(seln.h[:, :]), seln(seln.h[:, :]), 30000.0, cst(cst.h[:, o3 + i * 32:o3 + (i + 1) * 32]),
                              ALU.mult, ALU.add)
                    for g in range(4):
                        pt = nb()
                        P.transpose(pt(pt.h[0:32, 0:128]), selns[g](selns[g].h[:, :]), cv("ident"))
                        sT = selTs[g]
                        for s in range(2):
                            P.copy("vector", sT(sT.h[:, s, :]), pt(pt.h[0:32, 0:128]))
                    jl = max(0, i - 4)
                    for g in range(4):
                        steps = []
                        for j in range(jl, i + 1):
                            dw = 6 if i - j == 4 else i - j
                            steps.append((j, (lambda j=j, dw=dw, g=g: score(
                                g, lambda hf: kwd(kwd.h[hf * 64:(hf + 1) * 64, g, j * 128:(j + 1) * 128], j // 4), 128,
                                lambda hf: dall(dall.h[:, dw, 4 * g + 2 * hf:4 * g + 2 * hf + 2, :]), None))))
                        pow_ = nacc()
                        branch(g, steps, pow_, vwa, False)
                        combine(g, 2, pow_, False)
                    for g in range(4):
                        steps = []
                        for j in range(i + 1):
                            dl = min(i - j, 5)
                            steps.append((j, (lambda j=j, dl=dl, g=g: score(
                                g, lambda hf: ksd(ksd.h[hf * 64:(hf + 1) * 64, g, j * 128:(j + 1) * 128], j // 4), 128,
                                lambda hf: dall(dall.h[:, dl, 4 * g + 2 * hf:4 * g + 2 * hf + 2, :]), j))))
                        pos = nacc()
                        branch(g, steps, pos, vsa, True)
                        combine(g, 1, pos, False)
                    for c4 in range(2):
                        pt = nb()
                        for cc in range(4):
                            c = c4 * 4 + cc
                            P.transpose(pt(pt.h[:, cc * 128:(cc + 1) * 128]),
                                        aacc(aacc.h[:, 2 * c:2 * c + 2, :].rearrange("p h d -> p (h d)"), 2 * c, 2 * c + 1), cv("ident"))
                        P.copy("scalar", qT(qT.h[:, c4 * 4:(c4 + 1) * 4, q0:q0 + 128], ("a", i), i // 4),
                               pt(pt.h[:, :].rearrange("p (c t) -> p c t", c=4)))


        def stage_d(l, aT, xsrc, xdst, wcols):
            wpn, wps, wo = dr["w_proj_nsa"], dr["w_proj_sgu"], dr["w_out"]
            with ExitStack() as st:
                xt = sbt(st, "d_x", [128, 16, 512], F32)
                ht = sbt(st, "d_h", [128, 16, 512], BF16)
                sq = [sbt(st, "d_sq%d" % i, [128, 512], BF16) for i in range(2)]
                sd = sbt(st, "d_sd", [128, 512], F32)
                uT = sbt(st, "d_uT", [128, 8, 512], BF16)
                sgT = sbt(st, "d_sgT", [128, 8, 512], BF16)
                mrg = sbt(st, "d_mrg", [128, 16, 512], BF16)
                w16 = [sbt(st, "d_w16_%d" % i, [128, 16, 128], BF16) for i in range(4)]
                w8 = [sbt(st, "d_w8_%d" % i, [128, 8, 128], BF16) for i in range(4)]
                wv2 = sbt(st, "d_wv2", [128, 16, 512], BF16)
                vg = sbt(st, "d_vg", [128, 1024], F32)
                vsq = sbt(st, "d_vsq", [128, 1024], F32)
                vln = sbt(st, "d_vln", [128, 1024], BF16)
                stat = sbt(st, "d_stat", [128, 8], F32)
                lng = sbt(st, "d_lng", [128, 1024], F32)
                lnb = sbt(st, "d_lnb", [128, 1024], F32)
                wsT = sbt(st, "d_wsT", [128, 8, 128], BF16)
                wsf = sbt(st, "d_wsf", [128, 8, 128], F32)
                sbb = sbt(st, "d_sbb", [1, 1024], BF16)
                sg1 = sbt(st, "d_sg1", [128, 512], F32)
                sg2 = sbt(st, "d_sg2", [128, 512], F32)
                t1 = sbt(st, "d_t1", [128, 512], F32)
                P.dma("sync", lng(lng.h[:, :]), sgb_d(sgb_d.h.ap()[l, 0]))
                P.dma("sync", lnb(lnb.h[:, :]), sgb_d(sgb_d.h.ap()[l, 1]))
                P.dma("sync", wsf(wsf.h[:, :, :]), sgw_d(sgw_d.h.ap()[l]))
                for g in range(8):
                    P.tt("vector", wsT(wsT.h[:, g, :]), wsf(wsf.h[:, g, :]), cv("triu"), ALU.mult)
                P.dma("gpsimd", sbb(sbb.h[:, :]), sgub_d(sgub_d.h.ap()[l]))
                k16 = 0
                k8 = 0
                for tb in range(4):
                    t0 = tb * 512
                    make_h((xt, 0), (ht, 0), xsrc, l, "nm", t0, 0, sq, sd)
                    for c in range(8):
                        w = w16[k16 % 4]
                        k16 += 1
                        wfetch(tb, w(w.h[:, :, :]), wc16_d, wc16_d.h.ap()[20 + c], 20 + c, [(w(w.h[:, :, :]), wcols(OFF_UV + c * 128, 128))])
                        pz = nb()
                        for k in range(16):
                            P.mm(pz(pz.h[:, :]), w(w.h[:, k, :]), ht(ht.h[:, k, :], ("h", 0)), start=(k == 0), stop=(k == 15))
                        P.act(uT(uT.h[:, c, :], c), pz(pz.h[:, :]), AF.Gelu_apprx_tanh)
                    for sub in range(4):
                        for half in range(2):
                            wfetch(tb if sub == 0 else 1, wv2(wv2.h[:, :, :]), wcv2_d, wcv2_d.h.ap()[half], half,
                                   [(wv2(wv2.h[:, :, :]), wcols(OFF_UV + 1024 + half * 512, 512))])
                            pz = nb()
                            for k in range(16):
                                P.mm(pz(pz.h[:, :]), ht(ht.h[:, k, sub * 128:(sub + 1) * 128], ("h", 0)), wv2(wv2.h[:, k, :]),
                                     start=(k == 0), stop=(k == 15))
                            P.act(vg(vg.h[:, half * 512:(half + 1) * 512], half), pz(pz.h[:, :]), AF.Gelu_apprx_tanh)
                        vga = vg(vg.h[:, :], 0, 1)
                        P.op("vector", lambda e: e.tensor_reduce(stat.h[:, 0:1], vg.h[:, :], AX.X, ALU.add), [vga], [stat(stat.h[:, 0:1], 0)])
                        P.act(vsq(vsq.h[:, :]), vga, AF.Square)
                        P.op("vector", lambda e: e.tensor_reduce(stat.h[:, 1:2], vsq.h[:, :], AX.X, ALU.add), [vsq(vsq.h[:, :])], [stat(stat.h[:, 1:2], 1)])
                        P.ts("vector", stat(stat.h[:, 2:3], 2), stat(stat.h[:, 0:1], 0), 1.0 / 1024, None, ALU.mult)
                        P.tt("vector", stat(stat.h[:, 3:4], 3), stat(stat.h[:, 2:3], 2), stat(stat.h[:, 2:3], 2), ALU.mult)
                        P.stt("vector", stat(stat.h[:, 4:5], 4), stat(stat.h[:, 1:2], 1), 1.0 / 1024, stat(stat.h[:, 3:4], 3), ALU.mult, ALU.subtract)
                        P.act(stat(stat.h[:, 5:6], 5), stat(stat.h[:, 4:5], 4), AF.Sqrt, bias=epsc(epsc.h[:, 0:1]), scale=1.0)
                        P.recip(stat(stat.h[:, 6:7], 6), stat(stat.h[:, 5:6], 5))
                        P.ts("vector", vsq(vsq.h[:, :]), vga, stat(stat.h[:, 2:3], 2), stat(stat.h[:, 6:7], 6), ALU.subtract, ALU.mult)
                        P.tt("vector", vsq(vsq.h[:, :]), vsq(vsq.h[:, :]), lng(lng.h[:, :]), ALU.mult)
                        P.tt("vector", vln(vln.h[:, :]), vsq(vsq.h[:, :]), lnb(lnb.h[:, :]), ALU.add)
                        for gh in range(2):
                            pz = nb()
                            for gg in range(4):
                                g = gh * 4 + gg
                                P.mm(pz(pz.h[:, gg * 128:(gg + 1) * 128]), vln(vln.h[:, g * 128:(g + 1) * 128]), wsT(wsT.h[:, g, :]), start=(gg == 0), stop=False)
                                P.mm(pz(pz.h[:, gg * 128:(gg + 1) * 128]), ones_bf(ones_bf.h[0:1, :]), sbb(sbb.h[0:1, g * 128:(g + 1) * 128]), start=False, stop=(gg == 3))
                            P.tt("vector", sgT(sgT.h[:, gh * 4:(gh + 1) * 4, sub * 128:(sub + 1) * 128], (gh, sub)),
                                 pz(pz.h[:, :].rearrange("p (g t) -> p g t", g=4)),
                                 uT(uT.h[:, gh * 4:(gh + 1) * 4, sub * 128:(sub + 1) * 128], *range(gh * 4, gh * 4 + 4)), ALU.mult)
                    sgk = [(gh, sub) for gh in range(2) for sub in range(4)]
                    for j in range(16):
                        wa, wb_ = w8[k8 % 4], w8[(k8 + 1) % 4]
                        k8 += 2
                        wg1, wg2 = w16[k16 % 4], w16[(k16 + 1) % 4]
                        k16 += 2
                        wfetch(tb, wa(wa.h[:, :, :]), wc8_d, wc8_d.h.ap()[2 * j], 2 * j,
                               [(wa(wa.h[:, :, :]), wpn(wpn.h.ap()[l, :, j * 128:(j + 1) * 128].rearrange("(c p) f -> p c f", p=128)))])
                        wfetch(tb, wb_(wb_.h[:, :, :]), wc8_d, wc8_d.h.ap()[2 * j + 1], 2 * j + 1,
                               [(wb_(wb_.h[:, :, :]), wps(wps.h.ap()[l, :, j * 128:(j + 1) * 128].rearrange("(c p) f -> p c f", p=128)))])
                        wfetch(tb, wg1(wg1.h[:, :, :]), wc16_d, wc16_d.h.ap()[28 + 2 * j], 28 + 2 * j, [(wg1(wg1.h[:, :, :]), wcols(OFF_MG + j * 128, 128))])
                        wfetch(tb, wg2(wg2.h[:, :, :]), wc16_d, wc16_d.h.ap()[29 + 2 * j], 29 + 2 * j, [(wg2(wg2.h[:, :, :]), wcols(OFF_MG + D + j * 128, 128))])
                        pa, pb, pg1, pg2 = nb(), nb(), nb(), nb()
                        for k in range(8):
                            P.mm(pa(pa.h[:, :]), wa(wa.h[:, k, :]), aT(aT.h[:, k, t0:t0 + 512], *[("a", i) for i in range(tb * 4, tb * 4 + 4)]),
                                 start=(k == 0), stop=(k == 7))
                        for k in range(8):
                            P.mm(pb(pb.h[:, :]), wb_(wb_.h[:, k, :]), sgT(sgT.h[:, k, :], *sgk), start=(k == 0), stop=(k == 7))
                        for k in range(16):
                            P.mm(pg1(pg1.h[:, :]), wg1(wg1.h[:, k, :]), ht(ht.h[:, k, :], ("h", 0)), start=(k == 0), stop=(k == 15))
                        for k in range(16):
                            P.mm(pg2(pg2.h[:, :]), wg2(wg2.h[:, k, :]), ht(ht.h[:, k, :], ("h", 0)), start=(k == 0), stop=(k == 15))
                        P.act(sg1(sg1.h[:, :]), pg1(pg1.h[:, :]), AF.Sigmoid)
                        P.act(sg2(sg2.h[:, :]), pg2(pg2.h[:, :]), AF.Sigmoid)
                        P.tt("vector", t1(t1.h[:, :]), sg1(sg1.h[:, :]), pa(pa.h[:, :]), ALU.mult)
                        P.tt("vector", sg2(sg2.h[:, :]), sg2(sg2.h[:, :]), pb(pb.h[:, :]), ALU.mult)
                        P.tt("vector", mrg(mrg.h[:, j, :], j), t1(t1.h[:, :]), sg2(sg2.h[:, :]), ALU.add)
                    for j in range(16):
                        w = w16[k16 % 4]
                        k16 += 1
                        wfetch(tb, w(w.h[:, :, :]), wc16_d, wc16_d.h.ap()[60 + j], 60 + j,
                               [(w(w.h[:, :, :]), wo(wo.h.ap()[l, :, j * 128:(j + 1) * 128].rearrange("(c p) f -> p c f", p=128)))])
                        pz = nb()
                        for k in range(16):
                            P.mm(pz(pz.h[:, :]), w(w.h[:, k, :]), mrg(mrg.h[:, k, :], *range(16)), start=(k == 0), stop=(k == 15))
                        xv = xt(xt.h[:, j, :], ("x", 0))
                        P.tt("vector", xv, xv, pz(pz.h[:, :]), ALU.add)
                    P.dma("sync", xdst(xdst.h.ap()[:, :, t0:t0 + 512].rearrange("c p t -> p c t"), tb), xt(xt.h[:, :, :], ("x", 0)))

        cur = x_in
        for l in range(depth):
            last = (l == depth - 1)
            todo = [p for p in "1m2" if p in phases]
            for p in todo:
                dst = out_d if (last and p == todo[-1]) else xs_d
                if p == "1":
                    ffn(l, "ffn1", cur, dst)
                elif p == "m":
                    mixer(l, cur, dst)
                else:
                    ffn(l, "ffn2", cur, dst)
                cur = xs_d
        P.barrier()
        P.emit(top)
    return nc


_NC_CACHE = {}


def _prep_shared(inputs):
    f = lambda a: np.ascontiguousarray(np.asarray(a, np.float32))
    sh = {}
    for k in ("ffn1_w_gate", "ffn1_w_up", "ffn1_w_down", "ffn2_w_gate", "ffn2_w_up", "ffn2_w_down", "w_in",
              "cmp_k_w1", "cmp_k_w2", "cmp_v_w1", "cmp_v_w2", "w_proj_nsa", "w_proj_sgu", "w_out"):
        sh[k] = f(inputs[k])
    sh["cst"] = CST_ARR
    prm = np.zeros((L, 128, NPRM), np.float32)
    for l in range(L):
        for nm, key in (("n1", "ffn1_norm"), ("nm", "mix_norm"), ("n2", "ffn2_norm")):
            prm[l, :, PRM[nm]:PRM[nm] + 16] = f(inputs[key])[l].reshape(16, 128).T
        prm[l, :, PRM["qn"]] = np.tile(f(inputs["q_norm"])[l], 2)
        for i in range(3):
            prm[l, :, PRM["kn"] + i] = np.tile(f(inputs["k_norm"])[l, i], 2)
        prm[l, :, PRM["pk"]:PRM["pk"] + 32] = np.tile(f(inputs["cmp_pos_k"])[l].T, (2, 1))
        prm[l, :, PRM["pv"]:PRM["pv"] + 32] = np.tile(f(inputs["cmp_pos_v"])[l].T, (2, 1))
    sh["prm"] = prm
    rb = f(inputs["rel_bias"])
    rbp = np.zeros((32, 16), np.float32)
    for g in range(4):
        for s in range(4):
            rbp[:, 4 * g + s] = rb[:, 4 * g + 2 * (s % 2) + s // 2]
    sh["rbp"] = rbp
    sh["oh"] = OH_ARR
    sh["expm"] = EXPM_ARR
    sh["sgub"] = np.ascontiguousarray(f(inputs["sgu_b"]).reshape(L, 1, 1024))
    sgb = np.zeros((L, 2, 128, 1024), np.float32)
    sgb[:, 0] = f(inputs["sgu_norm_g"])[:, None, :]
    sgb[:, 1] = f(inputs["sgu_norm_b"])[:, None, :]
    sh["sgb"] = sgb
    sh["sgwT"] = np.ascontiguousarray(f(inputs["sgu_w"]).transpose(0, 3, 1, 2))
    return sh


def kernel(**inputs):
    x = np.asarray(inputs["x"], np.float32)
    sh = _prep_shared(inputs)
    if "nc" not in _NC_CACHE:
        _NC_CACHE["nc"] = build_program()
    nc = _NC_CACHE["nc"]
    in_maps = []
    for b in range(NCORES):
        m = dict(sh)
        m["x_fm"] = np.ascontiguousarray(x[b].T.reshape(16, 128, S))
        in_maps.append(m)
    res = run_bass_kernel_spmd(nc, in_maps, core_ids=list(range(NCORES)))
    out = np.empty((NCORES, S, D), np.float32)
    for b in range(NCORES):
        out[b] = res.results[b]["out_fm"].reshape(D, S).T
    return out
```

```python
import math
from contextlib import ExitStack
import numpy as np
import concourse.bass as bass
import concourse.mybir as mybir
from concourse.bass_utils import run_bass_kernel_spmd

F32 = mybir.dt.float32
BF16 = mybir.dt.bfloat16
ALU = mybir.AluOpType
AF = mybir.ActivationFunctionType
AX = mybir.AxisListType

D = 2048
S = 2048
L = 2
DFF = 5504
NFC = 43
INW = 8752
OFF_KV, OFF_NG, OFF_UV, OFF_MG = 1024, 2560, 2608, 4656
EPS = 1e-6
NEGB = -30000.0
NCORES = 4

ENGS = ["sync", "gpsimd", "scalar", "vector", "tensor"]
NDSEM = 24


class Trk:
    __slots__ = ("w", "r", "x")

    def __init__(self, x=False):
        self.w = None
        self.r = {}
        self.x = x


class V:
    __slots__ = ("ap", "trks")

    def __init__(self, ap, trks):
        self.ap = ap
        self.trks = trks


class Buf:
    def __init__(self, h, excl=False):
        self.h = h
        self.t = {}
        self.excl = excl

    def __call__(self, ap, *keys):
        if not keys:
            keys = (0,)
        out = []
        for k in keys:
            t = self.t.get(k)
            if t is None:
                t = self.t[k] = Trk(self.excl)
            out.append(t)
        return V(ap, out)


class Prog:
    def __init__(self, nc):
        self.nc = nc
        self.ops = {e: [] for e in ENGS}
        self.seen = {e: {} for e in ENGS}
        self.seen_dma = {e: set() for e in ENGS}
        self.ndma = {e: 0 for e in ENGS}

    def _dep(self, eng, waits, d):
        if d[0] == "dma":
            if d in self.seen_dma[eng]:
                return
            self.seen_dma[eng].add(d)
            waits.append(d)
        else:
            e, i = d
            if e == eng and eng in ("tensor", "sync"):
                return
            if self.seen[eng].get(e, -1) >= i:
                return
            self.seen[eng][e] = i
            self.ops[e][i]["inc"] = True
            waits.append(d)

    def op(self, eng, fn, reads=(), writes=(), dma=False):
        idx = len(self.ops[eng])
        waits = []
        deps = []
        rd, wr = [], []
        for v in reads:
            for t in v.trks:
                (wr if t.x else rd).append(t)
        for v in writes:
            wr.extend(v.trks)
        for t in rd:
            if t.w is not None:
                deps.append(t.w)
        for t in wr:
            if t.w is not None:
                deps.append(t.w)
            deps.extend(t.r.values())
        if dma:
            n = self.ndma[eng]
            self.ndma[eng] += 1
            ev = ("dma", eng, n)
            if n >= NDSEM:
                deps.append(("dma", eng, n - NDSEM))
        else:
            ev = (eng, idx)
        for d in deps:
            self._dep(eng, waits, d)
        self.ops[eng].append({"fn": fn, "waits": waits, "inc": False, "dma": ev if dma else None})
        for t in rd:
            t.r[ev if dma else eng] = ev
        for t in wr:
            t.w = ev
            t.r = {}
        return ev

    def barrier(self):
        evs = []
        for e in ENGS:
            if self.ops[e]:
                last = len(self.ops[e]) - 1
                while last >= 0 and self.ops[e][last]["fn"] is None:
                    last -= 1
                if last >= 0 and self.ops[e][last]["dma"] is None:
                    evs.append((e, last))
            n = self.ndma[e]
            for k in range(max(0, n - NDSEM), n):
                evs.append(("dma", e, k))
        for e in ENGS:
            waits = []
            for d in evs:
                if d[0] != "dma" and d[0] == e:
                    continue
                self._dep(e, waits, d)
            self.ops[e].append({"fn": None, "waits": waits, "inc": False, "dma": None})

    def emit(self, stack):
        nc = self.nc
        esem = {e: stack.enter_context(nc.semaphore("es_" + e)) for e in ENGS}
        dsem = {e: [stack.enter_context(nc.semaphore("ds_%s_%d" % (e, i))) for i in range(NDSEM)]
                for e in ENGS if self.ndma[e] > 0}
        for e in ENGS:
            c = 0
            for o in self.ops[e]:
                if o["inc"]:
                    c += 1
                o["cnt"] = c
        block = stack.enter_context(nc.Block())
        ops = self.ops

        def run(engobj, e):
            for o in ops[e]:
                for d in o["waits"]:
                    if d[0] == "dma":
                        _, q, n = d
                        engobj.wait_ge(dsem[q][n % NDSEM], 16 * (n // NDSEM + 1))
                    else:
                        engobj.wait_ge(esem[d[0]], ops[d[0]][d[1]]["cnt"])
                if o["fn"] is None:
                    continue
                ins = o["fn"](engobj)
                if o["dma"] is not None:
                    _, q, n = o["dma"]
                    ins.then_inc(dsem[q][n % NDSEM], 16)
                elif o["inc"]:
                    ins.then_inc(esem[e], 1)

        @block.sync
        def _(x):
            run(x, "sync")

        @block.gpsimd
        def _(x):
            run(x, "gpsimd")

        @block.scalar
        def _(x):
            run(x, "scalar")

        @block.vector
        def _(x):
            run(x, "vector")

        @block.tensor
        def _(x):
            run(x, "tensor")

    def dma(self, q, out, in_):
        return self.op(q, lambda e: e.dma_start(out=out.ap, in_=in_.ap), [in_], [out], dma=True)

    def mm(self, out, lhsT, rhs, start=True, stop=True):
        return self.op("tensor", lambda e: e.matmul(out.ap, lhsT.ap, rhs.ap, start=start, stop=stop),
                       [lhsT, rhs], [out])

    def transpose(self, out, in_, ident):
        return self.op("tensor", lambda e: e.transpose(out.ap, in_.ap, ident.ap), [in_, ident], [out])

    def act(self, out, in_, func, bias=None, scale=None):
        reads = [in_]
        kw = {}
        if bias is not None:
            reads.append(bias)
            kw["bias"] = bias.ap
        if scale is not None:
            kw["scale"] = scale
        return self.op("scalar", lambda e: e.activation(out.ap, in_.ap, func, **kw), reads, [out])

    def tt(self, eng, out, in0, in1, op):
        return self.op(eng, lambda e: e.tensor_tensor(out.ap, in0.ap, in1.ap, op), [in0, in1], [out])

    def ts(self, eng, out, in0, s1, s2, op0, op1=None):
        reads = [in0]
        a1, a2 = s1, s2
        if isinstance(s1, V):
            reads.append(s1)
            a1 = s1.ap
        if isinstance(s2, V):
            reads.append(s2)
            a2 = s2.ap
        kw = {}
        if op1 is not None:
            kw["op1"] = op1
        return self.op(eng, lambda e: e.tensor_scalar(out.ap, in0.ap, a1, a2, op0, **kw), reads, [out])

    def stt(self, eng, out, in0, scalar, in1, op0, op1):
        reads = [in0, in1]
        a = scalar
        if isinstance(scalar, V):
            reads.append(scalar)
            a = scalar.ap
        return self.op(eng, lambda e: e.scalar_tensor_tensor(out.ap, in0.ap, a, in1.ap, op0, op1), reads, [out])

    def copy(self, eng, out, in_):
        if eng == "scalar":
            return self.op(eng, lambda e: e.copy(out.ap, in_.ap), [in_], [out])
        return self.op(eng, lambda e: e.tensor_copy(out.ap, in_.ap), [in_], [out])

    def memset(self, eng, out, val):
        return self.op(eng, lambda e: e.memset(out.ap, val), [], [out])

    def recip(self, out, in_):
        return self.op("vector", lambda e: e.reciprocal(out.ap, in_.ap), [in_], [out])


def _t5_bucket(dist):
    n = np.maximum(dist, 0)
    nf = np.maximum(n, 1).astype(np.float32)
    large = 16 + (np.log(nf / np.float32(16)) / np.float32(math.log(128 / 16)) * np.float32(16)).astype(np.int32)
    large = np.minimum(large, 31)
    return np.where(n < 16, n, large)


CST = {}


def _build_consts():
    cols = []
    off = [0]

    def add(name, arr):
        arr = np.asarray(arr, np.float32)
        a = np.zeros((128, arr.shape[1]), np.float32)
        a[:arr.shape[0]] = arr
        CST[name] = (off[0], arr.shape[1])
        off[0] += arr.shape[1]
        cols.append(a)

    idx = np.arange(4096)
    dist = idx - 2048
    oh = np.zeros((33, 4096), np.float32)
    bk = _t5_bucket(dist)
    for i in range(4096):
        if dist[i] < 0:
            oh[32, i] = 1
        else:
            oh[bk[i], i] = 1
    CST["_oh"] = oh
    add("ident", np.eye(128))
    add("j128", np.eye(128)[::-1])
    j127 = np.zeros((128, 128))
    j127[:127, :127] = np.eye(127)[::-1]
    add("j127", j127)
    s_ = np.arange(128)[:, None]
    t_ = np.arange(128)[None, :]
    add("triu", (s_ <= t_))
    expm = np.zeros((32, 16, 128))
    for jc in range(16):
        for k in range(128):
            expm[2 * jc + k // 64, jc, k] = 1
    CST["_expm"] = expm.reshape(32, -1).astype(np.float32)
    ci = np.arange(127)[:, None] * 16
    sj = np.arange(32)[None, :] * 64
    ov = np.clip(np.minimum(ci + 32, sj + 64) - np.maximum(ci, sj), 0, None).astype(np.float32) / 32
    add("ov", ov)
    notf = np.zeros((128, 16, 32))
    addv = np.zeros((128, 16, 32))
    vneg = np.zeros((128, 16, 32))
    for i in range(16):
        for p in range(128):
            cur = (128 * i + p) // 64
            for j in range(32):
                if j > cur:
                    addv[p, i, j] = -1e30
                    vneg[p, i, j] = NEGB
                elif j == 0:
                    addv[p, i, j] = 1e4
                elif j == cur:
                    addv[p, i, j] = 2e4
                elif j == cur - 1:
                    addv[p, i, j] = 3e4
                else:
                    notf[p, i, j] = 1
    add("notf", notf.reshape(128, -1))
    add("addv", addv.reshape(128, -1))
    add("vneg", vneg.reshape(128, -1))
    m6 = (t_ < s_).astype(np.float32)
    add("m6", m6)
    add("n6", (m6 - 1) * 30000.0)
    return np.concatenate(cols, axis=1)


CST_ARR = _build_consts()
NCST = CST_ARR.shape[1]
PRM = {"n1": 0, "nm": 16, "n2": 32, "qn": 48, "kn": 49, "pk": 52, "pv": 84}
NPRM = 116
OH_ARR = CST["_oh"]
EXPM_ARR = CST["_expm"]


def build_program(depth=L, dbg=None, phases="1m2", mstop=None, ntiles=16, tstop=8):
    nc = bass.Bass("TRN2", target_bir_lowering=False)
    dr = {}

    def din(name, shape, dt=F32):
        dr[name] = Buf(nc.dram_tensor(name, list(shape), dt, kind="ExternalInput"))
        return dr[name]

    x_in = din("x_fm", [16, 128, S])
    cst_d = din("cst", [128, NCST])
    prm_d = din("prm", [L, 128, NPRM])
    rb_d = din("rbp", [32, 16])
    oh_d = din("oh", [33, 4096])
    expm_d = din("expm", [32, 2048])
    sgub_d = din("sgub", [L, 1, 1024])
    sgb_d = din("sgb", [L, 2, 128, 1024])
    sgw_d = din("sgwT", [L, 128, 8, 128])
    wnames = {"ffn1_w_gate": [L, D, DFF], "ffn1_w_up": [L, D, DFF], "ffn1_w_down": [L, DFF, D],
              "ffn2_w_gate": [L, D, DFF], "ffn2_w_up": [L, D, DFF], "ffn2_w_down": [L, DFF, D],
              "w_in": [L, D, INW], "cmp_k_w1": [L, 2048, 256], "cmp_k_w2": [L, 256, 64],
              "cmp_v_w1": [L, 2048, 256], "cmp_v_w2": [L, 256, 64],
              "w_proj_nsa": [L, 1024, D], "w_proj_sgu": [L, 1024, D], "w_out": [L, D, D]}
    for k, shp in wnames.items():
        din(k, shp)
    out_d = Buf(nc.dram_tensor("out_fm", [16, 128, S], F32, kind="ExternalOutput"))
    xs_d = Buf(nc.dram_tensor("xs", [16, 128, S], F32))
    fd_d = Buf(nc.dram_tensor("fdt", [16, 4096], BF16))
    wc16_d = Buf(nc.dram_tensor("wc16", [76, 128, 16, 128], BF16))
    wc8_d = Buf(nc.dram_tensor("wc8", [32, 128, 8, 128], BF16))
    wcv_d = Buf(nc.dram_tensor("wcv", [128, 16, 560], BF16))
    wcv2_d = Buf(nc.dram_tensor("wcv2", [2, 128, 16, 512], BF16))
    dbg_d = {}
    if dbg:
        for name, shp in dbg.items():
            dbg_d[name] = Buf(nc.dram_tensor("dbg_" + name, list(shp), F32, kind="ExternalOutput"))

    P = Prog(nc)
    _NC_CACHE['P'] = P
    out_events = []

    with ExitStack() as top:
        top.enter_context(nc.allow_low_precision("bf16 matmul operands, fp32 accumulation"))

        uniq = [0]

        def sbt(st, name, shape, dt):
            uniq[0] += 1
            return Buf(st.enter_context(nc.sbuf_tensor("%s_%d" % (name, uniq[0]), list(shape), dt)))

        banks = [Buf(top.enter_context(nc.psum_tensor("pb%d" % i, [128, 512], F32)), excl=True) for i in range(8)]
        bank_i = [0]
        bank_n = [8]

        def nb():
            b = banks[bank_i[0] % bank_n[0]]
            bank_i[0] += 1
            return b

        cst = sbt(top, "cst_sb", [128, NCST], F32)
        P.dma("sync", cst(cst.h[:, :]), cst_d(cst_d.h.ap()))

        def cv(name, rows=128, lo=0, n=None):
            o, w = CST[name]
            if n is None:
                n = w - lo
            return cst(cst.h[0:rows, o + lo:o + lo + n])

        prm = sbt(top, "prm_sb", [128, L, NPRM], F32)
        P.dma("sync", prm(prm.h[:, :, :]), prm_d(prm_d.h.ap().rearrange("l p n -> p l n")))
        ones_bf = sbt(top, "ones_bf", [128, 128], BF16)
        P.memset("vector", ones_bf(ones_bf.h[:, :]), 1.0)
        blk_bf = sbt(top, "blk_bf", [128, 128], BF16)
        P.memset("vector", blk_bf(blk_bf.h[:, :]), 0.0)
        P.memset("vector", blk_bf(blk_bf.h[0:64, 0:64]), 1.0)
        P.memset("vector", blk_bf(blk_bf.h[64:128, 64:128]), 1.0)
        ident_bf = sbt(top, "ident_bf", [128, 128], BF16)
        P.copy("vector", ident_bf(ident_bf.h[:, :]), cv("ident"))
        epsc = sbt(top, "epsc", [128, 2], F32)
        P.memset("vector", epsc(epsc.h[:, 0:1]), EPS)
        P.memset("vector", epsc(epsc.h[:, 1:2]), 0.0)
        gsc = sbt(top, "gsc", [128, L, 52], F32)
        for l in range(L):
            P.copy("vector", gsc(gsc.h[:, l, 0:49]), prm(prm.h[:, l, 0:49]))
            P.ts("vector", gsc(gsc.h[:, l, 49:50]), prm(prm.h[:, l, 48:49]), 0.125, None, ALU.mult)

        def pcol(l, name, c=0):
            return gsc(gsc.h[:, l, PRM[name] + c:PRM[name] + c + 1])

        def make_h(st_x, st_h, xsrc, l, nname, tok0, hcol0, sq, sd):
            xt, xo = st_x
            ht, ho = st_h
            P.dma("sync", xt(xt.h[:, :, xo:xo + 512], ("x", xo)),
                  xsrc(xsrc.h.ap()[:, :, tok0:tok0 + 512].rearrange("c p t -> p c t"), tok0 // 512))
            psn = nb()
            for c in range(16):
                s = sq[c % 2]
                P.act(s(s.h[:, :]), xt(xt.h[:, c, xo:xo + 512], ("x", xo)), AF.Square)
                P.mm(psn(psn.h[:, :]), ones_bf(ones_bf.h[:, :]), s(s.h[:, :]), start=(c == 0), stop=(c == 15))
            P.act(sd(sd.h[:, :]), psn(psn.h[:, :]), AF.Sqrt, bias=epsc(epsc.h[:, 0:1]), scale=1.0 / D)
            P.recip(sd(sd.h[:, :]), sd(sd.h[:, :]))
            for c in range(16):
                P.stt("vector", ht(ht.h[:, c, ho:ho + 512], ("h", ho)), xt(xt.h[:, c, xo:xo + 512], ("x", xo)),
                      pcol(l, nname, c), sd(sd.h[:, :]), ALU.mult, ALU.mult)

        def wfetch(tb, wt_view, cbuf, cap, ckey, loaders):
            if tb == 0:
                for d_, s_ in loaders:
                    P.dma("gpsimd", d_, s_)
                P.dma("sync", cbuf(cap, ckey), wt_view)
            else:
                P.dma("sync", wt_view, cbuf(cap, ckey))

        def wload(wt, src_ap):
            P.dma("gpsimd", wt(wt.h[:, :, :]) if len(wt.h.shape) == 3 else wt(wt.h[:, :]), src_ap)

        def ffn(l, which, xsrc, xdst):
            wg_d, wu_d, wd_d = dr[which + "_w_gate"], dr[which + "_w_up"], dr[which + "_w_down"]
            nname = "n1" if which == "ffn1" else "n2"
            with ExitStack() as st:
                xt = sbt(st, "f_x", [128, 16, 1024], F32)
                ht = sbt(st, "f_h", [128, 16, 1024], BF16)
                act = sbt(st, "f_act", [128, 22, 1024], BF16)
                sq = [sbt(st, "f_sq%d" % i, [128, 512], BF16) for i in range(2)]
                sd = sbt(st, "f_sd", [128, 512], F32)
                wg = [sbt(st, "f_wg%d" % i, [128, 16, 128], BF16) for i in range(2)]
                wu = [sbt(st, "f_wu%d" % i, [128, 16, 128], BF16) for i in range(2)]
                wd = [sbt(st, "f_wd%d" % i, [128, 22, 256], BF16) for i in range(2)]
                sl = [sbt(st, "f_sl%d" % i, [128, 512], BF16) for i in range(2)]
                wi = 0
                di = 0
                for tt in range(2):
                    for tb in range(2):
                        make_h((xt, tb * 512), (ht, tb * 512), xsrc, l, nname, tt * 1024 + tb * 512, 0, sq, sd)
                    for fh in range(2):
                        fcs = list(range(0, 22)) if fh == 0 else list(range(22, 43))
                        for fi, fc in enumerate(fcs):
                            g, u = wg[wi % 2], wu[wi % 2]
                            wi += 1
                            P.dma("gpsimd", g(g.h[:, :, :]),
                                  wg_d(wg_d.h.ap()[l, :, fc * 128:(fc + 1) * 128].rearrange("(c p) f -> p c f", p=128)))
                            P.dma("gpsimd", u(u.h[:, :, :]),
                                  wu_d(wu_d.h.ap()[l, :, fc * 128:(fc + 1) * 128].rearrange("(c p) f -> p c f", p=128)))
                            for tb in range(2):
                                pg, pu = nb(), nb()
                                for c in range(16):
                                    P.mm(pg(pg.h[:, :]), g(g.h[:, c, :]), ht(ht.h[:, c, tb * 512:(tb + 1) * 512], ("h", tb * 512)),
                                         start=(c == 0), stop=(c == 15))
                                for c in range(16):
                                    P.mm(pu(pu.h[:, :]), u(u.h[:, c, :]), ht(ht.h[:, c, tb * 512:(tb + 1) * 512], ("h", tb * 512)),
                                         start=(c == 0), stop=(c == 15))
                                s = sl[(fi * 2 + tb) % 2]
                                P.act(s(s.h[:, :]), pg(pg.h[:, :]), AF.Silu)
                                P.tt("vector", act(act.h[:, fi, tb * 512:(tb + 1) * 512], (fi, tb)), s(s.h[:, :]), pu(pu.h[:, :]), ALU.mult)
                        nf = len(fcs)
                        for dcp in range(8):
                            w = wd[di % 2]
                            di += 1
                            P.dma("gpsimd", w(w.h[:, 0:nf, :]),
                                  wd_d(wd_d.h.ap()[l, fcs[0] * 128:(fcs[-1] + 1) * 128, dcp * 256:(dcp + 1) * 256]
                                       .rearrange("(f p) d -> p f d", p=128)))
                            for ds in range(2):
                                dc = dcp * 2 + ds
                                for tb in range(2):
                                    pd = nb()
                                    for fi in range(nf):
                                        P.mm(pd(pd.h[:, :]), w(w.h[:, fi, ds * 128:(ds + 1) * 128]),
                                             act(act.h[:, fi, tb * 512:(tb + 1) * 512], (fi, tb)), start=(fi == 0), stop=(fi == nf - 1))
                                    xv = xt(xt.h[:, dc, tb * 512:(tb + 1) * 512], ("x", tb * 512))
                                    P.stt("vector", xv, pd(pd.h[:, :]), 0.5, xv, ALU.mult, ALU.add)
                    ev = P.dma("sync", xdst(xdst.h.ap()[:, :, tt * 1024:(tt + 1) * 1024].rearrange("c p t -> p c t"), 2 * tt, 2 * tt + 1),
                               xt(xt.h[:, :, :], ("x", 0), ("x", 512)))
                    if xdst is out_d:
                        out_events.append(ev)
            P.barrier()

        def build_tables(st):
            dall = sbt(st, "dall", [128, 7, 16, 128], BF16)
            with ExitStack() as s2:
                rbx = sbt(s2, "rbx", [33, 16], F32)
                rbb = sbt(s2, "rbb", [33, 16], BF16)
                ohb = sbt(s2, "ohb", [33, 4096], BF16)
                fsb = sbt(s2, "fsb", [16, 4096], BF16)
                hk = sbt(s2, "hk", [128, 16, 128], BF16)
                P.memset("vector", rbx(rbx.h[:, :]), NEGB)
                P.dma("sync", rbx(rbx.h[0:32, :]), rb_d(rb_d.h.ap()))
                P.copy("vector", rbb(rbb.h[:, :]), rbx(rbx.h[:, :]))
                P.dma("gpsimd", ohb(ohb.h[:, :]), oh_d(oh_d.h.ap()))
                for n in range(8):
                    pb = nb()
                    P.mm(pb(pb.h[0:16, :]), rbb(rbb.h[:, :]), ohb(ohb.h[:, n * 512:(n + 1) * 512]))
                    P.copy("vector", fsb(fsb.h[:, n * 512:(n + 1) * 512]), pb(pb.h[0:16, :]))
                P.dma("sync", fd_d(fd_d.h.ap()), fsb(fsb.h[:, :]))
                j128 = sbt(s2, "j128b", [128, 128], BF16)
                P.copy("vector", j128(j128.h[:, :]), cv("j128"))
                for dl in range(6):
                    src = bass.AP(fd_d.h, 2048 + dl * 128 - 127, [[1, 128], [4096, 16], [1, 128]])
                    P.dma("sync", hk(hk.h[:, :, :]), fd_d(src))
                    for n in range(4):
                        pb = nb()
                        P.mm(pb(pb.h[:, :]), j128(j128.h[:, :]), hk(hk.h[:, n * 4:(n + 1) * 4, :]))
                        P.copy("vector", dall(dall.h[:, dl, n * 4:(n + 1) * 4, :]), pb(pb.h[:, :]))
                for h in range(16):
                    P.tt("vector", dall(dall.h[:, 6, h, :]), dall(dall.h[:, 5, h, :]), cv("m6"), ALU.mult)
                    P.tt("vector", dall(dall.h[:, 6, h, :]), dall(dall.h[:, 6, h, :]), cv("n6"), ALU.add)
            P.barrier()
            return dall

        def mixer(l, xsrc, xdst):
            w_in = dr["w_in"]

            def wcols(c0, n):
                return w_in(w_in.h.ap()[l, :, c0:c0 + n].rearrange("(c p) f -> p c f", p=128))

            with ExitStack() as sq_:
                qT = sbt(sq_, "m_qT", [128, 8, S], BF16)
                with ExitStack() as st:
                    ksd = sbt(st, "m_ksd", [128, 4, S], BF16)
                    kwd = sbt(st, "m_kwd", [128, 4, S], BF16)
                    kcT = sbt(st, "m_kcT", [128, 2, S], BF16)
                    vcT = sbt(st, "m_vcT", [128, 2, S], BF16)
                    vsa = sbt(st, "m_vsa", [128, 16, 4, 65], BF16)
                    vwa = sbt(st, "m_vwa", [128, 16, 4, 65], BF16)
                    gat = sbt(st, "m_gat", [128, 16, 48], F32)
                    P.memset("vector", vsa(vsa.h[:, :, :, 64:65]), 1.0)
                    P.memset("vector", vwa(vwa.h[:, :, :, 64:65]), 1.0)
                    with ExitStack() as sa:
                        xt = sbt(sa, "a_x", [128, 16, 512], F32)
                        ht = sbt(sa, "a_h", [128, 16, 512], BF16)
                        sq = [sbt(sa, "a_sq%d" % i, [128, 512], BF16) for i in range(2)]
                        sd = sbt(sa, "a_sd", [128, 512], F32)
                        rs = sbt(sa, "a_rs", [128, 512], F32)
                        wq = [sbt(sa, "a_wq%d" % i, [128, 16, 128], BF16) for i in range(3)]
                        wv = sbt(sa, "a_wv", [128, 16, 560], BF16)
                        wi = 0
                        for tb in range(4):
                            t0 = tb * 512
                            make_h((xt, 0), (ht, 0), xsrc, l, "nm", t0, 0, sq, sd)
                            hv = lambda c: ht(ht.h[:, c, :], ("h", 0))
                            jobs = [("q", c) for c in range(8)] + [("ks", g) for g in range(4)] + \
                                   [("kw", g) for g in range(4)] + [("kc", c) for c in range(2)] + [("vc", c) for c in range(2)]
                            for jn, (kind, ix) in enumerate(jobs):
                                w = wq[wi % 3]
                                wi += 1
                                wall = w(w.h[:, :, :])
                                if kind == "q":
                                    lds = [(wall, wcols(ix * 128, 128))]
                                elif kind in ("ks", "kw"):
                                    c0 = OFF_KV + (2 if kind == "ks" else 4) * 256 + ix * 64
                                    lds = [(w(w.h[:, :, 0:64]), wcols(c0, 64)), (w(w.h[:, :, 64:128]), wcols(c0, 64))]
                                else:
                                    c0 = OFF_KV + (0 if kind == "kc" else 1) * 256 + ix * 128
                                    lds = [(wall, wcols(c0, 128))]
                                wfetch(tb, wall, wc16_d, wc16_d.h.ap()[jn], jn, lds)
                                wvw = wall
                                pz = nb()
                                for c in range(16):
                                    P.mm(pz(pz.h[:, :]), V(w.h[:, c, :], wvw.trks), hv(c), start=(c == 0), stop=(c == 15))
                                if kind in ("kc", "vc"):
                                    dst = kcT if kind == "kc" else vcT
                                    P.copy("scalar", dst(dst.h[:, ix, t0:t0 + 512], tb), pz(pz.h[:, :]))
                                    continue
                                s = sq[wi % 2]
                                P.act(s(s.h[:, :]), pz(pz.h[:, :]), AF.Square)
                                pm = nb()
                                P.mm(pm(pm.h[:, :]), blk_bf(blk_bf.h[:, :]), s(s.h[:, :]))
                                P.act(rs(rs.h[:, :]), pm(pm.h[:, :]), AF.Sqrt, bias=epsc(epsc.h[:, 0:1]), scale=1.0 / 64)
                                P.recip(rs(rs.h[:, :]), rs(rs.h[:, :]))
                                if kind == "q":
                                    dstv = qT(qT.h[:, ix, t0:t0 + 512], tb)
                                    gc = gsc(gsc.h[:, l, 49:50])
                                    P.stt("vector", dstv, pz(pz.h[:, :]), gc, rs(rs.h[:, :]), ALU.mult, ALU.mult)
                                else:
                                    dst = ksd if kind == "ks" else kwd
                                    kn = 1 if kind == "ks" else 2
                                    gc = prm(prm.h[:, l, PRM["kn"] + kn:PRM["kn"] + kn + 1])
                                    P.stt("vector", dst(dst.h[:, ix, t0:t0 + 512], tb), pz(pz.h[:, :]), gc, rs(rs.h[:, :]), ALU.mult, ALU.mult)
                            wfetch(tb, wv(wv.h[:, :, :]), wcv_d, wcv_d.h.ap(), 0,
                                   [(wv(wv.h[:, :, 0:256]), wcols(OFF_KV + 3 * 256, 256)),
                                    (wv(wv.h[:, :, 256:512]), wcols(OFF_KV + 5 * 256, 256)),
                                    (wv(wv.h[:, :, 512:560]), wcols(OFF_NG, 48))])
                            wvt = wv(wv.h[:, :, :]).trks
                            for sub in range(4):
                                ti = tb * 4 + sub
                                p1, p2 = nb(), nb()
                                for c in range(16):
                                    P.mm(p1(p1.h[:, :]), ht(ht.h[:, c, sub * 128:(sub + 1) * 128], ("h", 0)), V(wv.h[:, c, 0:512], wvt),
                                         start=(c == 0), stop=(c == 15))
                                for c in range(16):
                                    P.mm(p2(p2.h[:, 0:48]), ht(ht.h[:, c, sub * 128:(sub + 1) * 128], ("h", 0)), V(wv.h[:, c, 512:560], wvt),
                                         start=(c == 0), stop=(c == 15))
                                P.copy("vector", vsa(vsa.h[:, ti, :, 0:64], ti), p1(p1.h[:, 0:256].rearrange("p (g d) -> p g d", g=4)))
                                P.copy("vector", vwa(vwa.h[:, ti, :, 0:64], ti), p1(p1.h[:, 256:512].rearrange("p (g d) -> p g d", g=4)))
                                P.act(gat(gat.h[:, ti, :], ti), p2(p2.h[:, 0:48]), AF.Sigmoid)
                    P.barrier()
                    if mstop != "A":
                        dall = build_tables(st)
                    if mstop in ("A", "T"):
                        with ExitStack() as sx:
                            xc = sbt(sx, "passx", [128, 16, 512], F32)
                            for tb in range(4):
                                P.dma("sync", xc(xc.h[:, :, :]), xsrc(xsrc.h.ap()[:, :, tb * 512:(tb + 1) * 512].rearrange("c p t -> p c t"), tb))
                                P.dma("sync", xdst(xdst.h.ap()[:, :, tb * 512:(tb + 1) * 512].rearrange("c p t -> p c t"), tb), xc(xc.h[:, :, :]))
                            P.barrier()
                        return
                    attention(l, st, qT, ksd, kwd, kcT, vcT, vsa, vwa, gat, dall)
                    bank_n[0] = 8
                P.barrier()
                if mstop == "X":
                    with ExitStack() as sx:
                        xc = sbt(sx, "passx2", [128, 16, 512], F32)
                        for tb in range(4):
                            P.dma("sync", xc(xc.h[:, :, :]), xsrc(xsrc.h.ap()[:, :, tb * 512:(tb + 1) * 512].rearrange("c p t -> p c t"), tb))
                            P.dma("sync", xdst(xdst.h.ap()[:, :, tb * 512:(tb + 1) * 512].rearrange("c p t -> p c t"), tb), xc(xc.h[:, :, :]))
                        P.barrier()
                    return
                stage_d(l, qT, xsrc, xdst, wcols)
            P.barrier()

        def dump(name, view_fn_list):
            pass

        def dump_attn_inputs(qT, ksd, kwd, kcT, vcT, vsa, vwa, gat, dall):
            with ExitStack() as sd_:
                tmp = sbt(sd_, "dbg_tmp", [128, 8 * 512], F32)
                if "qT" in dbg_d:
                    o = dbg_d["qT"]
                    P.copy("vector", tmp(tmp.h[:, :].rearrange("p (c t) -> p c t", c=8)), qT(qT.h[:, :, 0:512], 0))
                    P.dma("sync", o(o.h.ap()), tmp(tmp.h[:, :]))
                if "ksd" in dbg_d:
                    o = dbg_d["ksd"]
                    t2 = sbt(sd_, "dbg_t2", [128, 4 * 512], F32)
                    P.copy("vector", t2(t2.h[:, :].rearrange("p (c t) -> p c t", c=4)), ksd(ksd.h[:, :, 0:512], 0))
                    P.dma("sync", o(o.h.ap()), t2(t2.h[:, :]))
                if "gat" in dbg_d:
                    o = dbg_d["gat"]
                    P.dma("sync", o(o.h.ap()), gat(gat.h[:, :, :].rearrange("p a b -> p (a b)") if False else gat.h[:, 0, :], 0))
                if "dall" in dbg_d:
                    o = dbg_d["dall"]
                    t3 = sbt(sd_, "dbg_t3", [128, 7 * 128], F32)
                    P.copy("vector", t3(t3.h[:, :].rearrange("p (c t) -> p c t", c=7)), dall(dall.h[:, :, 5, :]))
                    P.dma("sync", o(o.h.ap()), t3(t3.h[:, :]))

        def attention(l, st, qT, ksd, kwd, kcT, vcT, vsa, vwa, gat, dall):
            with ExitStack() as sc:
                kcn = sbt(sc, "c_kcn", [128, 4, 128], BF16)
                vcx = sbt(sc, "c_vcx", [128, 4, 97], BF16)
                P.memset("vector", vcx(vcx.h[:, :, 64:65]), 1.0)
                for g in range(4):
                    P.copy("vector", vcx(vcx.h[0:127, g, 65:97], "ov"), cv("ov", rows=127))
                expb = sbt(sc, "c_expb", [32, 16, 128], BF16)
                P.dma("gpsimd", expb(expb.h[:, :, :]), expm_d(expm_d.h.ap().rearrange("p (a b) -> p a b", a=16)))
                j127 = sbt(sc, "c_j127", [128, 128], BF16)
                P.copy("vector", j127(j127.h[:, :]), cv("j127"))
                with ExitStack() as s2:
                    w1 = sbt(s2, "c_w1", [128, 32, 256], BF16)
                    w2 = sbt(s2, "c_w2", [128, 2, 128], BF16)
                    posb = sbt(s2, "c_posb", [128, 32], BF16)
                    pbias = sbt(s2, "c_pbias", [128, 2], F32)
                    hid = sbt(s2, "c_hid", [128, 2, 128], BF16)
                    sqc = sbt(s2, "c_sq", [128, 128], BF16)
                    rsc = sbt(s2, "c_rs", [128, 128], F32)
                    for kind in ("k", "v"):
                        w1d = dr["cmp_%s_w1" % kind]
                        w2d = dr["cmp_%s_w2" % kind]
                        src1 = w1d.h.ap()[l].rearrange("(l d) f -> d l f", d=64)
                        P.dma("gpsimd", w1(w1.h[0:64, :, :], "a"), w1d(src1))
                        P.dma("gpsimd", w1(w1.h[64:128, :, :], "b"), w1d(src1))
                        src2 = w2d.h.ap()[l].rearrange("(fh p) d -> p fh d", p=128)
                        P.dma("gpsimd", w2(w2.h[:, :, 0:64], "a"), w2d(src2))
                        P.dma("gpsimd", w2(w2.h[:, :, 64:128], "b"), w2d(src2))
                        w1t = w1(w1.h[:, :, :], "a", "b").trks
                        w2t = w2(w2.h[:, :, :], "a", "b").trks
                        pn = "pk" if kind == "k" else "pv"
                        P.copy("vector", posb(posb.h[:, :]), prm(prm.h[:, l, PRM[pn]:PRM[pn] + 32]))
                        for fh in range(2):
                            pb = nb()
                            for ll in range(32):
                                P.mm(pb(pb.h[:, 0:1]), V(w1.h[0:64, ll, fh * 128:(fh + 1) * 128], w1t), posb(posb.h[0:64, ll:ll + 1]),
                                     start=(ll == 0), stop=(ll == 31))
                            P.copy("vector", pbias(pbias.h[:, fh:fh + 1]), pb(pb.h[:, 0:1]))
                        srcT = kcT if kind == "k" else vcT
                        for g in range(4):
                            hf, ch = g % 2, g // 2
                            for fh in range(2):
                                pb = nb()
                                for ll in range(32):
                                    r16 = srcT.h[hf * 64:(hf + 1) * 64, ch, :].rearrange("p (c s) -> p c s", s=16)
                                    rhs = r16[:, 0:127, ll] if ll < 16 else r16[:, 1:128, ll - 16]
                                    P.mm(pb(pb.h[:, 0:127]), V(w1.h[hf * 64:(hf + 1) * 64, ll, fh * 128:(fh + 1) * 128], w1t),
                                         srcT(rhs, 0, 1, 2, 3), start=(ll == 0), stop=(ll == 31))
                                P.act(hid(hid.h[:, fh, 0:127], fh), pb(pb.h[:, 0:127]), AF.Silu, bias=pbias(pbias.h[:, fh:fh + 1]))
                            if kind == "k":
                                pb = nb()
                                for fh in range(2):
                                    P.mm(pb(pb.h[:, 0:127]), V(w2.h[:, fh, :], w2t), hid(hid.h[:, fh, 0:127], fh), start=(fh == 0), stop=(fh == 1))
                                P.act(sqc(sqc.h[:, 0:127]), pb(pb.h[:, 0:127]), AF.Square)
                                pm = nb()
                                P.mm(pm(pm.h[:, 0:127]), blk_bf(blk_bf.h[:, :]), sqc(sqc.h[:, 0:127]))
                                P.act(rsc(rsc.h[:, 0:127]), pm(pm.h[:, 0:127]), AF.Sqrt, bias=epsc(epsc.h[:, 0:1]), scale=1.0 / 64)
                                P.recip(rsc(rsc.h[:, 0:127]), rsc(rsc.h[:, 0:127]))
                                gc = prm(prm.h[:, l, PRM["kn"]:PRM["kn"] + 1])
                                P.stt("vector", kcn(kcn.h[:, g, 0:127], g), pb(pb.h[:, 0:127]), gc, rsc(rsc.h[:, 0:127]), ALU.mult, ALU.mult)
                            else:
                                pb = nb()
                                for fh in range(2):
                                    P.mm(pb(pb.h[0:127, 0:64]), hid(hid.h[:, fh, 0:127], fh), V(w2.h[:, fh, 0:64], w2t), start=(fh == 0), stop=(fh == 1))
                                P.copy("vector", vcx(vcx.h[0:127, g, 0:64], ("v", g)), pb(pb.h[0:127, 0:64]))
                P.barrier()
                bct = [sbt(sc, "t_bch%d" % i, [128, 4, 128], BF16) for i in range(4)]
                bcf = [sbt(sc, "t_bcf%d" % i, [128, 4, 128], BF16) for i in range(4)]
                pT = [sbt(sc, "t_pT%d" % i, [128, 512], BF16) for i in range(5)]
                aaccs = [sbt(sc, "t_aacc%d" % k, [128, 16, 64], F32) for k in range(2)]
                abf = sbt(sc, "t_abf", [128, 1024], BF16)
                rden = sbt(sc, "t_rden", [128, 4], F32)
                coef = sbt(sc, "t_coef", [128, 4], F32)
                imp = sbt(sc, "t_imp", [128, 32], F32)
                scr = sbt(sc, "t_scr", [128, 32], F32)
                wrk = sbt(sc, "t_wrk", [128, 32], F32)
                m8 = sbt(sc, "t_m8", [128, 8], F32)
                selnss = [[sbt(sc, "t_seln%d_%d" % (k, g), [128, 32], F32) for g in range(4)] for k in range(2)]
                selTs = [sbt(sc, "t_selT%d" % g, [32, 2, 128], BF16) for g in range(4)]
                bank_n[0] = 6
                cnt = {"p": 0, "b": 0, "a": 0}
                accb = [banks[6], banks[7]]

                def nacc():
                    b = accb[cnt["a"] % 2]
                    cnt["a"] += 1
                    return b

                o1, o2, o3 = CST["notf"][0], CST["addv"][0], CST["vneg"][0]

                def hd(g, s):
                    return 4 * g + 2 * (s % 2) + s // 2

                def tile_fns(i):
                    q0 = i * 128
                    aacc = aaccs[i % 2]
                    selns = selnss[i % 2]

                    def score(g, klhs, rows, bias_fn, mask_j):
                        p = pT[cnt["p"] % 5]
                        cnt["p"] += 1
                        pss = (nb(), nb())
                        idv = ident_bf(ident_bf.h[0:rows, 0:rows])
                        for hf in range(2):
                            rhs = qT(qT.h[hf * 64:(hf + 1) * 64, 2 * g:2 * g + 2, q0:q0 + 128], i // 4)
                            P.mm(pss[hf](pss[hf].h[0:rows, 0:256]), klhs(hf), rhs, start=True, stop=False)
                        for hf in range(2):
                            P.mm(pss[hf](pss[hf].h[0:rows, 0:256]), idv, bias_fn(hf), start=False, stop=(mask_j is None))
                        if mask_j is not None:
                            sT = selTs[g]
                            for hf in range(2):
                                P.mm(pss[hf](pss[hf].h[0:rows, 0:256]), expb(expb.h[:, mask_j, :]), sT(sT.h[:, :, :]), start=False, stop=True)
                        for hf in range(2):
                            P.act(p(p.h[0:rows, hf * 256:(hf + 1) * 256], hf), pss[hf](pss[hf].h[0:rows, 0:256]), AF.Exp)
                        return p

                    def combine(g, br, pbk, first):
                        w = 97 if br == 0 else 65
                        den = V(pbk.h[:, 0:4 * w].rearrange("p (s c) -> p s c", s=4)[:, :, 64], pbk(pbk.h[:, :]).trks)
                        P.ts("vector", rden(rden.h[:, :]), den, 1e-30, None, ALU.max)
                        P.recip(rden(rden.h[:, :]), rden(rden.h[:, :]))
                        for s in range(4):
                            h = hd(g, s)
                            if br == 0:
                                if s == 0:
                                    P.ts("vector", imp(imp.h[:, :]), pbk(pbk.h[:, s * 97 + 65:s * 97 + 97]), rden(rden.h[:, s:s + 1]), None, ALU.mult)
                                else:
                                    P.stt("vector", imp(imp.h[:, :]), pbk(pbk.h[:, s * 97 + 65:s * 97 + 97]), rden(rden.h[:, s:s + 1]),
                                          imp(imp.h[:, :]), ALU.mult, ALU.add)
                            P.tt("vector", coef(coef.h[:, s:s + 1]), rden(rden.h[:, s:s + 1]), gat(gat.h[:, i, 3 * h + br:3 * h + br + 1], i), ALU.mult)
                            if first:
                                P.ts("vector", aacc(aacc.h[:, h, :], h), pbk(pbk.h[:, s * w:s * w + 64]), coef(coef.h[:, s:s + 1]), None, ALU.mult)
                            else:
                                P.stt("vector", aacc(aacc.h[:, h, :], h), pbk(pbk.h[:, s * w:s * w + 64]), coef(coef.h[:, s:s + 1]),
                                      aacc(aacc.h[:, h, :], h), ALU.mult, ALU.add)

                    def branch(g, steps, pbk, vbuf, mask):
                        LA = 2
                        ps_ = [None] * len(steps)
                        for n in range(min(LA, len(steps))):
                            ps_[n] = steps[n][1]()
                        for n, (j, _) in enumerate(steps):
                            if n + LA < len(steps):
                                ps_[n + LA] = steps[n + LA][1]()
                            p = ps_[n]
                            for s in range(4):
                                P.mm(pbk(pbk.h[:, s * 65:(s + 1) * 65]), p(p.h[:, s * 128:(s + 1) * 128], s // 2), vbuf(vbuf.h[:, j, g, :], j),
                                     start=(n == 0 and s == 0), stop=(n == len(steps) - 1 and s == 3))


                    def phase1():
                        for g in range(4):
                            bh, bf_ = bct[cnt["b"] % 4], bcf[cnt["b"] % 4]
                            cnt["b"] += 1
                            src = bass.AP(fd_d.h, 4 * g * 4096 + 2048 + q0 - 16 * 126 - 31, [[16, 128], [4096, 4], [1, 128]])
                            P.dma("sync", bh(bh.h[:, :, :]), fd_d(src))
                            pb = nb()
                            P.mm(pb(pb.h[0:127, :]), j127(j127.h[0:127, 0:127]), bh(bh.h[0:127, :, :]))
                            P.copy("scalar", bf_(bf_.h[0:127, :, :]), pb(pb.h[0:127, :]))
                            p = score(g, lambda hf: kcn(kcn.h[hf * 64:(hf + 1) * 64, g, 0:127], g), 127,
                                      lambda hf: bf_(bf_.h[0:127, 2 * hf:2 * hf + 2, :]), None)
                            po = nacc()
                            for s in range(4):
                                P.mm(po(po.h[:, s * 97:(s + 1) * 97]), p(p.h[0:127, s * 128:(s + 1) * 128], s // 2),
                                     vcx(vcx.h[0:127, g, :], "ov", ("v", g), 0), start=(s == 0), stop=(s == 3))
                            combine(g, 0, po, True)
                            P.tt("vector", scr(scr.h[:, :]), imp(imp.h[:, :]), cst(cst.h[:, o1 + i * 32:o1 + (i + 1) * 32]), ALU.mult)
                            P.tt("vector", scr(scr.h[:, :]), scr(scr.h[:, :]), cst(cst.h[:, o2 + i * 32:o2 + (i + 1) * 32]), ALU.add)
                            P.op("vector", lambda e: e.max(m8.h[:, :], scr.h[:, :]), [scr(scr.h[:, :])], [m8(m8.h[:, :])])
                            P.op("vector", lambda e: e.match_replace(wrk.h[:, :], m8.h[:, :], scr.h[:, :], -3e30),
                                 [scr(scr.h[:, :]), m8(m8.h[:, :])], [wrk(wrk.h[:, :])])
                            P.op("vector", lambda e: e.max(m8.h[:, :], wrk.h[:, :]), [wrk(wrk.h[:, :])], [m8(m8.h[:, :])])
                            P.op("vector", lambda e: e.match_replace(wrk.h[:, :], m8.h[:, :], wrk.h[:, :], -3e30),
                                 [wrk(wrk.h[:, :]), m8(m8.h[:, :])], [wrk(wrk.h[:, :])])
                            seln = selns[g]
                            P.tt("vector", seln(seln.h[:, :]), scr(scr.h[:, :]), wrk(wrk.h[:, :]), ALU.subtract)
                            P.ts("vector", seln(seln.h[:, :]), seln(seln.h[:, :]), 1.0, -1.0, ALU.min, ALU.add)
                            P.stt("vector", seln(seln.h[:, :]), seln(seln.h[:, :]), 30000.0, cst(cst.h[:, o3 + i * 32:o3 + (i + 1) * 32]),
                                  ALU.mult, ALU.add)

                    def phase23():
                        for g in range(4):
                            pt = nb()
                            P.transpose(pt(pt.h[0:32, 0:128]), selns[g](selns[g].h[:, :]), cv("ident"))
                            sT = selTs[g]
                            for s in range(2):
                                P.copy("vector", sT(sT.h[:, s, :]), pt(pt.h[0:32, 0:128]))
                        jl = max(0, i - 4)
                        segs = []
                        for g in range(4):
                            steps = []
                            for j in range(jl, i + 1):
                                dw = 6 if i - j == 4 else i - j
                                steps.append((j, (lambda j=j, dw=dw, g=g: score(
                                    g, lambda hf: kwd(kwd.h[hf * 64:(hf + 1) * 64, g, j * 128:(j + 1) * 128], j // 4), 128,
                                    lambda hf: dall(dall.h[:, dw, 4 * g + 2 * hf:4 * g + 2 * hf + 2, :]), None))))
                            segs.append((g, steps, vwa, 2))
                        for g in range(4):
                            steps = []
                            for j in range(i + 1):
                                dl = min(i - j, 5)
                                steps.append((j, (lambda j=j, dl=dl, g=g: score(
                                    g, lambda hf: ksd(ksd.h[hf * 64:(hf + 1) * 64, g, j * 128:(j + 1) * 128], j // 4), 128,
                                    lambda hf: dall(dall.h[:, dl, 4 * g + 2 * hf:4 * g + 2 * hf + 2, :]), j))))
                            segs.append((g, steps, vsa, 1))
                        flat = [(si, n, j, fn) for si, (g_, steps_, vb_, br_) in enumerate(segs) for n, (j, fn) in enumerate(steps_)]
                        LA = 2
                        ps_ = {}
                        accs = {}
                        for n in range(min(LA, len(flat))):
                            ps_[n] = flat[n][3]()
                        for n, (si, sn, j, fn) in enumerate(flat):
                            if n + LA < len(flat):
                                ps_[n + LA] = flat[n + LA][3]()
                            g_, steps_, vb_, br_ = segs[si]
                            if sn == 0:
                                accs[si] = nacc()
                            pbk = accs[si]
                            p = ps_.pop(n)
                            lastn = (sn == len(steps_) - 1)
                            for s in range(4):
                                P.mm(pbk(pbk.h[:, s * 65:(s + 1) * 65]), p(p.h[:, s * 128:(s + 1) * 128], s // 2), vb_(vb_.h[:, j, g_, :], j),
                                     start=(sn == 0 and s == 0), stop=(lastn and s == 3))
                            if lastn:
                                combine(g_, br_, pbk, False)
                        for c4 in range(2):
                            pt = nb()
                            for cc in range(4):
                                c = c4 * 4 + cc
                                P.transpose(pt(pt.h[:, cc * 128:(cc + 1) * 128]),
                                            aacc(aacc.h[:, 2 * c:2 * c + 2, :].rearrange("p h d -> p (h d)"), 2 * c, 2 * c + 1), cv("ident"))
                            P.copy("scalar", qT(qT.h[:, c4 * 4:(c4 + 1) * 4, q0:q0 + 128], ("a", i), i // 4),
                                   pt(pt.h[:, :].rearrange("p (c t) -> p c t", c=4)))

                    return phase1, phase23

                fns = [tile_fns(i) for i in range(ntiles)]
                if ntiles:
                    fns[0][0]()
                for i in range(ntiles):
                    if i + 1 < ntiles:
                        fns[i + 1][0]()
                    fns[i][1]()


        def stage_d(l, aT, xsrc, xdst, wcols):
            wpn, wps, wo = dr["w_proj_nsa"], dr["w_proj_sgu"], dr["w_out"]
            with ExitStack() as st:
                xt = sbt(st, "d_x", [128, 16, 512], F32)
                ht = sbt(st, "d_h", [128, 16, 512], BF16)
                sq = [sbt(st, "d_sq%d" % i, [128, 512], BF16) for i in range(2)]
                sd = sbt(st, "d_sd", [128, 512], F32)
                uT = sbt(st, "d_uT", [128, 8, 512], BF16)
                sgT = sbt(st, "d_sgT", [128, 8, 512], BF16)
                mrg = sbt(st, "d_mrg", [128, 16, 512], BF16)
                w16 = [sbt(st, "d_w16_%d" % i, [128, 16, 128], BF16) for i in range(4)]
                w8 = [sbt(st, "d_w8_%d" % i, [128, 8, 128], BF16) for i in range(4)]
                wv2 = sbt(st, "d_wv2", [128, 16, 512], BF16)
                vg = sbt(st, "d_vg", [128, 1024], F32)
                vsq = sbt(st, "d_vsq", [128, 1024], F32)
                vln = sbt(st, "d_vln", [128, 1024], BF16)
                stat = sbt(st, "d_stat", [128, 8], F32)
                lng = sbt(st, "d_lng", [128, 1024], F32)
                lnb = sbt(st, "d_lnb", [128, 1024], F32)
                wsT = sbt(st, "d_wsT", [128, 8, 128], BF16)
                wsf = sbt(st, "d_wsf", [128, 8, 128], F32)
                sbb = sbt(st, "d_sbb", [1, 1024], BF16)
                sg1 = sbt(st, "d_sg1", [128, 512], F32)
                sg2 = sbt(st, "d_sg2", [128, 512], F32)
                t1 = sbt(st, "d_t1", [128, 512], F32)
                P.dma("sync", lng(lng.h[:, :]), sgb_d(sgb_d.h.ap()[l, 0]))
                P.dma("sync", lnb(lnb.h[:, :]), sgb_d(sgb_d.h.ap()[l, 1]))
                P.dma("sync", wsf(wsf.h[:, :, :]), sgw_d(sgw_d.h.ap()[l]))
                for g in range(8):
                    P.tt("vector", wsT(wsT.h[:, g, :]), wsf(wsf.h[:, g, :]), cv("triu"), ALU.mult)
                P.dma("gpsimd", sbb(sbb.h[:, :]), sgub_d(sgub_d.h.ap()[l]))
                k16 = 0
                k8 = 0
                for tb in range(4):
                    t0 = tb * 512
                    make_h((xt, 0), (ht, 0), xsrc, l, "nm", t0, 0, sq, sd)
                    for c in range(8):
                        w = w16[k16 % 4]
                        k16 += 1
                        wfetch(tb, w(w.h[:, :, :]), wc16_d, wc16_d.h.ap()[20 + c], 20 + c, [(w(w.h[:, :, :]), wcols(OFF_UV + c * 128, 128))])
                        pz = nb()
                        for k in range(16):
                            P.mm(pz(pz.h[:, :]), w(w.h[:, k, :]), ht(ht.h[:, k, :], ("h", 0)), start=(k == 0), stop=(k == 15))
                        P.act(uT(uT.h[:, c, :], c), pz(pz.h[:, :]), AF.Gelu_apprx_tanh)
                    for sub in range(4):
                        for half in range(2):
                            wfetch(tb if sub == 0 else 1, wv2(wv2.h[:, :, :]), wcv2_d, wcv2_d.h.ap()[half], half,
                                   [(wv2(wv2.h[:, :, :]), wcols(OFF_UV + 1024 + half * 512, 512))])
                            pz = nb()
                            for k in range(16):
                                P.mm(pz(pz.h[:, :]), ht(ht.h[:, k, sub * 128:(sub + 1) * 128], ("h", 0)), wv2(wv2.h[:, k, :]),
                                     start=(k == 0), stop=(k == 15))
                            P.act(vg(vg.h[:, half * 512:(half + 1) * 512], half), pz(pz.h[:, :]), AF.Gelu_apprx_tanh)
                        vga = vg(vg.h[:, :], 0, 1)
                        P.op("vector", lambda e: e.tensor_reduce(stat.h[:, 0:1], vg.h[:, :], AX.X, ALU.add), [vga], [stat(stat.h[:, 0:1], 0)])
                        P.act(vsq(vsq.h[:, :]), vga, AF.Square)
                        P.op("vector", lambda e: e.tensor_reduce(stat.h[:, 1:2], vsq.h[:, :], AX.X, ALU.add), [vsq(vsq.h[:, :])], [stat(stat.h[:, 1:2], 1)])
                        P.ts("vector", stat(stat.h[:, 2:3], 2), stat(stat.h[:, 0:1], 0), 1.0 / 1024, None, ALU.mult)
                        P.tt("vector", stat(stat.h[:, 3:4], 3), stat(stat.h[:, 2:3], 2), stat(stat.h[:, 2:3], 2), ALU.mult)
                        P.stt("vector", stat(stat.h[:, 4:5], 4), stat(stat.h[:, 1:2], 1), 1.0 / 1024, stat(stat.h[:, 3:4], 3), ALU.mult, ALU.subtract)
                        P.act(stat(stat.h[:, 5:6], 5), stat(stat.h[:, 4:5], 4), AF.Sqrt, bias=epsc(epsc.h[:, 0:1]), scale=1.0)
                        P.recip(stat(stat.h[:, 6:7], 6), stat(stat.h[:, 5:6], 5))
                        P.ts("vector", vsq(vsq.h[:, :]), vga, stat(stat.h[:, 2:3], 2), stat(stat.h[:, 6:7], 6), ALU.subtract, ALU.mult)
                        P.tt("vector", vsq(vsq.h[:, :]), vsq(vsq.h[:, :]), lng(lng.h[:, :]), ALU.mult)
                        P.tt("vector", vln(vln.h[:, :]), vsq(vsq.h[:, :]), lnb(lnb.h[:, :]), ALU.add)
                        for gh in range(2):
                            pz = nb()
                            for gg in range(4):
                                g = gh * 4 + gg
                                P.mm(pz(pz.h[:, gg * 128:(gg + 1) * 128]), vln(vln.h[:, g * 128:(g + 1) * 128]), wsT(wsT.h[:, g, :]), start=(gg == 0), stop=False)
                                P.mm(pz(pz.h[:, gg * 128:(gg + 1) * 128]), ones_bf(ones_bf.h[0:1, :]), sbb(sbb.h[0:1, g * 128:(g + 1) * 128]), start=False, stop=(gg == 3))
                            P.tt("vector", sgT(sgT.h[:, gh * 4:(gh + 1) * 4, sub * 128:(sub + 1) * 128], (gh, sub)),
                                 pz(pz.h[:, :].rearrange("p (g t) -> p g t", g=4)),
                                 uT(uT.h[:, gh * 4:(gh + 1) * 4, sub * 128:(sub + 1) * 128], *range(gh * 4, gh * 4 + 4)), ALU.mult)
                    sgk = [(gh, sub) for gh in range(2) for sub in range(4)]
                    for j in range(16):
                        wa, wb_ = w8[k8 % 4], w8[(k8 + 1) % 4]
                        k8 += 2
                        wg1, wg2 = w16[k16 % 4], w16[(k16 + 1) % 4]
                        k16 += 2
                        wfetch(tb, wa(wa.h[:, :, :]), wc8_d, wc8_d.h.ap()[2 * j], 2 * j,
                               [(wa(wa.h[:, :, :]), wpn(wpn.h.ap()[l, :, j * 128:(j + 1) * 128].rearrange("(c p) f -> p c f", p=128)))])
                        wfetch(tb, wb_(wb_.h[:, :, :]), wc8_d, wc8_d.h.ap()[2 * j + 1], 2 * j + 1,
                               [(wb_(wb_.h[:, :, :]), wps(wps.h.ap()[l, :, j * 128:(j + 1) * 128].rearrange("(c p) f -> p c f", p=128)))])
                        wfetch(tb, wg1(wg1.h[:, :, :]), wc16_d, wc16_d.h.ap()[28 + 2 * j], 28 + 2 * j, [(wg1(wg1.h[:, :, :]), wcols(OFF_MG + j * 128, 128))])
                        wfetch(tb, wg2(wg2.h[:, :, :]), wc16_d, wc16_d.h.ap()[29 + 2 * j], 29 + 2 * j, [(wg2(wg2.h[:, :, :]), wcols(OFF_MG + D + j * 128, 128))])
                        pa, pb, pg1, pg2 = nb(), nb(), nb(), nb()
                        for k in range(8):
                            P.mm(pa(pa.h[:, :]), wa(wa.h[:, k, :]), aT(aT.h[:, k, t0:t0 + 512], *[("a", i) for i in range(tb * 4, tb * 4 + 4)]),
                                 start=(k == 0), stop=(k == 7))
                        for k in range(8):
                            P.mm(pb(pb.h[:, :]), wb_(wb_.h[:, k, :]), sgT(sgT.h[:, k, :], *sgk), start=(k == 0), stop=(k == 7))
                        for k in range(16):
                            P.mm(pg1(pg1.h[:, :]), wg1(wg1.h[:, k, :]), ht(ht.h[:, k, :], ("h", 0)), start=(k == 0), stop=(k == 15))
                        for k in range(16):
                            P.mm(pg2(pg2.h[:, :]), wg2(wg2.h[:, k, :]), ht(ht.h[:, k, :], ("h", 0)), start=(k == 0), stop=(k == 15))
                        P.act(sg1(sg1.h[:, :]), pg1(pg1.h[:, :]), AF.Sigmoid)
                        P.act(sg2(sg2.h[:, :]), pg2(pg2.h[:, :]), AF.Sigmoid)
                        P.tt("vector", t1(t1.h[:, :]), sg1(sg1.h[:, :]), pa(pa.h[:, :]), ALU.mult)
                        P.tt("vector", sg2(sg2.h[:, :]), sg2(sg2.h[:, :]), pb(pb.h[:, :]), ALU.mult)
                        P.tt("vector", mrg(mrg.h[:, j, :], j), t1(t1.h[:, :]), sg2(sg2.h[:, :]), ALU.add)
                    for j in range(16):
                        w = w16[k16 % 4]
                        k16 += 1
                        wfetch(tb, w(w.h[:, :, :]), wc16_d, wc16_d.h.ap()[60 + j], 60 + j,
                               [(w(w.h[:, :, :]), wo(wo.h.ap()[l, :, j * 128:(j + 1) * 128].rearrange("(c p) f -> p c f", p=128)))])
                        pz = nb()
                        for k in range(16):
                            P.mm(pz(pz.h[:, :]), w(w.h[:, k, :]), mrg(mrg.h[:, k, :], *range(16)), start=(k == 0), stop=(k == 15))
                        xv = xt(xt.h[:, j, :], ("x", 0))
                        P.tt("vector", xv, xv, pz(pz.h[:, :]), ALU.add)
                    P.dma("sync", xdst(xdst.h.ap()[:, :, t0:t0 + 512].rearrange("c p t -> p c t"), tb), xt(xt.h[:, :, :], ("x", 0)))

        cur = x_in
        for l in range(depth):
            last = (l == depth - 1)
            todo = [p for p in "1m2" if p in phases]
            for p in todo:
                dst = out_d if (last and p == todo[-1]) else xs_d
                if p == "1":
                    ffn(l, "ffn1", cur, dst)
                elif p == "m":
                    mixer(l, cur, dst)
                else:
                    ffn(l, "ffn2", cur, dst)
                cur = xs_d
        P.barrier()
        P.emit(top)
    return nc


_NC_CACHE = {}


def _prep_shared(inputs):
    f = lambda a: np.ascontiguousarray(np.asarray(a, np.float32))
    sh = {}
    for k in ("ffn1_w_gate", "ffn1_w_up", "ffn1_w_down", "ffn2_w_gate", "ffn2_w_up", "ffn2_w_down", "w_in",
              "cmp_k_w1", "cmp_k_w2", "cmp_v_w1", "cmp_v_w2", "w_proj_nsa", "w_proj_sgu", "w_out"):
        sh[k] = f(inputs[k])
    sh["cst"] = CST_ARR
    prm = np.zeros((L, 128, NPRM), np.float32)
    for l in range(L):
        for nm, key in (("n1", "ffn1_norm"), ("nm", "mix_norm"), ("n2", "ffn2_norm")):
            prm[l, :, PRM[nm]:PRM[nm] + 16] = f(inputs[key])[l].reshape(16, 128).T
        prm[l, :, PRM["qn"]] = np.tile(f(inputs["q_norm"])[l], 2)
        for i in range(3):
            prm[l, :, PRM["kn"] + i] = np.tile(f(inputs["k_norm"])[l, i], 2)
        prm[l, :, PRM["pk"]:PRM["pk"] + 32] = np.tile(f(inputs["cmp_pos_k"])[l].T, (2, 1))
        prm[l, :, PRM["pv"]:PRM["pv"] + 32] = np.tile(f(inputs["cmp_pos_v"])[l].T, (2, 1))
    sh["prm"] = prm
    rb = f(inputs["rel_bias"])
    rbp = np.zeros((32, 16), np.float32)
    for g in range(4):
        for s in range(4):
            rbp[:, 4 * g + s] = rb[:, 4 * g + 2 * (s % 2) + s // 2]
    sh["rbp"] = rbp
    sh["oh"] = OH_ARR
    sh["expm"] = EXPM_ARR
    sh["sgub"] = np.ascontiguousarray(f(inputs["sgu_b"]).reshape(L, 1, 1024))
    sgb = np.zeros((L, 2, 128, 1024), np.float32)
    sgb[:, 0] = f(inputs["sgu_norm_g"])[:, None, :]
    sgb[:, 1] = f(inputs["sgu_norm_b"])[:, None, :]
    sh["sgb"] = sgb
    sh["sgwT"] = np.ascontiguousarray(f(inputs["sgu_w"]).transpose(0, 3, 1, 2))
    return sh


def kernel(**inputs):
    x = np.asarray(inputs["x"], np.float32)
    sh = _prep_shared(inputs)
    if "nc" not in _NC_CACHE:
        _NC_CACHE["nc"] = build_program()
    nc = _NC_CACHE["nc"]
    in_maps = []
    for b in range(NCORES):
        m = dict(sh)
        m["x_fm"] = np.ascontiguousarray(x[b].T.reshape(16, 128, S))
        in_maps.append(m)
    res = run_bass_kernel_spmd(nc, in_maps, core_ids=list(range(NCORES)))
    out = np.empty((NCORES, S, D), np.float32)
    for b in range(NCORES):
        out[b] = res.results[b]["out_fm"].reshape(D, S).T
    return out
```

```python
import math
from contextlib import ExitStack
import numpy as np
import concourse.bass as bass
import concourse.mybir as mybir
from concourse.bass_utils import run_bass_kernel_spmd

F32 = mybir.dt.float32
BF16 = mybir.dt.bfloat16
ALU = mybir.AluOpType
AF = mybir.ActivationFunctionType
AX = mybir.AxisListType

D = 2048
S = 2048
L = 2
DFF = 5504
NFC = 43
INW = 8752
OFF_KV, OFF_NG, OFF_UV, OFF_MG = 1024, 2560, 2608, 4656
EPS = 1e-6
NEGB = -30000.0
NCORES = 4

ENGS = ["sync", "gpsimd", "scalar", "vector", "tensor"]
NDSEM = 24


class Trk:
    __slots__ = ("w", "r", "x")

    def __init__(self, x=False):
        self.w = None
        self.r = {}
        self.x = x


class V:
    __slots__ = ("ap", "trks")

    def __init__(self, ap, trks):
        self.ap = ap
        self.trks = trks


class Buf:
    def __init__(self, h, excl=False):
        self.h = h
        self.t = {}
        self.excl = excl

    def __call__(self, ap, *keys):
        if not keys:
            keys = (0,)
        out = []
        for k in keys:
            t = self.t.get(k)
            if t is None:
                t = self.t[k] = Trk(self.excl)
            out.append(t)
        return V(ap, out)


class Prog:
    def __init__(self, nc):
        self.nc = nc
        self.ops = {e: [] for e in ENGS}
        self.seen = {e: {} for e in ENGS}
        self.seen_dma = {e: set() for e in ENGS}
        self.ndma = {e: 0 for e in ENGS}

    def _dep(self, eng, waits, d):
        if d[0] == "dma":
            if d in self.seen_dma[eng]:
                return
            self.seen_dma[eng].add(d)
            waits.append(d)
        else:
            e, i = d
            if e == eng and eng in ("tensor", "sync"):
                return
            if self.seen[eng].get(e, -1) >= i:
                return
            self.seen[eng][e] = i
            self.ops[e][i]["inc"] = True
            waits.append(d)

    def op(self, eng, fn, reads=(), writes=(), dma=False):
        idx = len(self.ops[eng])
        waits = []
        deps = []
        rd, wr = [], []
        for v in reads:
            for t in v.trks:
                (wr if t.x else rd).append(t)
        for v in writes:
            wr.extend(v.trks)
        for t in rd:
            if t.w is not None:
                deps.append(t.w)
        for t in wr:
            if t.w is not None:
                deps.append(t.w)
            deps.extend(t.r.values())
        if dma:
            n = self.ndma[eng]
            self.ndma[eng] += 1
            ev = ("dma", eng, n)
            if n >= NDSEM:
                deps.append(("dma", eng, n - NDSEM))
        else:
            ev = (eng, idx)
        for d in deps:
            self._dep(eng, waits, d)
        self.ops[eng].append({"fn": fn, "waits": waits, "inc": False, "dma": ev if dma else None})
        for t in rd:
            t.r[ev if dma else eng] = ev
        for t in wr:
            t.w = ev
            t.r = {}
        return ev

    def barrier(self):
        evs = []
        for e in ENGS:
            if self.ops[e]:
                last = len(self.ops[e]) - 1
                while last >= 0 and self.ops[e][last]["fn"] is None:
                    last -= 1
                if last >= 0 and self.ops[e][last]["dma"] is None:
                    evs.append((e, last))
            n = self.ndma[e]
            for k in range(max(0, n - NDSEM), n):
                evs.append(("dma", e, k))
        for e in ENGS:
            waits = []
            for d in evs:
                if d[0] != "dma" and d[0] == e:
                    continue
                self._dep(e, waits, d)
            self.ops[e].append({"fn": None, "waits": waits, "inc": False, "dma": None})

    def emit(self, stack):
        nc = self.nc
        esem = {e: stack.enter_context(nc.semaphore("es_" + e)) for e in ENGS}
        dsem = {e: [stack.enter_context(nc.semaphore("ds_%s_%d" % (e, i))) for i in range(NDSEM)]
                for e in ENGS if self.ndma[e] > 0}
        for e in ENGS:
            c = 0
            for o in self.ops[e]:
                if o["inc"]:
                    c += 1
                o["cnt"] = c
        block = stack.enter_context(nc.Block())
        ops = self.ops

        def run(engobj, e):
            for o in ops[e]:
                for d in o["waits"]:
                    if d[0] == "dma":
                        _, q, n = d
                        engobj.wait_ge(dsem[q][n % NDSEM], 16 * (n // NDSEM + 1))
                    else:
                        engobj.wait_ge(esem[d[0]], ops[d[0]][d[1]]["cnt"])
                if o["fn"] is None:
                    continue
                ins = o["fn"](engobj)
                if o["dma"] is not None:
                    _, q, n = o["dma"]
                    ins.then_inc(dsem[q][n % NDSEM], 16)
                elif o["inc"]:
                    ins.then_inc(esem[e], 1)

        @block.sync
        def _(x):
            run(x, "sync")

        @block.gpsimd
        def _(x):
            run(x, "gpsimd")

        @block.scalar
        def _(x):
            run(x, "scalar")

        @block.vector
        def _(x):
            run(x, "vector")

        @block.tensor
        def _(x):
            run(x, "tensor")

    def dma(self, q, out, in_):
        return self.op(q, lambda e: e.dma_start(out=out.ap, in_=in_.ap), [in_], [out], dma=True)

    def mm(self, out, lhsT, rhs, start=True, stop=True):
        return self.op("tensor", lambda e: e.matmul(out.ap, lhsT.ap, rhs.ap, start=start, stop=stop),
                       [lhsT, rhs], [out])

    def transpose(self, out, in_, ident):
        return self.op("tensor", lambda e: e.transpose(out.ap, in_.ap, ident.ap), [in_, ident], [out])

    def act(self, out, in_, func, bias=None, scale=None):
        reads = [in_]
        kw = {}
        if bias is not None:
            reads.append(bias)
            kw["bias"] = bias.ap
        if scale is not None:
            kw["scale"] = scale
        return self.op("scalar", lambda e: e.activation(out.ap, in_.ap, func, **kw), reads, [out])

    def tt(self, eng, out, in0, in1, op):
        return self.op(eng, lambda e: e.tensor_tensor(out.ap, in0.ap, in1.ap, op), [in0, in1], [out])

    def ts(self, eng, out, in0, s1, s2, op0, op1=None):
        reads = [in0]
        a1, a2 = s1, s2
        if isinstance(s1, V):
            reads.append(s1)
            a1 = s1.ap
        if isinstance(s2, V):
            reads.append(s2)
            a2 = s2.ap
        kw = {}
        if op1 is not None:
            kw["op1"] = op1
        return self.op(eng, lambda e: e.tensor_scalar(out.ap, in0.ap, a1, a2, op0, **kw), reads, [out])

    def stt(self, eng, out, in0, scalar, in1, op0, op1):
        reads = [in0, in1]
        a = scalar
        if isinstance(scalar, V):
            reads.append(scalar)
            a = scalar.ap
        return self.op(eng, lambda e: e.scalar_tensor_tensor(out.ap, in0.ap, a, in1.ap, op0, op1), reads, [out])

    def copy(self, eng, out, in_):
        if eng == "scalar":
            return self.op(eng, lambda e: e.copy(out.ap, in_.ap), [in_], [out])
        return self.op(eng, lambda e: e.tensor_copy(out.ap, in_.ap), [in_], [out])

    def memset(self, eng, out, val):
        return self.op(eng, lambda e: e.memset(out.ap, val), [], [out])

    def recip(self, out, in_):
        return self.op("vector", lambda e: e.reciprocal(out.ap, in_.ap), [in_], [out])


def _t5_bucket(dist):
    n = np.maximum(dist, 0)
    nf = np.maximum(n, 1).astype(np.float32)
    large = 16 + (np.log(nf / np.float32(16)) / np.float32(math.log(128 / 16)) * np.float32(16)).astype(np.int32)
    large = np.minimum(large, 31)
    return np.where(n < 16, n, large)


CST = {}


def _build_consts():
    cols = []
    off = [0]

    def add(name, arr):
        arr = np.asarray(arr, np.float32)
        a = np.zeros((128, arr.shape[1]), np.float32)
        a[:arr.shape[0]] = arr
        CST[name] = (off[0], arr.shape[1])
        off[0] += arr.shape[1]
        cols.append(a)

    idx = np.arange(4096)
    dist = idx - 2048
    oh = np.zeros((33, 4096), np.float32)
    bk = _t5_bucket(dist)
    for i in range(4096):
        if dist[i] < 0:
            oh[32, i] = 1
        else:
            oh[bk[i], i] = 1
    CST["_oh"] = oh
    add("ident", np.eye(128))
    add("j128", np.eye(128)[::-1])
    j127 = np.zeros((128, 128))
    j127[:127, :127] = np.eye(127)[::-1]
    add("j127", j127)
    s_ = np.arange(128)[:, None]
    t_ = np.arange(128)[None, :]
    add("triu", (s_ <= t_))
    expm = np.zeros((32, 16, 128))
    for jc in range(16):
        for k in range(128):
            expm[2 * jc + k // 64, jc, k] = 1
    CST["_expm"] = expm.reshape(32, -1).astype(np.float32)
    ci = np.arange(127)[:, None] * 16
    sj = np.arange(32)[None, :] * 64
    ov = np.clip(np.minimum(ci + 32, sj + 64) - np.maximum(ci, sj), 0, None).astype(np.float32) / 32
    add("ov", ov)
    notf = np.zeros((128, 16, 32))
    addv = np.zeros((128, 16, 32))
    vneg = np.zeros((128, 16, 32))
    for i in range(16):
        for p in range(128):
            cur = (128 * i + p) // 64
            for j in range(32):
                if j > cur:
                    addv[p, i, j] = -1e30
                    vneg[p, i, j] = NEGB
                elif j == 0:
                    addv[p, i, j] = 1e4
                elif j == cur:
                    addv[p, i, j] = 2e4
                elif j == cur - 1:
                    addv[p, i, j] = 3e4
                else:
                    notf[p, i, j] = 1
    add("notf", notf.reshape(128, -1))
    add("addv", addv.reshape(128, -1))
    add("vneg", vneg.reshape(128, -1))
    m6 = (t_ < s_).astype(np.float32)
    add("m6", m6)
    add("n6", (m6 - 1) * 30000.0)
    return np.concatenate(cols, axis=1)


CST_ARR = _build_consts()
NCST = CST_ARR.shape[1]
PRM = {"n1": 0, "nm": 16, "n2": 32, "qn": 48, "kn": 49, "pk": 52, "pv": 84}
NPRM = 116
OH_ARR = CST["_oh"]
EXPM_ARR = CST["_expm"]


def build_program(depth=L, dbg=None, phases="1m2", mstop=None, ntiles=16, tstop=8):
    nc = bass.Bass("TRN2", target_bir_lowering=False)
    dr = {}

    def din(name, shape, dt=F32):
        dr[name] = Buf(nc.dram_tensor(name, list(shape), dt, kind="ExternalInput"))
        return dr[name]

    x_in = din("x_fm", [16, 128, S])
    cst_d = din("cst", [128, NCST])
    prm_d = din("prm", [L, 128, NPRM])
    rb_d = din("rbp", [32, 16])
    oh_d = din("oh", [33, 4096])
    expm_d = din("expm", [32, 2048])
    sgub_d = din("sgub", [L, 1, 1024])
    sgb_d = din("sgb", [L, 2, 128, 1024])
    sgw_d = din("sgwT", [L, 128, 8, 128])
    wnames = {"ffn1_w_gate": [L, D, DFF], "ffn1_w_up": [L, D, DFF], "ffn1_w_down": [L, DFF, D],
              "ffn2_w_gate": [L, D, DFF], "ffn2_w_up": [L, D, DFF], "ffn2_w_down": [L, DFF, D],
              "w_in": [L, D, INW], "cmp_k_w1": [L, 2048, 256], "cmp_k_w2": [L, 256, 64],
              "cmp_v_w1": [L, 2048, 256], "cmp_v_w2": [L, 256, 64],
              "w_proj_nsa": [L, 1024, D], "w_proj_sgu": [L, 1024, D], "w_out": [L, D, D]}
    for k, shp in wnames.items():
        din(k, shp)
    out_d = Buf(nc.dram_tensor("out_fm", [16, 128, S], F32, kind="ExternalOutput"))
    xs_d = Buf(nc.dram_tensor("xs", [16, 128, S], F32))
    fd_d = Buf(nc.dram_tensor("fdt", [16, 4096], BF16))
    wc16_d = Buf(nc.dram_tensor("wc16", [76, 128, 16, 128], BF16))
    wc8_d = Buf(nc.dram_tensor("wc8", [32, 128, 8, 128], BF16))
    wcv_d = Buf(nc.dram_tensor("wcv", [128, 16, 560], BF16))
    wcv2_d = Buf(nc.dram_tensor("wcv2", [2, 128, 16, 512], BF16))
    dbg_d = {}
    if dbg:
        for name, shp in dbg.items():
            dbg_d[name] = Buf(nc.dram_tensor("dbg_" + name, list(shp), F32, kind="ExternalOutput"))

    P = Prog(nc)
    _NC_CACHE['P'] = P
    out_events = []

    with ExitStack() as top:
        top.enter_context(nc.allow_low_precision("bf16 matmul operands, fp32 accumulation"))

        uniq = [0]

        def sbt(st, name, shape, dt):
            uniq[0] += 1
            return Buf(st.enter_context(nc.sbuf_tensor("%s_%d" % (name, uniq[0]), list(shape), dt)))

        banks = [Buf(top.enter_context(nc.psum_tensor("pb%d" % i, [128, 512], F32)), excl=True) for i in range(8)]
        bank_i = [0]
        bank_n = [8]

        def nb():
            b = banks[bank_i[0] % bank_n[0]]
            bank_i[0] += 1
            return b

        cst = sbt(top, "cst_sb", [128, NCST], F32)
        P.dma("sync", cst(cst.h[:, :]), cst_d(cst_d.h.ap()))

        def cv(name, rows=128, lo=0, n=None):
            o, w = CST[name]
            if n is None:
                n = w - lo
            return cst(cst.h[0:rows, o + lo:o + lo + n])

        prm = sbt(top, "prm_sb", [128, L, NPRM], F32)
        P.dma("sync", prm(prm.h[:, :, :]), prm_d(prm_d.h.ap().rearrange("l p n -> p l n")))
        ones_bf = sbt(top, "ones_bf", [128, 128], BF16)
        P.memset("vector", ones_bf(ones_bf.h[:, :]), 1.0)
        blk_bf = sbt(top, "blk_bf", [128, 128], BF16)
        P.memset("vector", blk_bf(blk_bf.h[:, :]), 0.0)
        P.memset("vector", blk_bf(blk_bf.h[0:64, 0:64]), 1.0)
        P.memset("vector", blk_bf(blk_bf.h[64:128, 64:128]), 1.0)
        ident_bf = sbt(top, "ident_bf", [128, 128], BF16)
        P.copy("vector", ident_bf(ident_bf.h[:, :]), cv("ident"))
        epsc = sbt(top, "epsc", [128, 2], F32)
        P.memset("vector", epsc(epsc.h[:, 0:1]), EPS)
        P.memset("vector", epsc(epsc.h[:, 1:2]), 0.0)
        gsc = sbt(top, "gsc", [128, L, 52], F32)
        for l in range(L):
            P.copy("vector", gsc(gsc.h[:, l, 0:49]), prm(prm.h[:, l, 0:49]))
            P.ts("vector", gsc(gsc.h[:, l, 49:50]), prm(prm.h[:, l, 48:49]), 0.125, None, ALU.mult)

        def pcol(l, name, c=0):
            return gsc(gsc.h[:, l, PRM[name] + c:PRM[name] + c + 1])

        def make_h(st_x, st_h, xsrc, l, nname, tok0, hcol0, sq, sd):
            xt, xo = st_x
            ht, ho = st_h
            P.dma("sync", xt(xt.h[:, :, xo:xo + 512], ("x", xo)),
                  xsrc(xsrc.h.ap()[:, :, tok0:tok0 + 512].rearrange("c p t -> p c t"), tok0 // 512))
            psn = nb()
            for c in range(16):
                s = sq[c % 2]
                P.act(s(s.h[:, :]), xt(xt.h[:, c, xo:xo + 512], ("x", xo)), AF.Square)
                P.mm(psn(psn.h[:, :]), ones_bf(ones_bf.h[:, :]), s(s.h[:, :]), start=(c == 0), stop=(c == 15))
            P.act(sd(sd.h[:, :]), psn(psn.h[:, :]), AF.Sqrt, bias=epsc(epsc.h[:, 0:1]), scale=1.0 / D)
            P.recip(sd(sd.h[:, :]), sd(sd.h[:, :]))
            for c in range(16):
                P.stt("vector", ht(ht.h[:, c, ho:ho + 512], ("h", ho)), xt(xt.h[:, c, xo:xo + 512], ("x", xo)),
                      pcol(l, nname, c), sd(sd.h[:, :]), ALU.mult, ALU.mult)

        def wfetch(tb, wt_view, cbuf, cap, ckey, loaders):
            if tb == 0:
                for d_, s_ in loaders:
                    P.dma("gpsimd", d_, s_)
                P.dma("sync", cbuf(cap, ckey), wt_view)
            else:
                P.dma("sync", wt_view, cbuf(cap, ckey))

        def wload(wt, src_ap):
            P.dma("gpsimd", wt(wt.h[:, :, :]) if len(wt.h.shape) == 3 else wt(wt.h[:, :]), src_ap)

        def ffn(l, which, xsrc, xdst):
            wg_d, wu_d, wd_d = dr[which + "_w_gate"], dr[which + "_w_up"], dr[which + "_w_down"]
            nname = "n1" if which == "ffn1" else "n2"
            with ExitStack() as st:
                xt = sbt(st, "f_x", [128, 16, 1024], F32)
                ht = sbt(st, "f_h", [128, 16, 1024], BF16)
                act = sbt(st, "f_act", [128, 22, 1024], BF16)
                sq = [sbt(st, "f_sq%d" % i, [128, 512], BF16) for i in range(2)]
                sd = sbt(st, "f_sd", [128, 512], F32)
                wg = [sbt(st, "f_wg%d" % i, [128, 16, 128], BF16) for i in range(3)]
                wu = [sbt(st, "f_wu%d" % i, [128, 16, 128], BF16) for i in range(3)]
                wd = [sbt(st, "f_wd%d" % i, [128, 22, 256], BF16) for i in range(2)]
                sl = [sbt(st, "f_sl%d" % i, [128, 512], BF16) for i in range(2)]
                wi = 0
                di = 0
                for tt in range(2):
                    for tb in range(2):
                        make_h((xt, tb * 512), (ht, tb * 512), xsrc, l, nname, tt * 1024 + tb * 512, 0, sq, sd)
                    for fh in range(2):
                        fcs = list(range(0, 22)) if fh == 0 else list(range(22, 43))
                        for fi, fc in enumerate(fcs):
                            g, u = wg[wi % 3], wu[wi % 3]
                            wi += 1
                            P.dma("gpsimd", g(g.h[:, :, :]),
                                  wg_d(wg_d.h.ap()[l, :, fc * 128:(fc + 1) * 128].rearrange("(c p) f -> p c f", p=128)))
                            P.dma("gpsimd", u(u.h[:, :, :]),
                                  wu_d(wu_d.h.ap()[l, :, fc * 128:(fc + 1) * 128].rearrange("(c p) f -> p c f", p=128)))
                            for tb in range(2):
                                pg, pu = nb(), nb()
                                for c in range(16):
                                    P.mm(pg(pg.h[:, :]), g(g.h[:, c, :]), ht(ht.h[:, c, tb * 512:(tb + 1) * 512], ("h", tb * 512)),
                                         start=(c == 0), stop=(c == 15))
                                for c in range(16):
                                    P.mm(pu(pu.h[:, :]), u(u.h[:, c, :]), ht(ht.h[:, c, tb * 512:(tb + 1) * 512], ("h", tb * 512)),
                                         start=(c == 0), stop=(c == 15))
                                s = sl[(fi * 2 + tb) % 2]
                                P.act(s(s.h[:, :]), pg(pg.h[:, :]), AF.Silu)
                                P.tt("vector", act(act.h[:, fi, tb * 512:(tb + 1) * 512], (fi, tb)), s(s.h[:, :]), pu(pu.h[:, :]), ALU.mult)
                        nf = len(fcs)
                        for dcp in range(8):
                            w = wd[di % 2]
                            di += 1
                            P.dma("gpsimd", w(w.h[:, 0:nf, :]),
                                  wd_d(wd_d.h.ap()[l, fcs[0] * 128:(fcs[-1] + 1) * 128, dcp * 256:(dcp + 1) * 256]
                                       .rearrange("(f p) d -> p f d", p=128)))
                            for ds in range(2):
                                dc = dcp * 2 + ds
                                for tb in range(2):
                                    pd = nb()
                                    for fi in range(nf):
                                        P.mm(pd(pd.h[:, :]), w(w.h[:, fi, ds * 128:(ds + 1) * 128]),
                                             act(act.h[:, fi, tb * 512:(tb + 1) * 512], (fi, tb)), start=(fi == 0), stop=(fi == nf - 1))
                                    xv = xt(xt.h[:, dc, tb * 512:(tb + 1) * 512], ("x", tb * 512))
                                    P.stt("vector", xv, pd(pd.h[:, :]), 0.5, xv, ALU.mult, ALU.add)
                    ev = P.dma("sync", xdst(xdst.h.ap()[:, :, tt * 1024:(tt + 1) * 1024].rearrange("c p t -> p c t"), 2 * tt, 2 * tt + 1),
                               xt(xt.h[:, :, :], ("x", 0), ("x", 512)))
                    if xdst is out_d:
                        out_events.append(ev)
            P.barrier()

        def build_tables(st):
            dall = sbt(st, "dall", [128, 7, 16, 128], BF16)
            with ExitStack() as s2:
                rbx = sbt(s2, "rbx", [33, 16], F32)
                rbb = sbt(s2, "rbb", [33, 16], BF16)
                ohb = sbt(s2, "ohb", [33, 4096], BF16)
                fsb = sbt(s2, "fsb", [16, 4096], BF16)
                hk = sbt(s2, "hk", [128, 16, 128], BF16)
                P.memset("vector", rbx(rbx.h[:, :]), NEGB)
                P.dma("sync", rbx(rbx.h[0:32, :]), rb_d(rb_d.h.ap()))
                P.copy("vector", rbb(rbb.h[:, :]), rbx(rbx.h[:, :]))
                P.dma("gpsimd", ohb(ohb.h[:, :]), oh_d(oh_d.h.ap()))
                for n in range(8):
                    pb = nb()
                    P.mm(pb(pb.h[0:16, :]), rbb(rbb.h[:, :]), ohb(ohb.h[:, n * 512:(n + 1) * 512]))
                    P.copy("vector", fsb(fsb.h[:, n * 512:(n + 1) * 512]), pb(pb.h[0:16, :]))
                P.dma("sync", fd_d(fd_d.h.ap()), fsb(fsb.h[:, :]))
                j128 = sbt(s2, "j128b", [128, 128], BF16)
                P.copy("vector", j128(j128.h[:, :]), cv("j128"))
                for dl in range(6):
                    src = bass.AP(fd_d.h, 2048 + dl * 128 - 127, [[1, 128], [4096, 16], [1, 128]])
                    P.dma("sync", hk(hk.h[:, :, :]), fd_d(src))
                    for n in range(4):
                        pb = nb()
                        P.mm(pb(pb.h[:, :]), j128(j128.h[:, :]), hk(hk.h[:, n * 4:(n + 1) * 4, :]))
                        P.copy("vector", dall(dall.h[:, dl, n * 4:(n + 1) * 4, :]), pb(pb.h[:, :]))
                for h in range(16):
                    P.tt("vector", dall(dall.h[:, 6, h, :]), dall(dall.h[:, 5, h, :]), cv("m6"), ALU.mult)
                    P.tt("vector", dall(dall.h[:, 6, h, :]), dall(dall.h[:, 6, h, :]), cv("n6"), ALU.add)
            P.barrier()
            return dall

        def mixer(l, xsrc, xdst):
            w_in = dr["w_in"]

            def wcols(c0, n):
                return w_in(w_in.h.ap()[l, :, c0:c0 + n].rearrange("(c p) f -> p c f", p=128))

            with ExitStack() as sq_:
                qT = sbt(sq_, "m_qT", [128, 8, S], BF16)
                with ExitStack() as st:
                    ksd = sbt(st, "m_ksd", [128, 4, S], BF16)
                    kwd = sbt(st, "m_kwd", [128, 4, S], BF16)
                    kcT = sbt(st, "m_kcT", [128, 2, S], BF16)
                    vcT = sbt(st, "m_vcT", [128, 2, S], BF16)
                    vsa = sbt(st, "m_vsa", [128, 16, 4, 65], BF16)
                    vwa = sbt(st, "m_vwa", [128, 16, 4, 65], BF16)
                    gat = sbt(st, "m_gat", [128, 16, 48], F32)
                    P.memset("vector", vsa(vsa.h[:, :, :, 64:65]), 1.0)
                    P.memset("vector", vwa(vwa.h[:, :, :, 64:65]), 1.0)
                    with ExitStack() as sa:
                        xt = sbt(sa, "a_x", [128, 16, 512], F32)
                        ht = sbt(sa, "a_h", [128, 16, 512], BF16)
                        sq = [sbt(sa, "a_sq%d" % i, [128, 512], BF16) for i in range(2)]
                        sd = sbt(sa, "a_sd", [128, 512], F32)
                        rs = sbt(sa, "a_rs", [128, 512], F32)
                        wq = [sbt(sa, "a_wq%d" % i, [128, 16, 128], BF16) for i in range(3)]
                        wv = sbt(sa, "a_wv", [128, 16, 560], BF16)
                        wi = 0
                        for tb in range(4):
                            t0 = tb * 512
                            make_h((xt, 0), (ht, 0), xsrc, l, "nm", t0, 0, sq, sd)
                            hv = lambda c: ht(ht.h[:, c, :], ("h", 0))
                            jobs = [("q", c) for c in range(8)] + [("ks", g) for g in range(4)] + \
                                   [("kw", g) for g in range(4)] + [("kc", c) for c in range(2)] + [("vc", c) for c in range(2)]
                            for jn, (kind, ix) in enumerate(jobs):
                                w = wq[wi % 3]
                                wi += 1
                                wall = w(w.h[:, :, :])
                                if kind == "q":
                                    lds = [(wall, wcols(ix * 128, 128))]
                                elif kind in ("ks", "kw"):
                                    c0 = OFF_KV + (2 if kind == "ks" else 4) * 256 + ix * 64
                                    lds = [(w(w.h[:, :, 0:64]), wcols(c0, 64)), (w(w.h[:, :, 64:128]), wcols(c0, 64))]
                                else:
                                    c0 = OFF_KV + (0 if kind == "kc" else 1) * 256 + ix * 128
                                    lds = [(wall, wcols(c0, 128))]
                                wfetch(tb, wall, wc16_d, wc16_d.h.ap()[jn], jn, lds)
                                wvw = wall
                                pz = nb()
                                for c in range(16):
                                    P.mm(pz(pz.h[:, :]), V(w.h[:, c, :], wvw.trks), hv(c), start=(c == 0), stop=(c == 15))
                                if kind in ("kc", "vc"):
                                    dst = kcT if kind == "kc" else vcT
                                    P.copy("scalar", dst(dst.h[:, ix, t0:t0 + 512], tb), pz(pz.h[:, :]))
                                    continue
                                s = sq[wi % 2]
                                P.act(s(s.h[:, :]), pz(pz.h[:, :]), AF.Square)
                                pm = nb()
                                P.mm(pm(pm.h[:, :]), blk_bf(blk_bf.h[:, :]), s(s.h[:, :]))
                                P.act(rs(rs.h[:, :]), pm(pm.h[:, :]), AF.Sqrt, bias=epsc(epsc.h[:, 0:1]), scale=1.0 / 64)
                                P.recip(rs(rs.h[:, :]), rs(rs.h[:, :]))
                                if kind == "q":
                                    dstv = qT(qT.h[:, ix, t0:t0 + 512], tb)
                                    gc = gsc(gsc.h[:, l, 49:50])
                                    P.stt("vector", dstv, pz(pz.h[:, :]), gc, rs(rs.h[:, :]), ALU.mult, ALU.mult)
                                else:
                                    dst = ksd if kind == "ks" else kwd
                                    kn = 1 if kind == "ks" else 2
                                    gc = prm(prm.h[:, l, PRM["kn"] + kn:PRM["kn"] + kn + 1])
                                    P.stt("vector", dst(dst.h[:, ix, t0:t0 + 512], tb), pz(pz.h[:, :]), gc, rs(rs.h[:, :]), ALU.mult, ALU.mult)
                            wfetch(tb, wv(wv.h[:, :, :]), wcv_d, wcv_d.h.ap(), 0,
                                   [(wv(wv.h[:, :, 0:256]), wcols(OFF_KV + 3 * 256, 256)),
                                    (wv(wv.h[:, :, 256:512]), wcols(OFF_KV + 5 * 256, 256)),
                                    (wv(wv.h[:, :, 512:560]), wcols(OFF_NG, 48))])
                            wvt = wv(wv.h[:, :, :]).trks
                            for sub in range(4):
                                ti = tb * 4 + sub
                                p1, p2 = nb(), nb()
                                for c in range(16):
                                    P.mm(p1(p1.h[:, :]), ht(ht.h[:, c, sub * 128:(sub + 1) * 128], ("h", 0)), V(wv.h[:, c, 0:512], wvt),
                                         start=(c == 0), stop=(c == 15))
                                for c in range(16):
                                    P.mm(p2(p2.h[:, 0:48]), ht(ht.h[:, c, sub * 128:(sub + 1) * 128], ("h", 0)), V(wv.h[:, c, 512:560], wvt),
                                         start=(c == 0), stop=(c == 15))
                                P.copy("vector", vsa(vsa.h[:, ti, :, 0:64], ti), p1(p1.h[:, 0:256].rearrange("p (g d) -> p g d", g=4)))
                                P.copy("vector", vwa(vwa.h[:, ti, :, 0:64], ti), p1(p1.h[:, 256:512].rearrange("p (g d) -> p g d", g=4)))
                                P.act(gat(gat.h[:, ti, :], ti), p2(p2.h[:, 0:48]), AF.Sigmoid)
                    P.barrier()
                    if mstop != "A":
                        dall = build_tables(st)
                    if mstop in ("A", "T"):
                        with ExitStack() as sx:
                            xc = sbt(sx, "passx", [128, 16, 512], F32)
                            for tb in range(4):
                                P.dma("sync", xc(xc.h[:, :, :]), xsrc(xsrc.h.ap()[:, :, tb * 512:(tb + 1) * 512].rearrange("c p t -> p c t"), tb))
                                P.dma("sync", xdst(xdst.h.ap()[:, :, tb * 512:(tb + 1) * 512].rearrange("c p t -> p c t"), tb), xc(xc.h[:, :, :]))
                            P.barrier()
                        return
                    attention(l, st, qT, ksd, kwd, kcT, vcT, vsa, vwa, gat, dall)
                    bank_n[0] = 8
                P.barrier()
                if mstop == "X":
                    with ExitStack() as sx:
                        xc = sbt(sx, "passx2", [128, 16, 512], F32)
                        for tb in range(4):
                            P.dma("sync", xc(xc.h[:, :, :]), xsrc(xsrc.h.ap()[:, :, tb * 512:(tb + 1) * 512].rearrange("c p t -> p c t"), tb))
                            P.dma("sync", xdst(xdst.h.ap()[:, :, tb * 512:(tb + 1) * 512].rearrange("c p t -> p c t"), tb), xc(xc.h[:, :, :]))
                        P.barrier()
                    return
                stage_d(l, qT, xsrc, xdst, wcols)
            P.barrier()

        def dump(name, view_fn_list):
            pass

        def dump_attn_inputs(qT, ksd, kwd, kcT, vcT, vsa, vwa, gat, dall):
            with ExitStack() as sd_:
                tmp = sbt(sd_, "dbg_tmp", [128, 8 * 512], F32)
                if "qT" in dbg_d:
                    o = dbg_d["qT"]
                    P.copy("vector", tmp(tmp.h[:, :].rearrange("p (c t) -> p c t", c=8)), qT(qT.h[:, :, 0:512], 0))
                    P.dma("sync", o(o.h.ap()), tmp(tmp.h[:, :]))
                if "ksd" in dbg_d:
                    o = dbg_d["ksd"]
                    t2 = sbt(sd_, "dbg_t2", [128, 4 * 512], F32)
                    P.copy("vector", t2(t2.h[:, :].rearrange("p (c t) -> p c t", c=4)), ksd(ksd.h[:, :, 0:512], 0))
                    P.dma("sync", o(o.h.ap()), t2(t2.h[:, :]))
                if "gat" in dbg_d:
                    o = dbg_d["gat"]
                    P.dma("sync", o(o.h.ap()), gat(gat.h[:, :, :].rearrange("p a b -> p (a b)") if False else gat.h[:, 0, :], 0))
                if "dall" in dbg_d:
                    o = dbg_d["dall"]
                    t3 = sbt(sd_, "dbg_t3", [128, 7 * 128], F32)
                    P.copy("vector", t3(t3.h[:, :].rearrange("p (c t) -> p c t", c=7)), dall(dall.h[:, :, 5, :]))
                    P.dma("sync", o(o.h.ap()), t3(t3.h[:, :]))

        def attention(l, st, qT, ksd, kwd, kcT, vcT, vsa, vwa, gat, dall):
            with ExitStack() as sc:
                kcn = sbt(sc, "c_kcn", [128, 4, 128], BF16)
                vcx = sbt(sc, "c_vcx", [128, 4, 97], BF16)
                P.memset("vector", vcx(vcx.h[:, :, 64:65]), 1.0)
                for g in range(4):
                    P.copy("vector", vcx(vcx.h[0:127, g, 65:97], "ov"), cv("ov", rows=127))
                expb = sbt(sc, "c_expb", [32, 16, 128], BF16)
                P.dma("gpsimd", expb(expb.h[:, :, :]), expm_d(expm_d.h.ap().rearrange("p (a b) -> p a b", a=16)))
                j127 = sbt(sc, "c_j127", [128, 128], BF16)
                P.copy("vector", j127(j127.h[:, :]), cv("j127"))
                with ExitStack() as s2:
                    w1 = sbt(s2, "c_w1", [128, 32, 256], BF16)
                    w2 = sbt(s2, "c_w2", [128, 2, 128], BF16)
                    posb = sbt(s2, "c_posb", [128, 32], BF16)
                    pbias = sbt(s2, "c_pbias", [128, 2], F32)
                    hid = sbt(s2, "c_hid", [128, 2, 128], BF16)
                    sqc = sbt(s2, "c_sq", [128, 128], BF16)
                    rsc = sbt(s2, "c_rs", [128, 128], F32)
                    for kind in ("k", "v"):
                        w1d = dr["cmp_%s_w1" % kind]
                        w2d = dr["cmp_%s_w2" % kind]
                        src1 = w1d.h.ap()[l].rearrange("(l d) f -> d l f", d=64)
                        P.dma("gpsimd", w1(w1.h[0:64, :, :], "a"), w1d(src1))
                        P.dma("gpsimd", w1(w1.h[64:128, :, :], "b"), w1d(src1))
                        src2 = w2d.h.ap()[l].rearrange("(fh p) d -> p fh d", p=128)
                        P.dma("gpsimd", w2(w2.h[:, :, 0:64], "a"), w2d(src2))
                        P.dma("gpsimd", w2(w2.h[:, :, 64:128], "b"), w2d(src2))
                        w1t = w1(w1.h[:, :, :], "a", "b").trks
                        w2t = w2(w2.h[:, :, :], "a", "b").trks
                        pn = "pk" if kind == "k" else "pv"
                        P.copy("vector", posb(posb.h[:, :]), prm(prm.h[:, l, PRM[pn]:PRM[pn] + 32]))
                        for fh in range(2):
                            pb = nb()
                            for ll in range(32):
                                P.mm(pb(pb.h[:, 0:1]), V(w1.h[0:64, ll, fh * 128:(fh + 1) * 128], w1t), posb(posb.h[0:64, ll:ll + 1]),
                                     start=(ll == 0), stop=(ll == 31))
                            P.copy("vector", pbias(pbias.h[:, fh:fh + 1]), pb(pb.h[:, 0:1]))
                        srcT = kcT if kind == "k" else vcT
                        for g in range(4):
                            hf, ch = g % 2, g // 2
                            for fh in range(2):
                                pb = nb()
                                for ll in range(32):
                                    r16 = srcT.h[hf * 64:(hf + 1) * 64, ch, :].rearrange("p (c s) -> p c s", s=16)
                                    rhs = r16[:, 0:127, ll] if ll < 16 else r16[:, 1:128, ll - 16]
                                    P.mm(pb(pb.h[:, 0:127]), V(w1.h[hf * 64:(hf + 1) * 64, ll, fh * 128:(fh + 1) * 128], w1t),
                                         srcT(rhs, 0, 1, 2, 3), start=(ll == 0), stop=(ll == 31))
                                P.act(hid(hid.h[:, fh, 0:127], fh), pb(pb.h[:, 0:127]), AF.Silu, bias=pbias(pbias.h[:, fh:fh + 1]))
                            if kind == "k":
                                pb = nb()
                                for fh in range(2):
                                    P.mm(pb(pb.h[:, 0:127]), V(w2.h[:, fh, :], w2t), hid(hid.h[:, fh, 0:127], fh), start=(fh == 0), stop=(fh == 1))
                                P.act(sqc(sqc.h[:, 0:127]), pb(pb.h[:, 0:127]), AF.Square)
                                pm = nb()
                                P.mm(pm(pm.h[:, 0:127]), blk_bf(blk_bf.h[:, :]), sqc(sqc.h[:, 0:127]))
                                P.act(rsc(rsc.h[:, 0:127]), pm(pm.h[:, 0:127]), AF.Sqrt, bias=epsc(epsc.h[:, 0:1]), scale=1.0 / 64)
                                P.recip(rsc(rsc.h[:, 0:127]), rsc(rsc.h[:, 0:127]))
                                gc = prm(prm.h[:, l, PRM["kn"]:PRM["kn"] + 1])
                                P.stt("vector", kcn(kcn.h[:, g, 0:127], g), pb(pb.h[:, 0:127]), gc, rsc(rsc.h[:, 0:127]), ALU.mult, ALU.mult)
                            else:
                                pb = nb()
                                for fh in range(2):
                                    P.mm(pb(pb.h[0:127, 0:64]), hid(hid.h[:, fh, 0:127], fh), V(w2.h[:, fh, 0:64], w2t), start=(fh == 0), stop=(fh == 1))
                                P.copy("vector", vcx(vcx.h[0:127, g, 0:64], ("v", g)), pb(pb.h[0:127, 0:64]))
                P.barrier()
                bct = [sbt(sc, "t_bch%d" % i, [128, 4, 128], BF16) for i in range(4)]
                bcf = [sbt(sc, "t_bcf%d" % i, [128, 4, 128], BF16) for i in range(4)]
                pT = [sbt(sc, "t_pT%d" % i, [128, 512], BF16) for i in range(5)]
                aaccs = [sbt(sc, "t_aacc%d" % k, [128, 16, 64], F32) for k in range(2)]
                abf = sbt(sc, "t_abf", [128, 1024], BF16)
                rden = sbt(sc, "t_rden", [128, 4], F32)
                coef = sbt(sc, "t_coef", [128, 4], F32)
                imp = sbt(sc, "t_imp", [128, 32], F32)
                scr = sbt(sc, "t_scr", [128, 32], F32)
                wrk = sbt(sc, "t_wrk", [128, 32], F32)
                m8 = sbt(sc, "t_m8", [128, 8], F32)
                selnss = [[sbt(sc, "t_seln%d_%d" % (k, g), [128, 32], F32) for g in range(4)] for k in range(2)]
                selTs = [sbt(sc, "t_selT%d" % g, [32, 2, 128], BF16) for g in range(4)]
                bank_n[0] = 6
                cnt = {"p": 0, "b": 0, "a": 0}
                accb = [banks[6], banks[7]]

                def nacc():
                    b = accb[cnt["a"] % 2]
                    cnt["a"] += 1
                    return b

                o1, o2, o3 = CST["notf"][0], CST["addv"][0], CST["vneg"][0]

                def hd(g, s):
                    return 4 * g + 2 * (s % 2) + s // 2

                def tile_fns(i):
                    q0 = i * 128
                    aacc = aaccs[i % 2]
                    selns = selnss[i % 2]

                    def score(g, klhs, rows, bias_fn, mask_j):
                        p = pT[cnt["p"] % 5]
                        cnt["p"] += 1
                        pss = (nb(), nb())
                        idv = ident_bf(ident_bf.h[0:rows, 0:rows])
                        for hf in range(2):
                            rhs = qT(qT.h[hf * 64:(hf + 1) * 64, 2 * g:2 * g + 2, q0:q0 + 128], i // 4)
                            P.mm(pss[hf](pss[hf].h[0:rows, 0:256]), klhs(hf), rhs, start=True, stop=False)
                        for hf in range(2):
                            P.mm(pss[hf](pss[hf].h[0:rows, 0:256]), idv, bias_fn(hf), start=False, stop=(mask_j is None))
                        if mask_j is not None:
                            sT = selTs[g]
                            for hf in range(2):
                                P.mm(pss[hf](pss[hf].h[0:rows, 0:256]), expb(expb.h[:, mask_j, :]), sT(sT.h[:, :, :]), start=False, stop=True)
                        for hf in range(2):
                            P.act(p(p.h[0:rows, hf * 256:(hf + 1) * 256], hf), pss[hf](pss[hf].h[0:rows, 0:256]), AF.Exp)
                        return p

                    def combine(g, br, pbk, first):
                        w = 97 if br == 0 else 65
                        den = V(pbk.h[:, 0:4 * w].rearrange("p (s c) -> p s c", s=4)[:, :, 64], pbk(pbk.h[:, :]).trks)
                        P.ts("vector", rden(rden.h[:, :]), den, 1e-30, None, ALU.max)
                        P.recip(rden(rden.h[:, :]), rden(rden.h[:, :]))
                        for s in range(4):
                            h = hd(g, s)
                            if br == 0:
                                if s == 0:
                                    P.ts("vector", imp(imp.h[:, :]), pbk(pbk.h[:, s * 97 + 65:s * 97 + 97]), rden(rden.h[:, s:s + 1]), None, ALU.mult)
                                else:
                                    P.stt("vector", imp(imp.h[:, :]), pbk(pbk.h[:, s * 97 + 65:s * 97 + 97]), rden(rden.h[:, s:s + 1]),
                                          imp(imp.h[:, :]), ALU.mult, ALU.add)
                            P.tt("vector", coef(coef.h[:, s:s + 1]), rden(rden.h[:, s:s + 1]), gat(gat.h[:, i, 3 * h + br:3 * h + br + 1], i), ALU.mult)
                            if first:
                                P.ts("vector", aacc(aacc.h[:, h, :], h), pbk(pbk.h[:, s * w:s * w + 64]), coef(coef.h[:, s:s + 1]), None, ALU.mult)
                            else:
                                P.stt("vector", aacc(aacc.h[:, h, :], h), pbk(pbk.h[:, s * w:s * w + 64]), coef(coef.h[:, s:s + 1]),
                                      aacc(aacc.h[:, h, :], h), ALU.mult, ALU.add)

                    def branch(g, steps, pbk, vbuf, mask):
                        LA = 2
                        ps_ = [None] * len(steps)
                        for n in range(min(LA, len(steps))):
                            ps_[n] = steps[n][1]()
                        for n, (j, _) in enumerate(steps):
                            if n + LA < len(steps):
                                ps_[n + LA] = steps[n + LA][1]()
                            p = ps_[n]
                            for s in range(4):
                                P.mm(pbk(pbk.h[:, s * 65:(s + 1) * 65]), p(p.h[:, s * 128:(s + 1) * 128], s // 2), vbuf(vbuf.h[:, j, g, :], j),
                                     start=(n == 0 and s == 0), stop=(n == len(steps) - 1 and s == 3))


                    def phase1():
                        for g in range(4):
                            bh, bf_ = bct[cnt["b"] % 4], bcf[cnt["b"] % 4]
                            cnt["b"] += 1
                            src = bass.AP(fd_d.h, 4 * g * 4096 + 2048 + q0 - 16 * 126 - 31, [[16, 128], [4096, 4], [1, 128]])
                            P.dma("sync", bh(bh.h[:, :, :]), fd_d(src))
                            pb = nb()
                            P.mm(pb(pb.h[0:127, :]), j127(j127.h[0:127, 0:127]), bh(bh.h[0:127, :, :]))
                            P.copy("scalar", bf_(bf_.h[0:127, :, :]), pb(pb.h[0:127, :]))
                            p = score(g, lambda hf: kcn(kcn.h[hf * 64:(hf + 1) * 64, g, 0:127], g), 127,
                                      lambda hf: bf_(bf_.h[0:127, 2 * hf:2 * hf + 2, :]), None)
                            po = nacc()
                            for s in range(4):
                                P.mm(po(po.h[:, s * 97:(s + 1) * 97]), p(p.h[0:127, s * 128:(s + 1) * 128], s // 2),
                                     vcx(vcx.h[0:127, g, :], "ov", ("v", g), 0), start=(s == 0), stop=(s == 3))
                            combine(g, 0, po, True)
                            P.tt("vector", scr(scr.h[:, :]), imp(imp.h[:, :]), cst(cst.h[:, o1 + i * 32:o1 + (i + 1) * 32]), ALU.mult)
                            P.tt("vector", scr(scr.h[:, :]), scr(scr.h[:, :]), cst(cst.h[:, o2 + i * 32:o2 + (i + 1) * 32]), ALU.add)
                            P.op("vector", lambda e: e.max(m8.h[:, :], scr.h[:, :]), [scr(scr.h[:, :])], [m8(m8.h[:, :])])
                            P.op("vector", lambda e: e.match_replace(wrk.h[:, :], m8.h[:, :], scr.h[:, :], -3e30),
                                 [scr(scr.h[:, :]), m8(m8.h[:, :])], [wrk(wrk.h[:, :])])
                            P.op("vector", lambda e: e.max(m8.h[:, :], wrk.h[:, :]), [wrk(wrk.h[:, :])], [m8(m8.h[:, :])])
                            P.op("vector", lambda e: e.match_replace(wrk.h[:, :], m8.h[:, :], wrk.h[:, :], -3e30),
                                 [wrk(wrk.h[:, :]), m8(m8.h[:, :])], [wrk(wrk.h[:, :])])
                            seln = selns[g]
                            P.tt("vector", seln(seln.h[:, :]), scr(scr.h[:, :]), wrk(wrk.h[:, :]), ALU.subtract)
                            P.ts("vector", seln(seln.h[:, :]), seln(seln.h[:, :]), 1.0, -1.0, ALU.min, ALU.add)
                            P.stt("vector", seln(seln.h[:, :]), seln(seln.h[:, :]), 30000.0, cst(cst.h[:, o3 + i * 32:o3 + (i + 1) * 32]),
                                  ALU.mult, ALU.add)

                    def phase23():
                        for g in range(4):
                            pt = nb()
                            P.transpose(pt(pt.h[0:32, 0:128]), selns[g](selns[g].h[:, :]), cv("ident"))
                            sT = selTs[g]
                            for s in range(2):
                                P.copy("vector", sT(sT.h[:, s, :]), pt(pt.h[0:32, 0:128]))
                        jl = max(0, i - 4)
                        segs = []
                        for g in range(4):
                            steps = []
                            for j in range(jl, i + 1):
                                dw = 6 if i - j == 4 else i - j
                                steps.append((j, (lambda j=j, dw=dw, g=g: score(
                                    g, lambda hf: kwd(kwd.h[hf * 64:(hf + 1) * 64, g, j * 128:(j + 1) * 128], j // 4), 128,
                                    lambda hf: dall(dall.h[:, dw, 4 * g + 2 * hf:4 * g + 2 * hf + 2, :]), None))))
                            segs.append((g, steps, vwa, 2))
                        for g in range(4):
                            steps = []
                            for j in range(i + 1):
                                dl = min(i - j, 5)
                                steps.append((j, (lambda j=j, dl=dl, g=g: score(
                                    g, lambda hf: ksd(ksd.h[hf * 64:(hf + 1) * 64, g, j * 128:(j + 1) * 128], j // 4), 128,
                                    lambda hf: dall(dall.h[:, dl, 4 * g + 2 * hf:4 * g + 2 * hf + 2, :]), j))))
                            segs.append((g, steps, vsa, 1))
                        flat = [(si, n, j, fn) for si, (g_, steps_, vb_, br_) in enumerate(segs) for n, (j, fn) in enumerate(steps_)]
                        LA = 2
                        ps_ = {}
                        accs = {}
                        for n in range(min(LA, len(flat))):
                            ps_[n] = flat[n][3]()
                        for n, (si, sn, j, fn) in enumerate(flat):
                            if n + LA < len(flat):
                                ps_[n + LA] = flat[n + LA][3]()
                            g_, steps_, vb_, br_ = segs[si]
                            if sn == 0:
                                accs[si] = nacc()
                            pbk = accs[si]
                            p = ps_.pop(n)
                            lastn = (sn == len(steps_) - 1)
                            for s in range(4):
                                P.mm(pbk(pbk.h[:, s * 65:(s + 1) * 65]), p(p.h[:, s * 128:(s + 1) * 128], s // 2), vb_(vb_.h[:, j, g_, :], j),
                                     start=(sn == 0 and s == 0), stop=(lastn and s == 3))
                            if lastn:
                                combine(g_, br_, pbk, False)
                        for c4 in range(2):
                            pt = nb()
                            for cc in range(4):
                                c = c4 * 4 + cc
                                P.transpose(pt(pt.h[:, cc * 128:(cc + 1) * 128]),
                                            aacc(aacc.h[:, 2 * c:2 * c + 2, :].rearrange("p h d -> p (h d)"), 2 * c, 2 * c + 1), cv("ident"))
                            P.copy("scalar", qT(qT.h[:, c4 * 4:(c4 + 1) * 4, q0:q0 + 128], ("a", i), i // 4),
                                   pt(pt.h[:, :].rearrange("p (c t) -> p c t", c=4)))

                    return phase1, phase23

                fns = [tile_fns(i) for i in range(ntiles)]
                if ntiles:
                    fns[0][0]()
                for i in range(ntiles):
                    if i + 1 < ntiles:
                        fns[i + 1][0]()
                    fns[i][1]()


        def stage_d(l, aT, xsrc, xdst, wcols):
            wpn, wps, wo = dr["w_proj_nsa"], dr["w_proj_sgu"], dr["w_out"]
            with ExitStack() as st:
                xt = sbt(st, "d_x", [128, 16, 512], F32)
                ht = sbt(st, "d_h", [128, 16, 512], BF16)
                sq = [sbt(st, "d_sq%d" % i, [128, 512], BF16) for i in range(2)]
                sd = sbt(st, "d_sd", [128, 512], F32)
                uT = sbt(st, "d_uT", [128, 8, 512], BF16)
                sgT = sbt(st, "d_sgT", [128, 8, 512], BF16)
                mrg = sbt(st, "d_mrg", [128, 16, 512], BF16)
                w16 = [sbt(st, "d_w16_%d" % i, [128, 16, 128], BF16) for i in range(4)]
                w8 = [sbt(st, "d_w8_%d" % i, [128, 8, 128], BF16) for i in range(4)]
                wv2 = sbt(st, "d_wv2", [128, 16, 512], BF16)
                vg = sbt(st, "d_vg", [128, 1024], F32)
                vsq = sbt(st, "d_vsq", [128, 1024], F32)
                vln = sbt(st, "d_vln", [128, 1024], BF16)
                stat = sbt(st, "d_stat", [128, 8], F32)
                lng = sbt(st, "d_lng", [128, 1024], F32)
                lnb = sbt(st, "d_lnb", [128, 1024], F32)
                wsT = sbt(st, "d_wsT", [128, 8, 128], BF16)
                wsf = sbt(st, "d_wsf", [128, 8, 128], F32)
                sbb = sbt(st, "d_sbb", [1, 1024], BF16)
                sg1 = sbt(st, "d_sg1", [128, 512], F32)
                sg2 = sbt(st, "d_sg2", [128, 512], F32)
                t1 = sbt(st, "d_t1", [128, 512], F32)
                P.dma("sync", lng(lng.h[:, :]), sgb_d(sgb_d.h.ap()[l, 0]))
                P.dma("sync", lnb(lnb.h[:, :]), sgb_d(sgb_d.h.ap()[l, 1]))
                P.dma("sync", wsf(wsf.h[:, :, :]), sgw_d(sgw_d.h.ap()[l]))
                for g in range(8):
                    P.tt("vector", wsT(wsT.h[:, g, :]), wsf(wsf.h[:, g, :]), cv("triu"), ALU.mult)
                P.dma("gpsimd", sbb(sbb.h[:, :]), sgub_d(sgub_d.h.ap()[l]))
                k16 = 0
                k8 = 0
                for tb in range(4):
                    t0 = tb * 512
                    make_h((xt, 0), (ht, 0), xsrc, l, "nm", t0, 0, sq, sd)
                    for c in range(8):
                        w = w16[k16 % 4]
                        k16 += 1
                        wfetch(tb, w(w.h[:, :, :]), wc16_d, wc16_d.h.ap()[20 + c], 20 + c, [(w(w.h[:, :, :]), wcols(OFF_UV + c * 128, 128))])
                        pz = nb()
                        for k in range(16):
                            P.mm(pz(pz.h[:, :]), w(w.h[:, k, :]), ht(ht.h[:, k, :], ("h", 0)), start=(k == 0), stop=(k == 15))
                        P.act(uT(uT.h[:, c, :], c), pz(pz.h[:, :]), AF.Gelu_apprx_tanh)
                    for sub in range(4):
                        for half in range(2):
                            wfetch(tb if sub == 0 else 1, wv2(wv2.h[:, :, :]), wcv2_d, wcv2_d.h.ap()[half], half,
                                   [(wv2(wv2.h[:, :, :]), wcols(OFF_UV + 1024 + half * 512, 512))])
                            pz = nb()
                            for k in range(16):
                                P.mm(pz(pz.h[:, :]), ht(ht.h[:, k, sub * 128:(sub + 1) * 128], ("h", 0)), wv2(wv2.h[:, k, :]),
                                     start=(k == 0), stop=(k == 15))
                            P.act(vg(vg.h[:, half * 512:(half + 1) * 512], half), pz(pz.h[:, :]), AF.Gelu_apprx_tanh)
                        vga = vg(vg.h[:, :], 0, 1)
                        P.op("vector", lambda e: e.tensor_reduce(stat.h[:, 0:1], vg.h[:, :], AX.X, ALU.add), [vga], [stat(stat.h[:, 0:1], 0)])
                        P.act(vsq(vsq.h[:, :]), vga, AF.Square)
                        P.op("vector", lambda e: e.tensor_reduce(stat.h[:, 1:2], vsq.h[:, :], AX.X, ALU.add), [vsq(vsq.h[:, :])], [stat(stat.h[:, 1:2], 1)])
                        P.ts("vector", stat(stat.h[:, 2:3], 2), stat(stat.h[:, 0:1], 0), 1.0 / 1024, None, ALU.mult)
                        P.tt("vector", stat(stat.h[:, 3:4], 3), stat(stat.h[:, 2:3], 2), stat(stat.h[:, 2:3], 2), ALU.mult)
                        P.stt("vector", stat(stat.h[:, 4:5], 4), stat(stat.h[:, 1:2], 1), 1.0 / 1024, stat(stat.h[:, 3:4], 3), ALU.mult, ALU.subtract)
                        P.act(stat(stat.h[:, 5:6], 5), stat(stat.h[:, 4:5], 4), AF.Sqrt, bias=epsc(epsc.h[:, 0:1]), scale=1.0)
                        P.recip(stat(stat.h[:, 6:7], 6), stat(stat.h[:, 5:6], 5))
                        P.ts("vector", vsq(vsq.h[:, :]), vga, stat(stat.h[:, 2:3], 2), stat(stat.h[:, 6:7], 6), ALU.subtract, ALU.mult)
                        P.tt("vector", vsq(vsq.h[:, :]), vsq(vsq.h[:, :]), lng(lng.h[:, :]), ALU.mult)
                        P.tt("vector", vln(vln.h[:, :]), vsq(vsq.h[:, :]), lnb(lnb.h[:, :]), ALU.add)
                        for gh in range(2):
                            pz = nb()
                            for gg in range(4):
                                g = gh * 4 + gg
                                P.mm(pz(pz.h[:, gg * 128:(gg + 1) * 128]), vln(vln.h[:, g * 128:(g + 1) * 128]), wsT(wsT.h[:, g, :]), start=(gg == 0), stop=False)
                                P.mm(pz(pz.h[:, gg * 128:(gg + 1) * 128]), ones_bf(ones_bf.h[0:1, :]), sbb(sbb.h[0:1, g * 128:(g + 1) * 128]), start=False, stop=(gg == 3))
                            P.tt("vector", sgT(sgT.h[:, gh * 4:(gh + 1) * 4, sub * 128:(sub + 1) * 128], (gh, sub)),
                                 pz(pz.h[:, :].rearrange("p (g t) -> p g t", g=4)),
                                 uT(uT.h[:, gh * 4:(gh + 1) * 4, sub * 128:(sub + 1) * 128], *range(gh * 4, gh * 4 + 4)), ALU.mult)
                    sgk = [(gh, sub) for gh in range(2) for sub in range(4)]
                    for j in range(16):
                        wa, wb_ = w8[k8 % 4], w8[(k8 + 1) % 4]
                        k8 += 2
                        wg1, wg2 = w16[k16 % 4], w16[(k16 + 1) % 4]
                        k16 += 2
                        wfetch(tb, wa(wa.h[:, :, :]), wc8_d, wc8_d.h.ap()[2 * j], 2 * j,
                               [(wa(wa.h[:, :, :]), wpn(wpn.h.ap()[l, :, j * 128:(j + 1) * 128].rearrange("(c p) f -> p c f", p=128)))])
                        wfetch(tb, wb_(wb_.h[:, :, :]), wc8_d, wc8_d.h.ap()[2 * j + 1], 2 * j + 1,
                               [(wb_(wb_.h[:, :, :]), wps(wps.h.ap()[l, :, j * 128:(j + 1) * 128].rearrange("(c p) f -> p c f", p=128)))])
                        wfetch(tb, wg1(wg1.h[:, :, :]), wc16_d, wc16_d.h.ap()[28 + 2 * j], 28 + 2 * j, [(wg1(wg1.h[:, :, :]), wcols(OFF_MG + j * 128, 128))])
                        wfetch(tb, wg2(wg2.h[:, :, :]), wc16_d, wc16_d.h.ap()[29 + 2 * j], 29 + 2 * j, [(wg2(wg2.h[:, :, :]), wcols(OFF_MG + D + j * 128, 128))])
                        pa, pb, pg1, pg2 = nb(), nb(), nb(), nb()
                        for k in range(8):
                            P.mm(pa(pa.h[:, :]), wa(wa.h[:, k, :]), aT(aT.h[:, k, t0:t0 + 512], *[("a", i) for i in range(tb * 4, tb * 4 + 4)]),
                                 start=(k == 0), stop=(k == 7))
                        for k in range(8):
                            P.mm(pb(pb.h[:, :]), wb_(wb_.h[:, k, :]), sgT(sgT.h[:, k, :], *sgk), start=(k == 0), stop=(k == 7))
                        for k in range(16):
                            P.mm(pg1(pg1.h[:, :]), wg1(wg1.h[:, k, :]), ht(ht.h[:, k, :], ("h", 0)), start=(k == 0), stop=(k == 15))
                        for k in range(16):
                            P.mm(pg2(pg2.h[:, :]), wg2(wg2.h[:, k, :]), ht(ht.h[:, k, :], ("h", 0)), start=(k == 0), stop=(k == 15))
                        P.act(sg1(sg1.h[:, :]), pg1(pg1.h[:, :]), AF.Sigmoid)
                        P.act(sg2(sg2.h[:, :]), pg2(pg2.h[:, :]), AF.Sigmoid)
                        P.tt("vector", t1(t1.h[:, :]), sg1(sg1.h[:, :]), pa(pa.h[:, :]), ALU.mult)
                        P.tt("vector", sg2(sg2.h[:, :]), sg2(sg2.h[:, :]), pb(pb.h[:, :]), ALU.mult)
                        P.tt("vector", mrg(mrg.h[:, j, :], j), t1(t1.h[:, :]), sg2(sg2.h[:, :]), ALU.add)
                    for j in range(16):
                        w = w16[k16 % 4]
                        k16 += 1
                        wfetch(tb, w(w.h[:, :, :]), wc16_d, wc16_d.h.ap()[60 + j], 60 + j,
                               [(w(w.h[:, :, :]), wo(wo.h.ap()[l, :, j * 128:(j + 1) * 128].rearrange("(c p) f -> p c f", p=128)))])
                        pz = nb()
                        for k in range(16):
                            P.mm(pz(pz.h[:, :]), w(w.h[:, k, :]), mrg(mrg.h[:, k, :], *range(16)), start=(k == 0), stop=(k == 15))
                        xv = xt(xt.h[:, j, :], ("x", 0))
                        P.tt("vector", xv, xv, pz(pz.h[:, :]), ALU.add)
                    P.dma("sync", xdst(xdst.h.ap()[:, :, t0:t0 + 512].rearrange("c p t -> p c t"), tb), xt(xt.h[:, :, :], ("x", 0)))

        cur = x_in
        for l in range(depth):
            last = (l == depth - 1)
            todo = [p for p in "1m2" if p in phases]
            for p in todo:
                dst = out_d if (last and p == todo[-1]) else xs_d
                if p == "1":
                    ffn(l, "ffn1", cur, dst)
                elif p == "m":
                    mixer(l, cur, dst)
                else:
                    ffn(l, "ffn2", cur, dst)
                cur = xs_d
        P.barrier()
        P.emit(top)
    return nc


_NC_CACHE = {}


def _prep_shared(inputs):
    f = lambda a: np.ascontiguousarray(np.asarray(a, np.float32))
    sh = {}
    for k in ("ffn1_w_gate", "ffn1_w_up", "ffn1_w_down", "ffn2_w_gate", "ffn2_w_up", "ffn2_w_down", "w_in",
              "cmp_k_w1", "cmp_k_w2", "cmp_v_w1", "cmp_v_w2", "w_proj_nsa", "w_proj_sgu", "w_out"):
        sh[k] = f(inputs[k])
    sh["cst"] = CST_ARR
    prm = np.zeros((L, 128, NPRM), np.float32)
    for l in range(L):
        for nm, key in (("n1", "ffn1_norm"), ("nm", "mix_norm"), ("n2", "ffn2_norm")):
            prm[l, :, PRM[nm]:PRM[nm] + 16] = f(inputs[key])[l].reshape(16, 128).T
        prm[l, :, PRM["qn"]] = np.tile(f(inputs["q_norm"])[l], 2)
        for i in range(3):
            prm[l, :, PRM["kn"] + i] = np.tile(f(inputs["k_norm"])[l, i], 2)
        prm[l, :, PRM["pk"]:PRM["pk"] + 32] = np.tile(f(inputs["cmp_pos_k"])[l].T, (2, 1))
        prm[l, :, PRM["pv"]:PRM["pv"] + 32] = np.tile(f(inputs["cmp_pos_v"])[l].T, (2, 1))
    sh["prm"] = prm
    rb = f(inputs["rel_bias"])
    rbp = np.zeros((32, 16), np.float32)
    for g in range(4):
        for s in range(4):
            rbp[:, 4 * g + s] = rb[:, 4 * g + 2 * (s % 2) + s // 2]
    sh["rbp"] = rbp
    sh["oh"] = OH_ARR
    sh["expm"] = EXPM_ARR
    sh["sgub"] = np.ascontiguousarray(f(inputs["sgu_b"]).reshape(L, 1, 1024))
    sgb = np.zeros((L, 2, 128, 1024), np.float32)
    sgb[:, 0] = f(inputs["sgu_norm_g"])[:, None, :]
    sgb[:, 1] = f(inputs["sgu_norm_b"])[:, None, :]
    sh["sgb"] = sgb
    sh["sgwT"] = np.ascontiguousarray(f(inputs["sgu_w"]).transpose(0, 3, 1, 2))
    return sh


def kernel(**inputs):
    x = np.asarray(inputs["x"], np.float32)
    sh = _prep_shared(inputs)
    if "nc" not in _NC_CACHE:
        _NC_CACHE["nc"] = build_program()
    nc = _NC_CACHE["nc"]
    in_maps = []
    for b in range(NCORES):
        m = dict(sh)
        m["x_fm"] = np.ascontiguousarray(x[b].T.reshape(16, 128, S))
        in_maps.append(m)
    res = run_bass_kernel_spmd(nc, in_maps, core_ids=list(range(NCORES)))
    out = np.empty((NCORES, S, D), np.float32)
    for b in range(NCORES):
        out[b] = res.results[b]["out_fm"].reshape(D, S).T
    return out
```

```python
import math
from contextlib import ExitStack
import numpy as np
import concourse.bass as bass
import concourse.mybir as mybir
from concourse.bass_utils import run_bass_kernel_spmd

F32 = mybir.dt.float32
BF16 = mybir.dt.bfloat16
ALU = mybir.AluOpType
AF = mybir.ActivationFunctionType
AX = mybir.AxisListType

D = 2048
S = 2048
L = 2
DFF = 5504
NFC = 43
INW = 8752
OFF_KV, OFF_NG, OFF_UV, OFF_MG = 1024, 2560, 2608, 4656
EPS = 1e-6
NEGB = -30000.0
NCORES = 4

ENGS = ["sync", "gpsimd", "scalar", "vector", "tensor"]
NDSEM = 24


class Trk:
    __slots__ = ("w", "r", "x")

    def __init__(self, x=False):
        self.w = None
        self.r = {}
        self.x = x


class V:
    __slots__ = ("ap", "trks")

    def __init__(self, ap, trks):
        self.ap = ap
        self.trks = trks


class Buf:
    def __init__(self, h, excl=False):
        self.h = h
        self.t = {}
        self.excl = excl

    def __call__(self, ap, *keys):
        if not keys:
            keys = (0,)
        out = []
        for k in keys:
            t = self.t.get(k)
            if t is None:
                t = self.t[k] = Trk(self.excl)
            out.append(t)
        return V(ap, out)


class Prog:
    def __init__(self, nc):
        self.nc = nc
        self.ops = {e: [] for e in ENGS}
        self.seen = {e: {} for e in ENGS}
        self.seen_dma = {e: set() for e in ENGS}
        self.ndma = {e: 0 for e in ENGS}

    def _dep(self, eng, waits, d):
        if d[0] == "dma":
            if d in self.seen_dma[eng]:
                return
            self.seen_dma[eng].add(d)
            waits.append(d)
        else:
            e, i = d
            if e == eng and eng in ("tensor", "sync"):
                return
            if self.seen[eng].get(e, -1) >= i:
                return
            self.seen[eng][e] = i
            self.ops[e][i]["inc"] = True
            waits.append(d)

    def op(self, eng, fn, reads=(), writes=(), dma=False):
        idx = len(self.ops[eng])
        waits = []
        deps = []
        rd, wr = [], []
        for v in reads:
            for t in v.trks:
                (wr if t.x else rd).append(t)
        for v in writes:
            wr.extend(v.trks)
        for t in rd:
            if t.w is not None:
                deps.append(t.w)
        for t in wr:
            if t.w is not None:
                deps.append(t.w)
            deps.extend(t.r.values())
        if dma:
            n = self.ndma[eng]
            self.ndma[eng] += 1
            ev = ("dma", eng, n)
            if n >= NDSEM:
                deps.append(("dma", eng, n - NDSEM))
        else:
            ev = (eng, idx)
        for d in deps:
            self._dep(eng, waits, d)
        self.ops[eng].append({"fn": fn, "waits": waits, "inc": False, "dma": ev if dma else None})
        for t in rd:
            t.r[ev if dma else eng] = ev
        for t in wr:
            t.w = ev
            t.r = {}
        return ev

    def barrier(self):
        evs = []
        for e in ENGS:
            if self.ops[e]:
                last = len(self.ops[e]) - 1
                while last >= 0 and self.ops[e][last]["fn"] is None:
                    last -= 1
                if last >= 0 and self.ops[e][last]["dma"] is None:
                    evs.append((e, last))
            n = self.ndma[e]
            for k in range(max(0, n - NDSEM), n):
                evs.append(("dma", e, k))
        for e in ENGS:
            waits = []
            for d in evs:
                if d[0] != "dma" and d[0] == e:
                    continue
                self._dep(e, waits, d)
            self.ops[e].append({"fn": None, "waits": waits, "inc": False, "dma": None})

    def emit(self, stack):
        nc = self.nc
        esem = {e: stack.enter_context(nc.semaphore("es_" + e)) for e in ENGS}
        dsem = {e: [stack.enter_context(nc.semaphore("ds_%s_%d" % (e, i))) for i in range(NDSEM)]
                for e in ENGS if self.ndma[e] > 0}
        for e in ENGS:
            c = 0
            for o in self.ops[e]:
                if o["inc"]:
                    c += 1
                o["cnt"] = c
        block = stack.enter_context(nc.Block())
        ops = self.ops

        def run(engobj, e):
            for o in ops[e]:
                for d in o["waits"]:
                    if d[0] == "dma":
                        _, q, n = d
                        engobj.wait_ge(dsem[q][n % NDSEM], 16 * (n // NDSEM + 1))
                    else:
                        engobj.wait_ge(esem[d[0]], ops[d[0]][d[1]]["cnt"])
                if o["fn"] is None:
                    continue
                ins = o["fn"](engobj)
                if o["dma"] is not None:
                    _, q, n = o["dma"]
                    ins.then_inc(dsem[q][n % NDSEM], 16)
                elif o["inc"]:
                    ins.then_inc(esem[e], 1)

        @block.sync
        def _(x):
            run(x, "sync")

        @block.gpsimd
        def _(x):
            run(x, "gpsimd")

        @block.scalar
        def _(x):
            run(x, "scalar")

        @block.vector
        def _(x):
            run(x, "vector")

        @block.tensor
        def _(x):
            run(x, "tensor")

    def dma(self, q, out, in_):
        return self.op(q, lambda e: e.dma_start(out=out.ap, in_=in_.ap), [in_], [out], dma=True)

    def mm(self, out, lhsT, rhs, start=True, stop=True):
        return self.op("tensor", lambda e: e.matmul(out.ap, lhsT.ap, rhs.ap, start=start, stop=stop),
                       [lhsT, rhs], [out])

    def transpose(self, out, in_, ident):
        return self.op("tensor", lambda e: e.transpose(out.ap, in_.ap, ident.ap), [in_, ident], [out])

    def act(self, out, in_, func, bias=None, scale=None):
        reads = [in_]
        kw = {}
        if bias is not None:
            reads.append(bias)
            kw["bias"] = bias.ap
        if scale is not None:
            kw["scale"] = scale
        return self.op("scalar", lambda e: e.activation(out.ap, in_.ap, func, **kw), reads, [out])

    def tt(self, eng, out, in0, in1, op):
        return self.op(eng, lambda e: e.tensor_tensor(out.ap, in0.ap, in1.ap, op), [in0, in1], [out])

    def ts(self, eng, out, in0, s1, s2, op0, op1=None):
        reads = [in0]
        a1, a2 = s1, s2
        if isinstance(s1, V):
            reads.append(s1)
            a1 = s1.ap
        if isinstance(s2, V):
            reads.append(s2)
            a2 = s2.ap
        kw = {}
        if op1 is not None:
            kw["op1"] = op1
        return self.op(eng, lambda e: e.tensor_scalar(out.ap, in0.ap, a1, a2, op0, **kw), reads, [out])

    def stt(self, eng, out, in0, scalar, in1, op0, op1):
        reads = [in0, in1]
        a = scalar
        if isinstance(scalar, V):
            reads.append(scalar)
            a = scalar.ap
        return self.op(eng, lambda e: e.scalar_tensor_tensor(out.ap, in0.ap, a, in1.ap, op0, op1), reads, [out])

    def copy(self, eng, out, in_):
        if eng == "scalar":
            return self.op(eng, lambda e: e.copy(out.ap, in_.ap), [in_], [out])
        return self.op(eng, lambda e: e.tensor_copy(out.ap, in_.ap), [in_], [out])

    def memset(self, eng, out, val):
        return self.op(eng, lambda e: e.memset(out.ap, val), [], [out])

    def recip(self, out, in_):
        return self.op("vector", lambda e: e.reciprocal(out.ap, in_.ap), [in_], [out])


def _t5_bucket(dist):
    n = np.maximum(dist, 0)
    nf = np.maximum(n, 1).astype(np.float32)
    large = 16 + (np.log(nf / np.float32(16)) / np.float32(math.log(128 / 16)) * np.float32(16)).astype(np.int32)
    large = np.minimum(large, 31)
    return np.where(n < 16, n, large)


CST = {}


def _build_consts():
    cols = []
    off = [0]

    def add(name, arr):
        arr = np.asarray(arr, np.float32)
        a = np.zeros((128, arr.shape[1]), np.float32)
        a[:arr.shape[0]] = arr
        CST[name] = (off[0], arr.shape[1])
        off[0] += arr.shape[1]
        cols.append(a)

    idx = np.arange(4096)
    dist = idx - 2048
    oh = np.zeros((33, 4096), np.float32)
    bk = _t5_bucket(dist)
    for i in range(4096):
        if dist[i] < 0:
            oh[32, i] = 1
        else:
            oh[bk[i], i] = 1
    CST["_oh"] = oh
    add("ident", np.eye(128))
    add("j128", np.eye(128)[::-1])
    j127 = np.zeros((128, 128))
    j127[:127, :127] = np.eye(127)[::-1]
    add("j127", j127)
    s_ = np.arange(128)[:, None]
    t_ = np.arange(128)[None, :]
    add("triu", (s_ <= t_))
    expm = np.zeros((32, 16, 128))
    for jc in range(16):
        for k in range(128):
            expm[2 * jc + k // 64, jc, k] = 1
    CST["_expm"] = expm.reshape(32, -1).astype(np.float32)
    ci = np.arange(127)[:, None] * 16
    sj = np.arange(32)[None, :] * 64
    ov = np.clip(np.minimum(ci + 32, sj + 64) - np.maximum(ci, sj), 0, None).astype(np.float32) / 32
    add("ov", ov)
    notf = np.zeros((128, 16, 32))
    addv = np.zeros((128, 16, 32))
    vneg = np.zeros((128, 16, 32))
    for i in range(16):
        for p in range(128):
            cur = (128 * i + p) // 64
            for j in range(32):
                if j > cur:
                    addv[p, i, j] = -1e30
                    vneg[p, i, j] = NEGB
                elif j == 0:
                    addv[p, i, j] = 1e4
                elif j == cur:
                    addv[p, i, j] = 2e4
                elif j == cur - 1:
                    addv[p, i, j] = 3e4
                else:
                    notf[p, i, j] = 1
    add("notf", notf.reshape(128, -1))
    add("addv", addv.reshape(128, -1))
    add("vneg", vneg.reshape(128, -1))
    m6 = (t_ < s_).astype(np.float32)
    add("m6", m6)
    add("n6", (m6 - 1) * 30000.0)
    return np.concatenate(cols, axis=1)


CST_ARR = _build_consts()
NCST = CST_ARR.shape[1]
PRM = {"n1": 0, "nm": 16, "n2": 32, "qn": 48, "kn": 49, "pk": 52, "pv": 84}
NPRM = 116
OH_ARR = CST["_oh"]
EXPM_ARR = CST["_expm"]


def build_program(depth=L, dbg=None, phases="1m2", mstop=None, ntiles=16, tstop=8):
    nc = bass.Bass("TRN2", target_bir_lowering=False)
    dr = {}

    def din(name, shape, dt=F32):
        dr[name] = Buf(nc.dram_tensor(name, list(shape), dt, kind="ExternalInput"))
        return dr[name]

    x_in = din("x_fm", [16, 128, S])
    cst_d = din("cst", [128, NCST])
    prm_d = din("prm", [L, 128, NPRM])
    rb_d = din("rbp", [32, 16])
    oh_d = din("oh", [33, 4096])
    expm_d = din("expm", [32, 2048])
    sgub_d = din("sgub", [L, 1, 1024])
    sgb_d = din("sgb", [L, 2, 128, 1024])
    sgw_d = din("sgwT", [L, 128, 8, 128])
    wnames = {"ffn1_w_gate": [L, D, DFF], "ffn1_w_up": [L, D, DFF], "ffn1_w_down": [L, DFF, D],
              "ffn2_w_gate": [L, D, DFF], "ffn2_w_up": [L, D, DFF], "ffn2_w_down": [L, DFF, D],
              "w_in": [L, D, INW], "cmp_k_w1": [L, 2048, 256], "cmp_k_w2": [L, 256, 64],
              "cmp_v_w1": [L, 2048, 256], "cmp_v_w2": [L, 256, 64],
              "w_proj_nsa": [L, 1024, D], "w_proj_sgu": [L, 1024, D], "w_out": [L, D, D]}
    for k, shp in wnames.items():
        din(k, shp)
    out_d = Buf(nc.dram_tensor("out_fm", [16, 128, S], F32, kind="ExternalOutput"))
    xs_d = Buf(nc.dram_tensor("xs", [16, 128, S], F32))
    fd_d = Buf(nc.dram_tensor("fdt", [16, 4096], BF16))
    wc16_d = Buf(nc.dram_tensor("wc16", [76, 128, 16, 128], BF16))
    wc8_d = Buf(nc.dram_tensor("wc8", [32, 128, 8, 128], BF16))
    wcv_d = Buf(nc.dram_tensor("wcv", [128, 16, 560], BF16))
    wcv2_d = Buf(nc.dram_tensor("wcv2", [2, 128, 16, 512], BF16))
    dbg_d = {}
    if dbg:
        for name, shp in dbg.items():
            dbg_d[name] = Buf(nc.dram_tensor("dbg_" + name, list(shp), F32, kind="ExternalOutput"))

    P = Prog(nc)
    _NC_CACHE['P'] = P
    out_events = []

    with ExitStack() as top:
        top.enter_context(nc.allow_low_precision("bf16 matmul operands, fp32 accumulation"))

        uniq = [0]

        def sbt(st, name, shape, dt):
            uniq[0] += 1
            return Buf(st.enter_context(nc.sbuf_tensor("%s_%d" % (name, uniq[0]), list(shape), dt)))

        banks = [Buf(top.enter_context(nc.psum_tensor("pb%d" % i, [128, 512], F32)), excl=True) for i in range(8)]
        bank_i = [0]
        bank_n = [8]

        def nb():
            b = banks[bank_i[0] % bank_n[0]]
            bank_i[0] += 1
            return b

        cst = sbt(top, "cst_sb", [128, NCST], F32)
        P.dma("sync", cst(cst.h[:, :]), cst_d(cst_d.h.ap()))

        def cv(name, rows=128, lo=0, n=None):
            o, w = CST[name]
            if n is None:
                n = w - lo
            return cst(cst.h[0:rows, o + lo:o + lo + n])

        prm = sbt(top, "prm_sb", [128, L, NPRM], F32)
        P.dma("sync", prm(prm.h[:, :, :]), prm_d(prm_d.h.ap().rearrange("l p n -> p l n")))
        ones_bf = sbt(top, "ones_bf", [128, 128], BF16)
        P.memset("vector", ones_bf(ones_bf.h[:, :]), 1.0)
        blk_bf = sbt(top, "blk_bf", [128, 128], BF16)
        P.memset("vector", blk_bf(blk_bf.h[:, :]), 0.0)
        P.memset("vector", blk_bf(blk_bf.h[0:64, 0:64]), 1.0)
        P.memset("vector", blk_bf(blk_bf.h[64:128, 64:128]), 1.0)
        ident_bf = sbt(top, "ident_bf", [128, 128], BF16)
        P.copy("vector", ident_bf(ident_bf.h[:, :]), cv("ident"))
        epsc = sbt(top, "epsc", [128, 2], F32)
        P.memset("vector", epsc(epsc.h[:, 0:1]), EPS)
        P.memset("vector", epsc(epsc.h[:, 1:2]), 0.0)
        gsc = sbt(top, "gsc", [128, L, 52], F32)
        for l in range(L):
            P.copy("vector", gsc(gsc.h[:, l, 0:49]), prm(prm.h[:, l, 0:49]))
            P.ts("vector", gsc(gsc.h[:, l, 49:50]), prm(prm.h[:, l, 48:49]), 0.125, None, ALU.mult)

        def pcol(l, name, c=0):
            return gsc(gsc.h[:, l, PRM[name] + c:PRM[name] + c + 1])

        def make_h(st_x, st_h, xsrc, l, nname, tok0, hcol0, sq, sd):
            xt, xo = st_x
            ht, ho = st_h
            P.dma("sync", xt(xt.h[:, :, xo:xo + 512], ("x", xo)),
                  xsrc(xsrc.h.ap()[:, :, tok0:tok0 + 512].rearrange("c p t -> p c t"), tok0 // 512))
            psn = nb()
            for c in range(16):
                s = sq[c % 2]
                P.act(s(s.h[:, :]), xt(xt.h[:, c, xo:xo + 512], ("x", xo)), AF.Square)
                P.mm(psn(psn.h[:, :]), ones_bf(ones_bf.h[:, :]), s(s.h[:, :]), start=(c == 0), stop=(c == 15))
            P.act(sd(sd.h[:, :]), psn(psn.h[:, :]), AF.Sqrt, bias=epsc(epsc.h[:, 0:1]), scale=1.0 / D)
            P.recip(sd(sd.h[:, :]), sd(sd.h[:, :]))
            for c in range(16):
                P.stt("vector", ht(ht.h[:, c, ho:ho + 512], ("h", ho)), xt(xt.h[:, c, xo:xo + 512], ("x", xo)),
                      pcol(l, nname, c), sd(sd.h[:, :]), ALU.mult, ALU.mult)

        def wfetch(tb, wt_view, cbuf, cap, ckey, loaders):
            if tb == 0:
                for d_, s_ in loaders:
                    P.dma("gpsimd", d_, s_)
                P.dma("sync", cbuf(cap, ckey), wt_view)
            else:
                P.dma("sync", wt_view, cbuf(cap, ckey))

        def wload(wt, src_ap):
            P.dma("gpsimd", wt(wt.h[:, :, :]) if len(wt.h.shape) == 3 else wt(wt.h[:, :]), src_ap)

        def ffn(l, which, xsrc, xdst):
            wg_d, wu_d, wd_d = dr[which + "_w_gate"], dr[which + "_w_up"], dr[which + "_w_down"]
            nname = "n1" if which == "ffn1" else "n2"
            with ExitStack() as st:
                xt = sbt(st, "f_x", [128, 16, 1024], F32)
                ht = sbt(st, "f_h", [128, 16, 1024], BF16)
                act = sbt(st, "f_act", [128, 22, 1024], BF16)
                sq = [sbt(st, "f_sq%d" % i, [128, 512], BF16) for i in range(2)]
                sd = sbt(st, "f_sd", [128, 512], F32)
                wg = [sbt(st, "f_wg%d" % i, [128, 16, 128], BF16) for i in range(3)]
                wu = [sbt(st, "f_wu%d" % i, [128, 16, 128], BF16) for i in range(3)]
                wd = [sbt(st, "f_wd%d" % i, [128, 22, 256], BF16) for i in range(2)]
                sl = [sbt(st, "f_sl%d" % i, [128, 512], BF16) for i in range(2)]
                wi = 0
                di = 0
                for tt in range(2):
                    for tb in range(2):
                        make_h((xt, tb * 512), (ht, tb * 512), xsrc, l, nname, tt * 1024 + tb * 512, 0, sq, sd)
                    for fh in range(2):
                        fcs = list(range(0, 22)) if fh == 0 else list(range(22, 43))
                        for fi, fc in enumerate(fcs):
                            g, u = wg[wi % 3], wu[wi % 3]
                            wi += 1
                            P.dma("gpsimd", g(g.h[:, :, :]),
                                  wg_d(wg_d.h.ap()[l, :, fc * 128:(fc + 1) * 128].rearrange("(c p) f -> p c f", p=128)))
                            P.dma("gpsimd", u(u.h[:, :, :]),
                                  wu_d(wu_d.h.ap()[l, :, fc * 128:(fc + 1) * 128].rearrange("(c p) f -> p c f", p=128)))
                            for tb in range(2):
                                pg, pu = nb(), nb()
                                for c in range(16):
                                    P.mm(pg(pg.h[:, :]), g(g.h[:, c, :]), ht(ht.h[:, c, tb * 512:(tb + 1) * 512], ("h", tb * 512)),
                                         start=(c == 0), stop=(c == 15))
                                for c in range(16):
                                    P.mm(pu(pu.h[:, :]), u(u.h[:, c, :]), ht(ht.h[:, c, tb * 512:(tb + 1) * 512], ("h", tb * 512)),
                                         start=(c == 0), stop=(c == 15))
                                s = sl[(fi * 2 + tb) % 2]
                                P.act(s(s.h[:, :]), pg(pg.h[:, :]), AF.Silu)
                                P.tt("vector", act(act.h[:, fi, tb * 512:(tb + 1) * 512], (fi, tb)), s(s.h[:, :]), pu(pu.h[:, :]), ALU.mult)
                        nf = len(fcs)
                        for dcp in range(8):
                            w = wd[di % 2]
                            di += 1
                            P.dma("gpsimd", w(w.h[:, 0:nf, :]),
                                  wd_d(wd_d.h.ap()[l, fcs[0] * 128:(fcs[-1] + 1) * 128, dcp * 256:(dcp + 1) * 256]
                                       .rearrange("(f p) d -> p f d", p=128)))
                            for ds in range(2):
                                dc = dcp * 2 + ds
                                for tb in range(2):
                                    pd = nb()
                                    for fi in range(nf):
                                        P.mm(pd(pd.h[:, :]), w(w.h[:, fi, ds * 128:(ds + 1) * 128]),
                                             act(act.h[:, fi, tb * 512:(tb + 1) * 512], (fi, tb)), start=(fi == 0), stop=(fi == nf - 1))
                                    xv = xt(xt.h[:, dc, tb * 512:(tb + 1) * 512], ("x", tb * 512))
                                    P.stt("vector", xv, pd(pd.h[:, :]), 0.5, xv, ALU.mult, ALU.add)
                    ev = P.dma("sync", xdst(xdst.h.ap()[:, :, tt * 1024:(tt + 1) * 1024].rearrange("c p t -> p c t"), 2 * tt, 2 * tt + 1),
                               xt(xt.h[:, :, :], ("x", 0), ("x", 512)))
                    if xdst is out_d:
                        out_events.append(ev)
            P.barrier()

        def build_tables(st):
            dall = sbt(st, "dall", [128, 7, 16, 128], BF16)
            with ExitStack() as s2:
                rbx = sbt(s2, "rbx", [33, 16], F32)
                rbb = sbt(s2, "rbb", [33, 16], BF16)
                ohb = sbt(s2, "ohb", [33, 4096], BF16)
                fsb = sbt(s2, "fsb", [16, 4096], BF16)
                hk = sbt(s2, "hk", [128, 16, 128], BF16)
                P.memset("vector", rbx(rbx.h[:, :]), NEGB)
                P.dma("sync", rbx(rbx.h[0:32, :]), rb_d(rb_d.h.ap()))
                P.copy("vector", rbb(rbb.h[:, :]), rbx(rbx.h[:, :]))
                P.dma("gpsimd", ohb(ohb.h[:, :]), oh_d(oh_d.h.ap()))
                for n in range(8):
                    pb = nb()
                    P.mm(pb(pb.h[0:16, :]), rbb(rbb.h[:, :]), ohb(ohb.h[:, n * 512:(n + 1) * 512]))
                    P.copy("vector", fsb(fsb.h[:, n * 512:(n + 1) * 512]), pb(pb.h[0:16, :]))
                P.dma("sync", fd_d(fd_d.h.ap()), fsb(fsb.h[:, :]))
                j128 = sbt(s2, "j128b", [128, 128], BF16)
                P.copy("vector", j128(j128.h[:, :]), cv("j128"))
                for dl in range(6):
                    src = bass.AP(fd_d.h, 2048 + dl * 128 - 127, [[1, 128], [4096, 16], [1, 128]])
                    P.dma("sync", hk(hk.h[:, :, :]), fd_d(src))
                    for n in range(4):
                        pb = nb()
                        P.mm(pb(pb.h[:, :]), j128(j128.h[:, :]), hk(hk.h[:, n * 4:(n + 1) * 4, :]))
                        P.copy("vector", dall(dall.h[:, dl, n * 4:(n + 1) * 4, :]), pb(pb.h[:, :]))
                for h in range(16):
                    P.tt("vector", dall(dall.h[:, 6, h, :]), dall(dall.h[:, 5, h, :]), cv("m6"), ALU.mult)
                    P.tt("vector", dall(dall.h[:, 6, h, :]), dall(dall.h[:, 6, h, :]), cv("n6"), ALU.add)
            P.barrier()
            return dall

        def mixer(l, xsrc, xdst):
            w_in = dr["w_in"]

            def wcols(c0, n):
                return w_in(w_in.h.ap()[l, :, c0:c0 + n].rearrange("(c p) f -> p c f", p=128))

            with ExitStack() as sq_:
                qT = sbt(sq_, "m_qT", [128, 8, S], BF16)
                with ExitStack() as st:
                    ksd = sbt(st, "m_ksd", [128, 4, S], BF16)
                    kwd = sbt(st, "m_kwd", [128, 4, S], BF16)
                    kcT = sbt(st, "m_kcT", [128, 2, S], BF16)
                    vcT = sbt(st, "m_vcT", [128, 2, S], BF16)
                    vsa = sbt(st, "m_vsa", [128, 16, 4, 65], BF16)
                    vwa = sbt(st, "m_vwa", [128, 16, 4, 65], BF16)
                    gat = sbt(st, "m_gat", [128, 16, 48], F32)
                    P.memset("vector", vsa(vsa.h[:, :, :, 64:65]), 1.0)
                    P.memset("vector", vwa(vwa.h[:, :, :, 64:65]), 1.0)
                    with ExitStack() as sa:
                        xt = sbt(sa, "a_x", [128, 16, 512], F32)
                        ht = sbt(sa, "a_h", [128, 16, 512], BF16)
                        sq = [sbt(sa, "a_sq%d" % i, [128, 512], BF16) for i in range(2)]
                        sd = sbt(sa, "a_sd", [128, 512], F32)
                        rs = sbt(sa, "a_rs", [128, 512], F32)
                        wq = [sbt(sa, "a_wq%d" % i, [128, 16, 128], BF16) for i in range(4)]
                        wv = sbt(sa, "a_wv", [128, 16, 560], BF16)
                        wi = 0
                        for tb in range(4):
                            t0 = tb * 512
                            make_h((xt, 0), (ht, 0), xsrc, l, "nm", t0, 0, sq, sd)
                            hv = lambda c: ht(ht.h[:, c, :], ("h", 0))
                            jobs = [("q", c) for c in range(8)] + [("ks", g) for g in range(4)] + \
                                   [("kw", g) for g in range(4)] + [("kc", c) for c in range(2)] + [("vc", c) for c in range(2)]
                            for jn, (kind, ix) in enumerate(jobs):
                                w = wq[wi % 4]
                                wi += 1
                                wall = w(w.h[:, :, :])
                                if kind == "q":
                                    lds = [(wall, wcols(ix * 128, 128))]
                                elif kind in ("ks", "kw"):
                                    c0 = OFF_KV + (2 if kind == "ks" else 4) * 256 + ix * 64
                                    lds = [(w(w.h[:, :, 0:64]), wcols(c0, 64)), (w(w.h[:, :, 64:128]), wcols(c0, 64))]
                                else:
                                    c0 = OFF_KV + (0 if kind == "kc" else 1) * 256 + ix * 128
                                    lds = [(wall, wcols(c0, 128))]
                                wfetch(tb, wall, wc16_d, wc16_d.h.ap()[jn], jn, lds)
                                wvw = wall
                                pz = nb()
                                for c in range(16):
                                    P.mm(pz(pz.h[:, :]), V(w.h[:, c, :], wvw.trks), hv(c), start=(c == 0), stop=(c == 15))
                                if kind in ("kc", "vc"):
                                    dst = kcT if kind == "kc" else vcT
                                    P.copy("scalar", dst(dst.h[:, ix, t0:t0 + 512], tb), pz(pz.h[:, :]))
                                    continue
                                s = sq[wi % 2]
                                P.act(s(s.h[:, :]), pz(pz.h[:, :]), AF.Square)
                                pm = nb()
                                P.mm(pm(pm.h[:, :]), blk_bf(blk_bf.h[:, :]), s(s.h[:, :]))
                                P.act(rs(rs.h[:, :]), pm(pm.h[:, :]), AF.Sqrt, bias=epsc(epsc.h[:, 0:1]), scale=1.0 / 64)
                                P.recip(rs(rs.h[:, :]), rs(rs.h[:, :]))
                                if kind == "q":
                                    dstv = qT(qT.h[:, ix, t0:t0 + 512], tb)
                                    gc = gsc(gsc.h[:, l, 49:50])
                                    P.stt("vector", dstv, pz(pz.h[:, :]), gc, rs(rs.h[:, :]), ALU.mult, ALU.mult)
                                else:
                                    dst = ksd if kind == "ks" else kwd
                                    kn = 1 if kind == "ks" else 2
                                    gc = prm(prm.h[:, l, PRM["kn"] + kn:PRM["kn"] + kn + 1])
                                    P.stt("vector", dst(dst.h[:, ix, t0:t0 + 512], tb), pz(pz.h[:, :]), gc, rs(rs.h[:, :]), ALU.mult, ALU.mult)
                            wfetch(tb, wv(wv.h[:, :, :]), wcv_d, wcv_d.h.ap(), 0,
                                   [(wv(wv.h[:, :, 0:256]), wcols(OFF_KV + 3 * 256, 256)),
                                    (wv(wv.h[:, :, 256:512]), wcols(OFF_KV + 5 * 256, 256)),
                                    (wv(wv.h[:, :, 512:560]), wcols(OFF_NG, 48))])
                            wvt = wv(wv.h[:, :, :]).trks
                            for sub in range(4):
                                ti = tb * 4 + sub
                                p1, p2 = nb(), nb()
                                for c in range(16):
                                    P.mm(p1(p1.h[:, :]), ht(ht.h[:, c, sub * 128:(sub + 1) * 128], ("h", 0)), V(wv.h[:, c, 0:512], wvt),
                                         start=(c == 0), stop=(c == 15))
                                for c in range(16):
                                    P.mm(p2(p2.h[:, 0:48]), ht(ht.h[:, c, sub * 128:(sub + 1) * 128], ("h", 0)), V(wv.h[:, c, 512:560], wvt),
                                         start=(c == 0), stop=(c == 15))
                                P.copy("vector", vsa(vsa.h[:, ti, :, 0:64], ti), p1(p1.h[:, 0:256].rearrange("p (g d) -> p g d", g=4)))
                                P.copy("vector", vwa(vwa.h[:, ti, :, 0:64], ti), p1(p1.h[:, 256:512].rearrange("p (g d) -> p g d", g=4)))
                                P.act(gat(gat.h[:, ti, :], ti), p2(p2.h[:, 0:48]), AF.Sigmoid)
                    P.barrier()
                    if mstop != "A":
                        dall = build_tables(st)
                    if mstop in ("A", "T"):
                        with ExitStack() as sx:
                            xc = sbt(sx, "passx", [128, 16, 512], F32)
                            for tb in range(4):
                                P.dma("sync", xc(xc.h[:, :, :]), xsrc(xsrc.h.ap()[:, :, tb * 512:(tb + 1) * 512].rearrange("c p t -> p c t"), tb))
                                P.dma("sync", xdst(xdst.h.ap()[:, :, tb * 512:(tb + 1) * 512].rearrange("c p t -> p c t"), tb), xc(xc.h[:, :, :]))
                            P.barrier()
                        return
                    attention(l, st, qT, ksd, kwd, kcT, vcT, vsa, vwa, gat, dall)
                    bank_n[0] = 8
                P.barrier()
                if mstop == "X":
                    with ExitStack() as sx:
                        xc = sbt(sx, "passx2", [128, 16, 512], F32)
                        for tb in range(4):
                            P.dma("sync", xc(xc.h[:, :, :]), xsrc(xsrc.h.ap()[:, :, tb * 512:(tb + 1) * 512].rearrange("c p t -> p c t"), tb))
                            P.dma("sync", xdst(xdst.h.ap()[:, :, tb * 512:(tb + 1) * 512].rearrange("c p t -> p c t"), tb), xc(xc.h[:, :, :]))
                        P.barrier()
                    return
                stage_d(l, qT, xsrc, xdst, wcols)
            P.barrier()

        def dump(name, view_fn_list):
            pass

        def dump_attn_inputs(qT, ksd, kwd, kcT, vcT, vsa, vwa, gat, dall):
            with ExitStack() as sd_:
                tmp = sbt(sd_, "dbg_tmp", [128, 8 * 512], F32)
                if "qT" in dbg_d:
                    o = dbg_d["qT"]
                    P.copy("vector", tmp(tmp.h[:, :].rearrange("p (c t) -> p c t", c=8)), qT(qT.h[:, :, 0:512], 0))
                    P.dma("sync", o(o.h.ap()), tmp(tmp.h[:, :]))
                if "ksd" in dbg_d:
                    o = dbg_d["ksd"]
                    t2 = sbt(sd_, "dbg_t2", [128, 4 * 512], F32)
                    P.copy("vector", t2(t2.h[:, :].rearrange("p (c t) -> p c t", c=4)), ksd(ksd.h[:, :, 0:512], 0))
                    P.dma("sync", o(o.h.ap()), t2(t2.h[:, :]))
                if "gat" in dbg_d:
                    o = dbg_d["gat"]
                    P.dma("sync", o(o.h.ap()), gat(gat.h[:, :, :].rearrange("p a b -> p (a b)") if False else gat.h[:, 0, :], 0))
                if "dall" in dbg_d:
                    o = dbg_d["dall"]
                    t3 = sbt(sd_, "dbg_t3", [128, 7 * 128], F32)
                    P.copy("vector", t3(t3.h[:, :].rearrange("p (c t) -> p c t", c=7)), dall(dall.h[:, :, 5, :]))
                    P.dma("sync", o(o.h.ap()), t3(t3.h[:, :]))

        def attention(l, st, qT, ksd, kwd, kcT, vcT, vsa, vwa, gat, dall):
            with ExitStack() as sc:
                kcn = sbt(sc, "c_kcn", [128, 4, 128], BF16)
                vcx = sbt(sc, "c_vcx", [128, 4, 97], BF16)
                P.memset("vector", vcx(vcx.h[:, :, 64:65]), 1.0)
                for g in range(4):
                    P.copy("vector", vcx(vcx.h[0:127, g, 65:97], "ov"), cv("ov", rows=127))
                expb = sbt(sc, "c_expb", [32, 16, 128], BF16)
                P.dma("gpsimd", expb(expb.h[:, :, :]), expm_d(expm_d.h.ap().rearrange("p (a b) -> p a b", a=16)))
                j127 = sbt(sc, "c_j127", [128, 128], BF16)
                P.copy("vector", j127(j127.h[:, :]), cv("j127"))
                with ExitStack() as s2:
                    w1 = sbt(s2, "c_w1", [128, 32, 256], BF16)
                    w2 = sbt(s2, "c_w2", [128, 2, 128], BF16)
                    posb = sbt(s2, "c_posb", [128, 32], BF16)
                    pbias = sbt(s2, "c_pbias", [128, 2], F32)
                    hid = sbt(s2, "c_hid", [128, 2, 128], BF16)
                    sqc = sbt(s2, "c_sq", [128, 128], BF16)
                    rsc = sbt(s2, "c_rs", [128, 128], F32)
                    for kind in ("k", "v"):
                        w1d = dr["cmp_%s_w1" % kind]
                        w2d = dr["cmp_%s_w2" % kind]
                        src1 = w1d.h.ap()[l].rearrange("(l d) f -> d l f", d=64)
                        P.dma("gpsimd", w1(w1.h[0:64, :, :], "a"), w1d(src1))
                        P.dma("gpsimd", w1(w1.h[64:128, :, :], "b"), w1d(src1))
                        src2 = w2d.h.ap()[l].rearrange("(fh p) d -> p fh d", p=128)
                        P.dma("gpsimd", w2(w2.h[:, :, 0:64], "a"), w2d(src2))
                        P.dma("gpsimd", w2(w2.h[:, :, 64:128], "b"), w2d(src2))
                        w1t = w1(w1.h[:, :, :], "a", "b").trks
                        w2t = w2(w2.h[:, :, :], "a", "b").trks
                        pn = "pk" if kind == "k" else "pv"
                        P.copy("vector", posb(posb.h[:, :]), prm(prm.h[:, l, PRM[pn]:PRM[pn] + 32]))
                        for fh in range(2):
                            pb = nb()
                            for ll in range(32):
                                P.mm(pb(pb.h[:, 0:1]), V(w1.h[0:64, ll, fh * 128:(fh + 1) * 128], w1t), posb(posb.h[0:64, ll:ll + 1]),
                                     start=(ll == 0), stop=(ll == 31))
                            P.copy("vector", pbias(pbias.h[:, fh:fh + 1]), pb(pb.h[:, 0:1]))
                        srcT = kcT if kind == "k" else vcT
                        for g in range(4):
                            hf, ch = g % 2, g // 2
                            for fh in range(2):
                                pb = nb()
                                for ll in range(32):
                                    r16 = srcT.h[hf * 64:(hf + 1) * 64, ch, :].rearrange("p (c s) -> p c s", s=16)
                                    rhs = r16[:, 0:127, ll] if ll < 16 else r16[:, 1:128, ll - 16]
                                    P.mm(pb(pb.h[:, 0:127]), V(w1.h[hf * 64:(hf + 1) * 64, ll, fh * 128:(fh + 1) * 128], w1t),
                                         srcT(rhs, 0, 1, 2, 3), start=(ll == 0), stop=(ll == 31))
                                P.act(hid(hid.h[:, fh, 0:127], fh), pb(pb.h[:, 0:127]), AF.Silu, bias=pbias(pbias.h[:, fh:fh + 1]))
                            if kind == "k":
                                pb = nb()
                                for fh in range(2):
                                    P.mm(pb(pb.h[:, 0:127]), V(w2.h[:, fh, :], w2t), hid(hid.h[:, fh, 0:127], fh), start=(fh == 0), stop=(fh == 1))
                                P.act(sqc(sqc.h[:, 0:127]), pb(pb.h[:, 0:127]), AF.Square)
                                pm = nb()
                                P.mm(pm(pm.h[:, 0:127]), blk_bf(blk_bf.h[:, :]), sqc(sqc.h[:, 0:127]))
                                P.act(rsc(rsc.h[:, 0:127]), pm(pm.h[:, 0:127]), AF.Sqrt, bias=epsc(epsc.h[:, 0:1]), scale=1.0 / 64)
                                P.recip(rsc(rsc.h[:, 0:127]), rsc(rsc.h[:, 0:127]))
                                gc = prm(prm.h[:, l, PRM["kn"]:PRM["kn"] + 1])
                                P.stt("vector", kcn(kcn.h[:, g, 0:127], g), pb(pb.h[:, 0:127]), gc, rsc(rsc.h[:, 0:127]), ALU.mult, ALU.mult)
                            else:
                                pb = nb()
                                for fh in range(2):
                                    P.mm(pb(pb.h[0:127, 0:64]), hid(hid.h[:, fh, 0:127], fh), V(w2.h[:, fh, 0:64], w2t), start=(fh == 0), stop=(fh == 1))
                                P.copy("vector", vcx(vcx.h[0:127, g, 0:64], ("v", g)), pb(pb.h[0:127, 0:64]))
                P.barrier()
                bct = [sbt(sc, "t_bch%d" % i, [128, 4, 128], BF16) for i in range(4)]
                bcf = [sbt(sc, "t_bcf%d" % i, [128, 4, 128], BF16) for i in range(4)]
                pT = [sbt(sc, "t_pT%d" % i, [128, 512], BF16) for i in range(5)]
                aaccs = [sbt(sc, "t_aacc%d" % k, [128, 16, 64], F32) for k in range(2)]
                abf = sbt(sc, "t_abf", [128, 1024], BF16)
                rden = sbt(sc, "t_rden", [128, 4], F32)
                coef = sbt(sc, "t_coef", [128, 4], F32)
                imp = sbt(sc, "t_imp", [128, 32], F32)
                scr = sbt(sc, "t_scr", [128, 32], F32)
                wrk = sbt(sc, "t_wrk", [128, 32], F32)
                m8 = sbt(sc, "t_m8", [128, 8], F32)
                selnss = [[sbt(sc, "t_seln%d_%d" % (k, g), [128, 32], F32) for g in range(4)] for k in range(2)]
                selTs = [sbt(sc, "t_selT%d" % g, [32, 2, 128], BF16) for g in range(4)]
                bank_n[0] = 6
                cnt = {"p": 0, "b": 0, "a": 0}
                accb = [banks[6], banks[7]]

                def nacc():
                    b = accb[cnt["a"] % 2]
                    cnt["a"] += 1
                    return b

                o1, o2, o3 = CST["notf"][0], CST["addv"][0], CST["vneg"][0]

                def hd(g, s):
                    return 4 * g + 2 * (s % 2) + s // 2

                def tile_fns(i):
                    q0 = i * 128
                    aacc = aaccs[i % 2]
                    selns = selnss[i % 2]

                    def score(g, klhs, rows, bias_fn, mask_j):
                        p = pT[cnt["p"] % 5]
                        cnt["p"] += 1
                        pss = (nb(), nb())
                        idv = ident_bf(ident_bf.h[0:rows, 0:rows])
                        for hf in range(2):
                            rhs = qT(qT.h[hf * 64:(hf + 1) * 64, 2 * g:2 * g + 2, q0:q0 + 128], i // 4)
                            P.mm(pss[hf](pss[hf].h[0:rows, 0:256]), klhs(hf), rhs, start=True, stop=False)
                        for hf in range(2):
                            P.mm(pss[hf](pss[hf].h[0:rows, 0:256]), idv, bias_fn(hf), start=False, stop=(mask_j is None))
                        if mask_j is not None:
                            sT = selTs[g]
                            for hf in range(2):
                                P.mm(pss[hf](pss[hf].h[0:rows, 0:256]), expb(expb.h[:, mask_j, :]), sT(sT.h[:, :, :]), start=False, stop=True)
                        for hf in range(2):
                            P.act(p(p.h[0:rows, hf * 256:(hf + 1) * 256], hf), pss[hf](pss[hf].h[0:rows, 0:256]), AF.Exp)
                        return p

                    def combine(g, br, pbk, first):
                        w = 97 if br == 0 else 65
                        den = V(pbk.h[:, 0:4 * w].rearrange("p (s c) -> p s c", s=4)[:, :, 64], pbk(pbk.h[:, :]).trks)
                        P.ts("vector", rden(rden.h[:, :]), den, 1e-30, None, ALU.max)
                        P.recip(rden(rden.h[:, :]), rden(rden.h[:, :]))
                        for s in range(4):
                            h = hd(g, s)
                            if br == 0:
                                if s == 0:
                                    P.ts("vector", imp(imp.h[:, :]), pbk(pbk.h[:, s * 97 + 65:s * 97 + 97]), rden(rden.h[:, s:s + 1]), None, ALU.mult)
                                else:
                                    P.stt("vector", imp(imp.h[:, :]), pbk(pbk.h[:, s * 97 + 65:s * 97 + 97]), rden(rden.h[:, s:s + 1]),
                                          imp(imp.h[:, :]), ALU.mult, ALU.add)
                            P.tt("vector", coef(coef.h[:, s:s + 1]), rden(rden.h[:, s:s + 1]), gat(gat.h[:, i, 3 * h + br:3 * h + br + 1], i), ALU.mult)
                            if first:
                                P.ts("vector", aacc(aacc.h[:, h, :], h), pbk(pbk.h[:, s * w:s * w + 64]), coef(coef.h[:, s:s + 1]), None, ALU.mult)
                            else:
                                P.stt("vector", aacc(aacc.h[:, h, :], h), pbk(pbk.h[:, s * w:s * w + 64]), coef(coef.h[:, s:s + 1]),
                                      aacc(aacc.h[:, h, :], h), ALU.mult, ALU.add)

                    def branch(g, steps, pbk, vbuf, mask):
                        LA = 2
                        ps_ = [None] * len(steps)
                        for n in range(min(LA, len(steps))):
                            ps_[n] = steps[n][1]()
                        for n, (j, _) in enumerate(steps):
                            if n + LA < len(steps):
                                ps_[n + LA] = steps[n + LA][1]()
                            p = ps_[n]
                            for s in range(4):
                                P.mm(pbk(pbk.h[:, s * 65:(s + 1) * 65]), p(p.h[:, s * 128:(s + 1) * 128], s // 2), vbuf(vbuf.h[:, j, g, :], j),
                                     start=(n == 0 and s == 0), stop=(n == len(steps) - 1 and s == 3))


                    def phase1():
                        for g in range(4):
                            bh, bf_ = bct[cnt["b"] % 4], bcf[cnt["b"] % 4]
                            cnt["b"] += 1
                            src = bass.AP(fd_d.h, 4 * g * 4096 + 2048 + q0 - 16 * 126 - 31, [[16, 128], [4096, 4], [1, 128]])
                            P.dma("sync", bh(bh.h[:, :, :]), fd_d(src))
                            pb = nb()
                            P.mm(pb(pb.h[0:127, :]), j127(j127.h[0:127, 0:127]), bh(bh.h[0:127, :, :]))
                            P.copy("scalar", bf_(bf_.h[0:127, :, :]), pb(pb.h[0:127, :]))
                            p = score(g, lambda hf: kcn(kcn.h[hf * 64:(hf + 1) * 64, g, 0:127], g), 127,
                                      lambda hf: bf_(bf_.h[0:127, 2 * hf:2 * hf + 2, :]), None)
                            po = nacc()
                            for s in range(4):
                                P.mm(po(po.h[:, s * 97:(s + 1) * 97]), p(p.h[0:127, s * 128:(s + 1) * 128], s // 2),
                                     vcx(vcx.h[0:127, g, :], "ov", ("v", g), 0), start=(s == 0), stop=(s == 3))
                            combine(g, 0, po, True)
                            P.tt("vector", scr(scr.h[:, :]), imp(imp.h[:, :]), cst(cst.h[:, o1 + i * 32:o1 + (i + 1) * 32]), ALU.mult)
                            P.tt("vector", scr(scr.h[:, :]), scr(scr.h[:, :]), cst(cst.h[:, o2 + i * 32:o2 + (i + 1) * 32]), ALU.add)
                            P.op("vector", lambda e: e.max(m8.h[:, :], scr.h[:, :]), [scr(scr.h[:, :])], [m8(m8.h[:, :])])
                            P.op("vector", lambda e: e.match_replace(wrk.h[:, :], m8.h[:, :], scr.h[:, :], -3e30),
                                 [scr(scr.h[:, :]), m8(m8.h[:, :])], [wrk(wrk.h[:, :])])
                            P.op("vector", lambda e: e.max(m8.h[:, :], wrk.h[:, :]), [wrk(wrk.h[:, :])], [m8(m8.h[:, :])])
                            P.op("vector", lambda e: e.match_replace(wrk.h[:, :], m8.h[:, :], wrk.h[:, :], -3e30),
                                 [wrk(wrk.h[:, :]), m8(m8.h[:, :])], [wrk(wrk.h[:, :])])
                            seln = selns[g]
                            P.tt("vector", seln(seln.h[:, :]), scr(scr.h[:, :]), wrk(wrk.h[:, :]), ALU.subtract)
                            P.ts("vector", seln(seln.h[:, :]), seln(seln.h[:, :]), 1.0, -1.0, ALU.min, ALU.add)
                            P.stt("vector", seln(seln.h[:, :]), seln(seln.h[:, :]), 30000.0, cst(cst.h[:, o3 + i * 32:o3 + (i + 1) * 32]),
                                  ALU.mult, ALU.add)

                    def phase23():
                        for g in range(4):
                            pt = nb()
                            P.transpose(pt(pt.h[0:32, 0:128]), selns[g](selns[g].h[:, :]), cv("ident"))
                            sT = selTs[g]
                            for s in range(2):
                                P.copy("vector", sT(sT.h[:, s, :]), pt(pt.h[0:32, 0:128]))
                        jl = max(0, i - 4)
                        segs = []
                        for g in range(4):
                            steps = []
                            for j in range(jl, i + 1):
                                dw = 6 if i - j == 4 else i - j
                                steps.append((j, (lambda j=j, dw=dw, g=g: score(
                                    g, lambda hf: kwd(kwd.h[hf * 64:(hf + 1) * 64, g, j * 128:(j + 1) * 128], j // 4), 128,
                                    lambda hf: dall(dall.h[:, dw, 4 * g + 2 * hf:4 * g + 2 * hf + 2, :]), None))))
                            segs.append((g, steps, vwa, 2))
                        for g in range(4):
                            steps = []
                            for j in range(i + 1):
                                dl = min(i - j, 5)
                                steps.append((j, (lambda j=j, dl=dl, g=g: score(
                                    g, lambda hf: ksd(ksd.h[hf * 64:(hf + 1) * 64, g, j * 128:(j + 1) * 128], j // 4), 128,
                                    lambda hf: dall(dall.h[:, dl, 4 * g + 2 * hf:4 * g + 2 * hf + 2, :]), j))))
                            segs.append((g, steps, vsa, 1))
                        flat = [(si, n, j, fn) for si, (g_, steps_, vb_, br_) in enumerate(segs) for n, (j, fn) in enumerate(steps_)]
                        LA = 2
                        ps_ = {}
                        accs = {}
                        for n in range(min(LA, len(flat))):
                            ps_[n] = flat[n][3]()
                        for n, (si, sn, j, fn) in enumerate(flat):
                            if n + LA < len(flat):
                                ps_[n + LA] = flat[n + LA][3]()
                            g_, steps_, vb_, br_ = segs[si]
                            if sn == 0:
                                accs[si] = nacc()
                            pbk = accs[si]
                            p = ps_.pop(n)
                            lastn = (sn == len(steps_) - 1)
                            for s in range(4):
                                P.mm(pbk(pbk.h[:, s * 65:(s + 1) * 65]), p(p.h[:, s * 128:(s + 1) * 128], s // 2), vb_(vb_.h[:, j, g_, :], j),
                                     start=(sn == 0 and s == 0), stop=(lastn and s == 3))
                            if lastn:
                                combine(g_, br_, pbk, False)
                        for c4 in range(2):
                            pt = nb()
                            for cc in range(4):
                                c = c4 * 4 + cc
                                P.transpose(pt(pt.h[:, cc * 128:(cc + 1) * 128]),
                                            aacc(aacc.h[:, 2 * c:2 * c + 2, :].rearrange("p h d -> p (h d)"), 2 * c, 2 * c + 1), cv("ident"))
                            P.copy("scalar", qT(qT.h[:, c4 * 4:(c4 + 1) * 4, q0:q0 + 128], ("a", i), i // 4),
                                   pt(pt.h[:, :].rearrange("p (c t) -> p c t", c=4)))

                    return phase1, phase23

                fns = [tile_fns(i) for i in range(ntiles)]
                if ntiles:
                    fns[0][0]()
                for i in range(ntiles):
                    if i + 1 < ntiles:
                        fns[i + 1][0]()
                    fns[i][1]()


        def stage_d(l, aT, xsrc, xdst, wcols):
            wpn, wps, wo = dr["w_proj_nsa"], dr["w_proj_sgu"], dr["w_out"]
            with ExitStack() as st:
                xt = sbt(st, "d_x", [128, 16, 512], F32)
                ht = sbt(st, "d_h", [128, 16, 512], BF16)
                sq = [sbt(st, "d_sq%d" % i, [128, 512], BF16) for i in range(2)]
                sd = sbt(st, "d_sd", [128, 512], F32)
                uT = sbt(st, "d_uT", [128, 8, 512], BF16)
                sgT = sbt(st, "d_sgT", [128, 8, 512], BF16)
                mrg = sbt(st, "d_mrg", [128, 16, 512], BF16)
                w16 = [sbt(st, "d_w16_%d" % i, [128, 16, 128], BF16) for i in range(4)]
                w8 = [sbt(st, "d_w8_%d" % i, [128, 8, 128], BF16) for i in range(4)]
                wv2 = sbt(st, "d_wv2", [128, 16, 512], BF16)
                vg = sbt(st, "d_vg", [128, 1024], F32)
                vsq = sbt(st, "d_vsq", [128, 1024], F32)
                vln = sbt(st, "d_vln", [128, 1024], BF16)
                stat = sbt(st, "d_stat", [128, 8], F32)
                lng = sbt(st, "d_lng", [128, 1024], F32)
                lnb = sbt(st, "d_lnb", [128, 1024], F32)
                wsT = sbt(st, "d_wsT", [128, 8, 128], BF16)
                wsf = sbt(st, "d_wsf", [128, 8, 128], F32)
                sbb = sbt(st, "d_sbb", [1, 1024], BF16)
                sg1 = sbt(st, "d_sg1", [128, 512], F32)
                sg2 = sbt(st, "d_sg2", [128, 512], F32)
                t1 = sbt(st, "d_t1", [128, 512], F32)
                P.dma("sync", lng(lng.h[:, :]), sgb_d(sgb_d.h.ap()[l, 0]))
                P.dma("sync", lnb(lnb.h[:, :]), sgb_d(sgb_d.h.ap()[l, 1]))
                P.dma("sync", wsf(wsf.h[:, :, :]), sgw_d(sgw_d.h.ap()[l]))
                for g in range(8):
                    P.tt("vector", wsT(wsT.h[:, g, :]), wsf(wsf.h[:, g, :]), cv("triu"), ALU.mult)
                P.dma("gpsimd", sbb(sbb.h[:, :]), sgub_d(sgub_d.h.ap()[l]))
                k16 = 0
                k8 = 0
                for tb in range(4):
                    t0 = tb * 512
                    make_h((xt, 0), (ht, 0), xsrc, l, "nm", t0, 0, sq, sd)
                    for c in range(8):
                        w = w16[k16 % 4]
                        k16 += 1
                        wfetch(tb, w(w.h[:, :, :]), wc16_d, wc16_d.h.ap()[20 + c], 20 + c, [(w(w.h[:, :, :]), wcols(OFF_UV + c * 128, 128))])
                        pz = nb()
                        for k in range(16):
                            P.mm(pz(pz.h[:, :]), w(w.h[:, k, :]), ht(ht.h[:, k, :], ("h", 0)), start=(k == 0), stop=(k == 15))
                        P.act(uT(uT.h[:, c, :], c), pz(pz.h[:, :]), AF.Gelu_apprx_tanh)
                    for sub in range(4):
                        for half in range(2):
                            wfetch(tb if sub == 0 else 1, wv2(wv2.h[:, :, :]), wcv2_d, wcv2_d.h.ap()[half], half,
                                   [(wv2(wv2.h[:, :, :]), wcols(OFF_UV + 1024 + half * 512, 512))])
                            pz = nb()
                            for k in range(16):
                                P.mm(pz(pz.h[:, :]), ht(ht.h[:, k, sub * 128:(sub + 1) * 128], ("h", 0)), wv2(wv2.h[:, k, :]),
                                     start=(k == 0), stop=(k == 15))
                            P.act(vg(vg.h[:, half * 512:(half + 1) * 512], half), pz(pz.h[:, :]), AF.Gelu_apprx_tanh)
                        vga = vg(vg.h[:, :], 0, 1)
                        P.op("vector", lambda e: e.tensor_reduce(stat.h[:, 0:1], vg.h[:, :], AX.X, ALU.add), [vga], [stat(stat.h[:, 0:1], 0)])
                        P.act(vsq(vsq.h[:, :]), vga, AF.Square)
                        P.op("vector", lambda e: e.tensor_reduce(stat.h[:, 1:2], vsq.h[:, :], AX.X, ALU.add), [vsq(vsq.h[:, :])], [stat(stat.h[:, 1:2], 1)])
                        P.ts("vector", stat(stat.h[:, 2:3], 2), stat(stat.h[:, 0:1], 0), 1.0 / 1024, None, ALU.mult)
                        P.tt("vector", stat(stat.h[:, 3:4], 3), stat(stat.h[:, 2:3], 2), stat(stat.h[:, 2:3], 2), ALU.mult)
                        P.stt("vector", stat(stat.h[:, 4:5], 4), stat(stat.h[:, 1:2], 1), 1.0 / 1024, stat(stat.h[:, 3:4], 3), ALU.mult, ALU.subtract)
                        P.act(stat(stat.h[:, 5:6], 5), stat(stat.h[:, 4:5], 4), AF.Sqrt, bias=epsc(epsc.h[:, 0:1]), scale=1.0)
                        P.recip(stat(stat.h[:, 6:7], 6), stat(stat.h[:, 5:6], 5))
                        P.ts("vector", vsq(vsq.h[:, :]), vga, stat(stat.h[:, 2:3], 2), stat(stat.h[:, 6:7], 6), ALU.subtract, ALU.mult)
                        P.tt("vector", vsq(vsq.h[:, :]), vsq(vsq.h[:, :]), lng(lng.h[:, :]), ALU.mult)
                        P.tt("vector", vln(vln.h[:, :]), vsq(vsq.h[:, :]), lnb(lnb.h[:, :]), ALU.add)
                        for gh in range(2):
                            pz = nb()
                            for gg in range(4):
                                g = gh * 4 + gg
                                P.mm(pz(pz.h[:, gg * 128:(gg + 1) * 128]), vln(vln.h[:, g * 128:(g + 1) * 128]), wsT(wsT.h[:, g, :]), start=(gg == 0), stop=False)
                                P.mm(pz(pz.h[:, gg * 128:(gg + 1) * 128]), ones_bf(ones_bf.h[0:1, :]), sbb(sbb.h[0:1, g * 128:(g + 1) * 128]), start=False, stop=(gg == 3))
                            P.tt("vector", sgT(sgT.h[:, gh * 4:(gh + 1) * 4, sub * 128:(sub + 1) * 128], (gh, sub)),
                                 pz(pz.h[:, :].rearrange("p (g t) -> p g t", g=4)),
                                 uT(uT.h[:, gh * 4:(gh + 1) * 4, sub * 128:(sub + 1) * 128], *range(gh * 4, gh * 4 + 4)), ALU.mult)
                    sgk = [(gh, sub) for gh in range(2) for sub in range(4)]
                    for j in range(16):
                        wa, wb_ = w8[k8 % 4], w8[(k8 + 1) % 4]
                        k8 += 2
                        wg1, wg2 = w16[k16 % 4], w16[(k16 + 1) % 4]
                        k16 += 2
                        wfetch(tb, wa(wa.h[:, :, :]), wc8_d, wc8_d.h.ap()[2 * j], 2 * j,
                               [(wa(wa.h[:, :, :]), wpn(wpn.h.ap()[l, :, j * 128:(j + 1) * 128].rearrange("(c p) f -> p c f", p=128)))])
                        wfetch(tb, wb_(wb_.h[:, :, :]), wc8_d, wc8_d.h.ap()[2 * j + 1], 2 * j + 1,
                               [(wb_(wb_.h[:, :, :]), wps(wps.h.ap()[l, :, j * 128:(j + 1) * 128].rearrange("(c p) f -> p c f", p=128)))])
                        wfetch(tb, wg1(wg1.h[:, :, :]), wc16_d, wc16_d.h.ap()[28 + 2 * j], 28 + 2 * j, [(wg1(wg1.h[:, :, :]), wcols(OFF_MG + j * 128, 128))])
                        wfetch(tb, wg2(wg2.h[:, :, :]), wc16_d, wc16_d.h.ap()[29 + 2 * j], 29 + 2 * j, [(wg2(wg2.h[:, :, :]), wcols(OFF_MG + D + j * 128, 128))])
                        pa, pb, pg1, pg2 = nb(), nb(), nb(), nb()
                        for k in range(8):
                            P.mm(pa(pa.h[:, :]), wa(wa.h[:, k, :]), aT(aT.h[:, k, t0:t0 + 512], *[("a", i) for i in range(tb * 4, tb * 4 + 4)]),
                                 start=(k == 0), stop=(k == 7))
                        for k in range(8):
                            P.mm(pb(pb.h[:, :]), wb_(wb_.h[:, k, :]), sgT(sgT.h[:, k, :], *sgk), start=(k == 0), stop=(k == 7))
                        for k in range(16):
                            P.mm(pg1(pg1.h[:, :]), wg1(wg1.h[:, k, :]), ht(ht.h[:, k, :], ("h", 0)), start=(k == 0), stop=(k == 15))
                        for k in range(16):
                            P.mm(pg2(pg2.h[:, :]), wg2(wg2.h[:, k, :]), ht(ht.h[:, k, :], ("h", 0)), start=(k == 0), stop=(k == 15))
                        P.act(sg1(sg1.h[:, :]), pg1(pg1.h[:, :]), AF.Sigmoid)
                        P.act(sg2(sg2.h[:, :]), pg2(pg2.h[:, :]), AF.Sigmoid)
                        P.tt("vector", t1(t1.h[:, :]), sg1(sg1.h[:, :]), pa(pa.h[:, :]), ALU.mult)
                        P.tt("vector", sg2(sg2.h[:, :]), sg2(sg2.h[:, :]), pb(pb.h[:, :]), ALU.mult)
                        P.tt("vector", mrg(mrg.h[:, j, :], j), t1(t1.h[:, :]), sg2(sg2.h[:, :]), ALU.add)
                    for j in range(16):
                        w = w16[k16 % 4]
                        k16 += 1
                        wfetch(tb, w(w.h[:, :, :]), wc16_d, wc16_d.h.ap()[60 + j], 60 + j,
                               [(w(w.h[:, :, :]), wo(wo.h.ap()[l, :, j * 128:(j + 1) * 128].rearrange("(c p) f -> p c f", p=128)))])
                        pz = nb()
                        for k in range(16):
                            P.mm(pz(pz.h[:, :]), w(w.h[:, k, :]), mrg(mrg.h[:, k, :], *range(16)), start=(k == 0), stop=(k == 15))
                        xv = xt(xt.h[:, j, :], ("x", 0))
                        P.tt("vector", xv, xv, pz(pz.h[:, :]), ALU.add)
                    P.dma("sync", xdst(xdst.h.ap()[:, :, t0:t0 + 512].rearrange("c p t -> p c t"), tb), xt(xt.h[:, :, :], ("x", 0)))

        cur = x_in
        for l in range(depth):
            last = (l == depth - 1)
            todo = [p for p in "1m2" if p in phases]
            for p in todo:
                dst = out_d if (last and p == todo[-1]) else xs_d
                if p == "1":
                    ffn(l, "ffn1", cur, dst)
                elif p == "m":
                    mixer(l, cur, dst)
                else:
                    ffn(l, "ffn2", cur, dst)
                cur = xs_d
        P.barrier()
        P.emit(top)
    return nc


_NC_CACHE = {}


def _prep_shared(inputs):
    f = lambda a: np.ascontiguousarray(np.asarray(a, np.float32))
    sh = {}
    for k in ("ffn1_w_gate", "ffn1_w_up", "ffn1_w_down", "ffn2_w_gate", "ffn2_w_up", "ffn2_w_down", "w_in",
              "cmp_k_w1", "cmp_k_w2", "cmp_v_w1", "cmp_v_w2", "w_proj_nsa", "w_proj_sgu", "w_out"):
        sh[k] = f(inputs[k])
    sh["cst"] = CST_ARR
    prm = np.zeros((L, 128, NPRM), np.float32)
    for l in range(L):
        for nm, key in (("n1", "ffn1_norm"), ("nm", "mix_norm"), ("n2", "ffn2_norm")):
            prm[l, :, PRM[nm]:PRM[nm] + 16] = f(inputs[key])[l].reshape(16, 128).T
        prm[l, :, PRM["qn"]] = np.tile(f(inputs["q_norm"])[l], 2)
        for i in range(3):
            prm[l, :, PRM["kn"] + i] = np.tile(f(inputs["k_norm"])[l, i], 2)
        prm[l, :, PRM["pk"]:PRM["pk"] + 32] = np.tile(f(inputs["cmp_pos_k"])[l].T, (2, 1))
        prm[l, :, PRM["pv"]:PRM["pv"] + 32] = np.tile(f(inputs["cmp_pos_v"])[l].T, (2, 1))
    sh["prm"] = prm
    rb = f(inputs["rel_bias"])
    rbp = np.zeros((32, 16), np.float32)
    for g in range(4):
        for s in range(4):
            rbp[:, 4 * g + s] = rb[:, 4 * g + 2 * (s % 2) + s // 2]
    sh["rbp"] = rbp
    sh["oh"] = OH_ARR
    sh["expm"] = EXPM_ARR
    sh["sgub"] = np.ascontiguousarray(f(inputs["sgu_b"]).reshape(L, 1, 1024))
    sgb = np.zeros((L, 2, 128, 1024), np.float32)
    sgb[:, 0] = f(inputs["sgu_norm_g"])[:, None, :]
    sgb[:, 1] = f(inputs["sgu_norm_b"])[:, None, :]
    sh["sgb"] = sgb
    sh["sgwT"] = np.ascontiguousarray(f(inputs["sgu_w"]).transpose(0, 3, 1, 2))
    return sh


def kernel(**inputs):
    x = np.asarray(inputs["x"], np.float32)
    sh = _prep_shared(inputs)
    if "nc" not in _NC_CACHE:
        _NC_CACHE["nc"] = build_program()
    nc = _NC_CACHE["nc"]
    in_maps = []
    for b in range(NCORES):
        m = dict(sh)
        m["x_fm"] = np.ascontiguousarray(x[b].T.reshape(16, 128, S))
        in_maps.append(m)
    res = run_bass_kernel_spmd(nc, in_maps, core_ids=list(range(NCORES)))
    out = np.empty((NCORES, S, D), np.float32)
    for b in range(NCORES):
        out[b] = res.results[b]["out_fm"].reshape(D, S).T
    return out
```

```python
import math
from contextlib import ExitStack
import numpy as np
import concourse.bass as bass
import concourse.mybir as mybir
from concourse.bass_utils import run_bass_kernel_spmd

F32 = mybir.dt.float32
BF16 = mybir.dt.bfloat16
ALU = mybir.AluOpType
AF = mybir.ActivationFunctionType
AX = mybir.AxisListType

D = 2048
S = 2048
L = 2
DFF = 5504
NFC = 43
INW = 8752
OFF_KV, OFF_NG, OFF_UV, OFF_MG = 1024, 2560, 2608, 4656
EPS = 1e-6
NEGB = -30000.0
NCORES = 4

ENGS = ["sync", "gpsimd", "scalar", "vector", "tensor"]
NDSEM = 24


class Trk:
    __slots__ = ("w", "r", "x")

    def __init__(self, x=False):
        self.w = None
        self.r = {}
        self.x = x


class V:
    __slots__ = ("ap", "trks")

    def __init__(self, ap, trks):
        self.ap = ap
        self.trks = trks


class Buf:
    def __init__(self, h, excl=False):
        self.h = h
        self.t = {}
        self.excl = excl

    def __call__(self, ap, *keys):
        if not keys:
            keys = (0,)
        out = []
        for k in keys:
            t = self.t.get(k)
            if t is None:
                t = self.t[k] = Trk(self.excl)
            out.append(t)
        return V(ap, out)


class Prog:
    def __init__(self, nc):
        self.nc = nc
        self.ops = {e: [] for e in ENGS}
        self.seen = {e: {} for e in ENGS}
        self.seen_dma = {e: set() for e in ENGS}
        self.ndma = {e: 0 for e in ENGS}

    def _dep(self, eng, waits, d):
        if d[0] == "dma":
            if d in self.seen_dma[eng]:
                return
            self.seen_dma[eng].add(d)
            waits.append(d)
        else:
            e, i = d
            if e == eng and eng in ("tensor", "sync"):
                return
            if self.seen[eng].get(e, -1) >= i:
                return
            self.seen[eng][e] = i
            self.ops[e][i]["inc"] = True
            waits.append(d)

    def op(self, eng, fn, reads=(), writes=(), dma=False):
        idx = len(self.ops[eng])
        waits = []
        deps = []
        rd, wr = [], []
        for v in reads:
            for t in v.trks:
                (wr if t.x else rd).append(t)
        for v in writes:
            wr.extend(v.trks)
        for t in rd:
            if t.w is not None:
                deps.append(t.w)
        for t in wr:
            if t.w is not None:
                deps.append(t.w)
            deps.extend(t.r.values())
        if dma:
            n = self.ndma[eng]
            self.ndma[eng] += 1
            ev = ("dma", eng, n)
            if n >= NDSEM:
                deps.append(("dma", eng, n - NDSEM))
        else:
            ev = (eng, idx)
        for d in deps:
            self._dep(eng, waits, d)
        self.ops[eng].append({"fn": fn, "waits": waits, "inc": False, "dma": ev if dma else None})
        for t in rd:
            t.r[ev if dma else eng] = ev
        for t in wr:
            t.w = ev
            t.r = {}
        return ev

    def barrier(self):
        evs = []
        for e in ENGS:
            if self.ops[e]:
                last = len(self.ops[e]) - 1
                while last >= 0 and self.ops[e][last]["fn"] is None:
                    last -= 1
                if last >= 0 and self.ops[e][last]["dma"] is None:
                    evs.append((e, last))
            n = self.ndma[e]
            for k in range(max(0, n - NDSEM), n):
                evs.append(("dma", e, k))
        for e in ENGS:
            waits = []
            for d in evs:
                if d[0] != "dma" and d[0] == e:
                    continue
                self._dep(e, waits, d)
            self.ops[e].append({"fn": None, "waits": waits, "inc": False, "dma": None})

    def emit(self, stack):
        nc = self.nc
        esem = {e: stack.enter_context(nc.semaphore("es_" + e)) for e in ENGS}
        dsem = {e: [stack.enter_context(nc.semaphore("ds_%s_%d" % (e, i))) for i in range(NDSEM)]
                for e in ENGS if self.ndma[e] > 0}
        for e in ENGS:
            c = 0
            for o in self.ops[e]:
                if o["inc"]:
                    c += 1
                o["cnt"] = c
        block = stack.enter_context(nc.Block())
        ops = self.ops

        def run(engobj, e):
            for o in ops[e]:
                for d in o["waits"]:
                    if d[0] == "dma":
                        _, q, n = d
                        engobj.wait_ge(dsem[q][n % NDSEM], 16 * (n // NDSEM + 1))
                    else:
                        engobj.wait_ge(esem[d[0]], ops[d[0]][d[1]]["cnt"])
                if o["fn"] is None:
                    continue
                ins = o["fn"](engobj)
                if o["dma"] is not None:
                    _, q, n = o["dma"]
                    ins.then_inc(dsem[q][n % NDSEM], 16)
                elif o["inc"]:
                    ins.then_inc(esem[e], 1)

        @block.sync
        def _(x):
            run(x, "sync")

        @block.gpsimd
        def _(x):
            run(x, "gpsimd")

        @block.scalar
        def _(x):
            run(x, "scalar")

        @block.vector
        def _(x):
            run(x, "vector")

        @block.tensor
        def _(x):
            run(x, "tensor")

    def dma(self, q, out, in_):
        return self.op(q, lambda e: e.dma_start(out=out.ap, in_=in_.ap), [in_], [out], dma=True)

    def mm(self, out, lhsT, rhs, start=True, stop=True):
        return self.op("tensor", lambda e: e.matmul(out.ap, lhsT.ap, rhs.ap, start=start, stop=stop),
                       [lhsT, rhs], [out])

    def transpose(self, out, in_, ident):
        return self.op("tensor", lambda e: e.transpose(out.ap, in_.ap, ident.ap), [in_, ident], [out])

    def act(self, out, in_, func, bias=None, scale=None):
        reads = [in_]
        kw = {}
        if bias is not None:
            reads.append(bias)
            kw["bias"] = bias.ap
        if scale is not None:
            kw["scale"] = scale
        return self.op("scalar", lambda e: e.activation(out.ap, in_.ap, func, **kw), reads, [out])

    def tt(self, eng, out, in0, in1, op):
        return self.op(eng, lambda e: e.tensor_tensor(out.ap, in0.ap, in1.ap, op), [in0, in1], [out])

    def ts(self, eng, out, in0, s1, s2, op0, op1=None):
        reads = [in0]
        a1, a2 = s1, s2
        if isinstance(s1, V):
            reads.append(s1)
            a1 = s1.ap
        if isinstance(s2, V):
            reads.append(s2)
            a2 = s2.ap
        kw = {}
        if op1 is not None:
            kw["op1"] = op1
        return self.op(eng, lambda e: e.tensor_scalar(out.ap, in0.ap, a1, a2, op0, **kw), reads, [out])

    def stt(self, eng, out, in0, scalar, in1, op0, op1):
        reads = [in0, in1]
        a = scalar
        if isinstance(scalar, V):
            reads.append(scalar)
            a = scalar.ap
        return self.op(eng, lambda e: e.scalar_tensor_tensor(out.ap, in0.ap, a, in1.ap, op0, op1), reads, [out])

    def copy(self, eng, out, in_):
        if eng == "scalar":
            return self.op(eng, lambda e: e.copy(out.ap, in_.ap), [in_], [out])
        return self.op(eng, lambda e: e.tensor_copy(out.ap, in_.ap), [in_], [out])

    def memset(self, eng, out, val):
        return self.op(eng, lambda e: e.memset(out.ap, val), [], [out])

    def recip(self, out, in_):
        return self.op("vector", lambda e: e.reciprocal(out.ap, in_.ap), [in_], [out])


def _t5_bucket(dist):
    n = np.maximum(dist, 0)
    nf = np.maximum(n, 1).astype(np.float32)
    large = 16 + (np.log(nf / np.float32(16)) / np.float32(math.log(128 / 16)) * np.float32(16)).astype(np.int32)
    large = np.minimum(large, 31)
    return np.where(n < 16, n, large)


CST = {}


def _build_consts():
    cols = []
    off = [0]

    def add(name, arr):
        arr = np.asarray(arr, np.float32)
        a = np.zeros((128, arr.shape[1]), np.float32)
        a[:arr.shape[0]] = arr
        CST[name] = (off[0], arr.shape[1])
        off[0] += arr.shape[1]
        cols.append(a)

    idx = np.arange(4096)
    dist = idx - 2048
    oh = np.zeros((33, 4096), np.float32)
    bk = _t5_bucket(dist)
    for i in range(4096):
        if dist[i] < 0:
            oh[32, i] = 1
        else:
            oh[bk[i], i] = 1
    CST["_oh"] = oh
    add("ident", np.eye(128))
    add("j128", np.eye(128)[::-1])
    j127 = np.zeros((128, 128))
    j127[:127, :127] = np.eye(127)[::-1]
    add("j127", j127)
    s_ = np.arange(128)[:, None]
    t_ = np.arange(128)[None, :]
    add("triu", (s_ <= t_))
    expm = np.zeros((32, 16, 128))
    for jc in range(16):
        for k in range(128):
            expm[2 * jc + k // 64, jc, k] = 1
    CST["_expm"] = expm.reshape(32, -1).astype(np.float32)
    ci = np.arange(127)[:, None] * 16
    sj = np.arange(32)[None, :] * 64
    ov = np.clip(np.minimum(ci + 32, sj + 64) - np.maximum(ci, sj), 0, None).astype(np.float32) / 32
    add("ov", ov)
    notf = np.zeros((128, 16, 32))
    addv = np.zeros((128, 16, 32))
    vneg = np.zeros((128, 16, 32))
    for i in range(16):
        for p in range(128):
            cur = (128 * i + p) // 64
            for j in range(32):
                if j > cur:
                    addv[p, i, j] = -1e30
                    vneg[p, i, j] = NEGB
                elif j == 0:
                    addv[p, i, j] = 1e4
                elif j == cur:
                    addv[p, i, j] = 2e4
                elif j == cur - 1:
                    addv[p, i, j] = 3e4
                else:
                    notf[p, i, j] = 1
    add("notf", notf.reshape(128, -1))
    add("addv", addv.reshape(128, -1))
    add("vneg", vneg.reshape(128, -1))
    m6 = (t_ < s_).astype(np.float32)
    add("m6", m6)
    add("n6", (m6 - 1) * 30000.0)
    return np.concatenate(cols, axis=1)


CST_ARR = _build_consts()
NCST = CST_ARR.shape[1]
PRM = {"n1": 0, "nm": 16, "n2": 32, "qn": 48, "kn": 49, "pk": 52, "pv": 84}
NPRM = 116
OH_ARR = CST["_oh"]
EXPM_ARR = CST["_expm"]


def build_program(depth=L, dbg=None, phases="1m2", mstop=None, ntiles=16, tstop=8):
    nc = bass.Bass("TRN2", target_bir_lowering=False)
    dr = {}

    def din(name, shape, dt=F32):
        dr[name] = Buf(nc.dram_tensor(name, list(shape), dt, kind="ExternalInput"))
        return dr[name]

    x_in = din("x_fm", [16, 128, S])
    cst_d = din("cst", [128, NCST])
    prm_d = din("prm", [L, 128, NPRM])
    rb_d = din("rbp", [32, 16])
    oh_d = din("oh", [33, 4096])
    expm_d = din("expm", [32, 2048])
    sgub_d = din("sgub", [L, 1, 1024])
    sgb_d = din("sgb", [L, 2, 128, 1024])
    sgw_d = din("sgwT", [L, 128, 8, 128])
    wnames = {"ffn1_w_gate": [L, D, DFF], "ffn1_w_up": [L, D, DFF], "ffn1_w_down": [L, DFF, D],
              "ffn2_w_gate": [L, D, DFF], "ffn2_w_up": [L, D, DFF], "ffn2_w_down": [L, DFF, D],
              "w_in": [L, D, INW], "cmp_k_w1": [L, 2048, 256], "cmp_k_w2": [L, 256, 64],
              "cmp_v_w1": [L, 2048, 256], "cmp_v_w2": [L, 256, 64],
              "w_proj_nsa": [L, 1024, D], "w_proj_sgu": [L, 1024, D], "w_out": [L, D, D]}
    for k, shp in wnames.items():
        din(k, shp)
    out_d = Buf(nc.dram_tensor("out_fm", [16, 128, S], F32, kind="ExternalOutput"))
    xs_d = Buf(nc.dram_tensor("xs", [16, 128, S], F32))
    fd_d = Buf(nc.dram_tensor("fdt", [16, 4096], BF16))
    wc16_d = Buf(nc.dram_tensor("wc16", [76, 128, 16, 128], BF16))
    wc8_d = Buf(nc.dram_tensor("wc8", [32, 128, 8, 128], BF16))
    wcv_d = Buf(nc.dram_tensor("wcv", [128, 16, 560], BF16))
    wcv2_d = Buf(nc.dram_tensor("wcv2", [2, 128, 16, 512], BF16))
    dbg_d = {}
    if dbg:
        for name, shp in dbg.items():
            dbg_d[name] = Buf(nc.dram_tensor("dbg_" + name, list(shp), F32, kind="ExternalOutput"))

    P = Prog(nc)
    _NC_CACHE['P'] = P
    out_events = []

    with ExitStack() as top:
        top.enter_context(nc.allow_low_precision("bf16 matmul operands, fp32 accumulation"))

        uniq = [0]

        def sbt(st, name, shape, dt):
            uniq[0] += 1
            return Buf(st.enter_context(nc.sbuf_tensor("%s_%d" % (name, uniq[0]), list(shape), dt)))

        banks = [Buf(top.enter_context(nc.psum_tensor("pb%d" % i, [128, 512], F32)), excl=True) for i in range(8)]
        bank_i = [0]
        bank_n = [8]

        def nb():
            b = banks[bank_i[0] % bank_n[0]]
            bank_i[0] += 1
            return b

        cst = sbt(top, "cst_sb", [128, NCST], F32)
        P.dma("sync", cst(cst.h[:, :]), cst_d(cst_d.h.ap()))

        def cv(name, rows=128, lo=0, n=None):
            o, w = CST[name]
            if n is None:
                n = w - lo
            return cst(cst.h[0:rows, o + lo:o + lo + n])

        prm = sbt(top, "prm_sb", [128, L, NPRM], F32)
        P.dma("sync", prm(prm.h[:, :, :]), prm_d(prm_d.h.ap().rearrange("l p n -> p l n")))
        ones_bf = sbt(top, "ones_bf", [128, 128], BF16)
        P.memset("vector", ones_bf(ones_bf.h[:, :]), 1.0)
        blk_bf = sbt(top, "blk_bf", [128, 128], BF16)
        P.memset("vector", blk_bf(blk_bf.h[:, :]), 0.0)
        P.memset("vector", blk_bf(blk_bf.h[0:64, 0:64]), 1.0)
        P.memset("vector", blk_bf(blk_bf.h[64:128, 64:128]), 1.0)
        ident_bf = sbt(top, "ident_bf", [128, 128], BF16)
        P.copy("vector", ident_bf(ident_bf.h[:, :]), cv("ident"))
        epsc = sbt(top, "epsc", [128, 2], F32)
        P.memset("vector", epsc(epsc.h[:, 0:1]), EPS)
        P.memset("vector", epsc(epsc.h[:, 1:2]), 0.0)
        gsc = sbt(top, "gsc", [128, L, 52], F32)
        for l in range(L):
            P.copy("vector", gsc(gsc.h[:, l, 0:49]), prm(prm.h[:, l, 0:49]))
            P.ts("vector", gsc(gsc.h[:, l, 49:50]), prm(prm.h[:, l, 48:49]), 0.125, None, ALU.mult)

        def pcol(l, name, c=0):
            return gsc(gsc.h[:, l, PRM[name] + c:PRM[name] + c + 1])

        def make_h(st_x, st_h, xsrc, l, nname, tok0, hcol0, sq, sd):
            xt, xo = st_x
            ht, ho = st_h
            P.dma("sync", xt(xt.h[:, :, xo:xo + 512], ("x", xo)),
                  xsrc(xsrc.h.ap()[:, :, tok0:tok0 + 512].rearrange("c p t -> p c t"), tok0 // 512))
            psn = nb()
            for c in range(16):
                s = sq[c % 2]
                P.act(s(s.h[:, :]), xt(xt.h[:, c, xo:xo + 512], ("x", xo)), AF.Square)
                P.mm(psn(psn.h[:, :]), ones_bf(ones_bf.h[:, :]), s(s.h[:, :]), start=(c == 0), stop=(c == 15))
            P.act(sd(sd.h[:, :]), psn(psn.h[:, :]), AF.Sqrt, bias=epsc(epsc.h[:, 0:1]), scale=1.0 / D)
            P.recip(sd(sd.h[:, :]), sd(sd.h[:, :]))
            for c in range(16):
                P.stt("vector", ht(ht.h[:, c, ho:ho + 512], ("h", ho)), xt(xt.h[:, c, xo:xo + 512], ("x", xo)),
                      pcol(l, nname, c), sd(sd.h[:, :]), ALU.mult, ALU.mult)

        def wfetch(tb, wt_view, cbuf, cap, ckey, loaders):
            if tb == 0:
                for d_, s_ in loaders:
                    P.dma("gpsimd", d_, s_)
                P.dma("sync", cbuf(cap, ckey), wt_view)
            else:
                P.dma("sync", wt_view, cbuf(cap, ckey))

        def wload(wt, src_ap):
            P.dma("gpsimd", wt(wt.h[:, :, :]) if len(wt.h.shape) == 3 else wt(wt.h[:, :]), src_ap)

        def ffn(l, which, xsrc, xdst):
            wg_d, wu_d, wd_d = dr[which + "_w_gate"], dr[which + "_w_up"], dr[which + "_w_down"]
            nname = "n1" if which == "ffn1" else "n2"
            with ExitStack() as st:
                xt = sbt(st, "f_x", [128, 16, 1024], F32)
                ht = sbt(st, "f_h", [128, 16, 1024], BF16)
                act = sbt(st, "f_act", [128, 22, 1024], BF16)
                sq = [sbt(st, "f_sq%d" % i, [128, 512], BF16) for i in range(2)]
                sd = sbt(st, "f_sd", [128, 512], F32)
                wg = [sbt(st, "f_wg%d" % i, [128, 16, 128], BF16) for i in range(3)]
                wu = [sbt(st, "f_wu%d" % i, [128, 16, 128], BF16) for i in range(3)]
                wd = [sbt(st, "f_wd%d" % i, [128, 22, 256], BF16) for i in range(2)]
                sl = [sbt(st, "f_sl%d" % i, [128, 512], BF16) for i in range(4)]
                wi = 0
                di = 0
                for tt in range(2):
                    for tb in range(2):
                        make_h((xt, tb * 512), (ht, tb * 512), xsrc, l, nname, tt * 1024 + tb * 512, 0, sq, sd)
                    for fh in range(2):
                        fcs = list(range(0, 22)) if fh == 0 else list(range(22, 43))
                        for fi, fc in enumerate(fcs):
                            g, u = wg[wi % 3], wu[wi % 3]
                            wi += 1
                            P.dma("gpsimd", g(g.h[:, :, :]),
                                  wg_d(wg_d.h.ap()[l, :, fc * 128:(fc + 1) * 128].rearrange("(c p) f -> p c f", p=128)))
                            P.dma("gpsimd", u(u.h[:, :, :]),
                                  wu_d(wu_d.h.ap()[l, :, fc * 128:(fc + 1) * 128].rearrange("(c p) f -> p c f", p=128)))
                            for tb in range(2):
                                pg, pu = nb(), nb()
                                for c in range(16):
                                    P.mm(pg(pg.h[:, :]), g(g.h[:, c, :]), ht(ht.h[:, c, tb * 512:(tb + 1) * 512], ("h", tb * 512)),
                                         start=(c == 0), stop=(c == 15))
                                for c in range(16):
                                    P.mm(pu(pu.h[:, :]), u(u.h[:, c, :]), ht(ht.h[:, c, tb * 512:(tb + 1) * 512], ("h", tb * 512)),
                                         start=(c == 0), stop=(c == 15))
                                s = sl[(fi * 2 + tb) % 4]
                                P.act(s(s.h[:, :]), pg(pg.h[:, :]), AF.Silu)
                                P.tt("vector", act(act.h[:, fi, tb * 512:(tb + 1) * 512], (fi, tb)), s(s.h[:, :]), pu(pu.h[:, :]), ALU.mult)
                        nf = len(fcs)
                        for dcp in range(8):
                            w = wd[di % 2]
                            di += 1
                            P.dma("gpsimd", w(w.h[:, 0:nf, :]),
                                  wd_d(wd_d.h.ap()[l, fcs[0] * 128:(fcs[-1] + 1) * 128, dcp * 256:(dcp + 1) * 256]
                                       .rearrange("(f p) d -> p f d", p=128)))
                            for ds in range(2):
                                dc = dcp * 2 + ds
                                for tb in range(2):
                                    pd = nb()
                                    for fi in range(nf):
                                        P.mm(pd(pd.h[:, :]), w(w.h[:, fi, ds * 128:(ds + 1) * 128]),
                                             act(act.h[:, fi, tb * 512:(tb + 1) * 512], (fi, tb)), start=(fi == 0), stop=(fi == nf - 1))
                                    xv = xt(xt.h[:, dc, tb * 512:(tb + 1) * 512], ("x", tb * 512))
                                    P.stt("vector", xv, pd(pd.h[:, :]), 0.5, xv, ALU.mult, ALU.add)
                    ev = P.dma("sync", xdst(xdst.h.ap()[:, :, tt * 1024:(tt + 1) * 1024].rearrange("c p t -> p c t"), 2 * tt, 2 * tt + 1),
                               xt(xt.h[:, :, :], ("x", 0), ("x", 512)))
                    if xdst is out_d:
                        out_events.append(ev)
            P.barrier()

        def build_tables(st):
            dall = sbt(st, "dall", [128, 7, 16, 128], BF16)
            with ExitStack() as s2:
                rbx = sbt(s2, "rbx", [33, 16], F32)
                rbb = sbt(s2, "rbb", [33, 16], BF16)
                ohb = sbt(s2, "ohb", [33, 4096], BF16)
                fsb = sbt(s2, "fsb", [16, 4096], BF16)
                hk = sbt(s2, "hk", [128, 16, 128], BF16)
                P.memset("vector", rbx(rbx.h[:, :]), NEGB)
                P.dma("sync", rbx(rbx.h[0:32, :]), rb_d(rb_d.h.ap()))
                P.copy("vector", rbb(rbb.h[:, :]), rbx(rbx.h[:, :]))
                P.dma("gpsimd", ohb(ohb.h[:, :]), oh_d(oh_d.h.ap()))
                for n in range(8):
                    pb = nb()
                    P.mm(pb(pb.h[0:16, :]), rbb(rbb.h[:, :]), ohb(ohb.h[:, n * 512:(n + 1) * 512]))
                    P.copy("vector", fsb(fsb.h[:, n * 512:(n + 1) * 512]), pb(pb.h[0:16, :]))
                P.dma("sync", fd_d(fd_d.h.ap()), fsb(fsb.h[:, :]))
                j128 = sbt(s2, "j128b", [128, 128], BF16)
                P.copy("vector", j128(j128.h[:, :]), cv("j128"))
                for dl in range(6):
                    src = bass.AP(fd_d.h, 2048 + dl * 128 - 127, [[1, 128], [4096, 16], [1, 128]])
                    P.dma("sync", hk(hk.h[:, :, :]), fd_d(src))
                    for n in range(4):
                        pb = nb()
                        P.mm(pb(pb.h[:, :]), j128(j128.h[:, :]), hk(hk.h[:, n * 4:(n + 1) * 4, :]))
                        P.copy("vector", dall(dall.h[:, dl, n * 4:(n + 1) * 4, :]), pb(pb.h[:, :]))
                for h in range(16):
                    P.tt("vector", dall(dall.h[:, 6, h, :]), dall(dall.h[:, 5, h, :]), cv("m6"), ALU.mult)
                    P.tt("vector", dall(dall.h[:, 6, h, :]), dall(dall.h[:, 6, h, :]), cv("n6"), ALU.add)
            P.barrier()
            return dall

        def mixer(l, xsrc, xdst):
            w_in = dr["w_in"]

            def wcols(c0, n):
                return w_in(w_in.h.ap()[l, :, c0:c0 + n].rearrange("(c p) f -> p c f", p=128))

            with ExitStack() as sq_:
                qT = sbt(sq_, "m_qT", [128, 8, S], BF16)
                with ExitStack() as st:
                    ksd = sbt(st, "m_ksd", [128, 4, S], BF16)
                    kwd = sbt(st, "m_kwd", [128, 4, S], BF16)
                    kcT = sbt(st, "m_kcT", [128, 2, S], BF16)
                    vcT = sbt(st, "m_vcT", [128, 2, S], BF16)
                    vsa = sbt(st, "m_vsa", [128, 16, 4, 65], BF16)
                    vwa = sbt(st, "m_vwa", [128, 16, 4, 65], BF16)
                    gat = sbt(st, "m_gat", [128, 16, 48], F32)
                    P.memset("vector", vsa(vsa.h[:, :, :, 64:65]), 1.0)
                    P.memset("vector", vwa(vwa.h[:, :, :, 64:65]), 1.0)
                    with ExitStack() as sa:
                        xt = sbt(sa, "a_x", [128, 16, 512], F32)
                        ht = sbt(sa, "a_h", [128, 16, 512], BF16)
                        sq = [sbt(sa, "a_sq%d" % i, [128, 512], BF16) for i in range(2)]
                        sd = sbt(sa, "a_sd", [128, 512], F32)
                        rs = sbt(sa, "a_rs", [128, 512], F32)
                        wq = [sbt(sa, "a_wq%d" % i, [128, 16, 128], BF16) for i in range(3)]
                        wv = sbt(sa, "a_wv", [128, 16, 560], BF16)
                        wi = 0
                        for tb in range(4):
                            t0 = tb * 512
                            make_h((xt, 0), (ht, 0), xsrc, l, "nm", t0, 0, sq, sd)
                            hv = lambda c: ht(ht.h[:, c, :], ("h", 0))
                            jobs = [("q", c) for c in range(8)] + [("ks", g) for g in range(4)] + \
                                   [("kw", g) for g in range(4)] + [("kc", c) for c in range(2)] + [("vc", c) for c in range(2)]
                            for jn, (kind, ix) in enumerate(jobs):
                                w = wq[wi % 3]
                                wi += 1
                                wall = w(w.h[:, :, :])
                                if kind == "q":
                                    lds = [(wall, wcols(ix * 128, 128))]
                                elif kind in ("ks", "kw"):
                                    c0 = OFF_KV + (2 if kind == "ks" else 4) * 256 + ix * 64
                                    lds = [(w(w.h[:, :, 0:64]), wcols(c0, 64)), (w(w.h[:, :, 64:128]), wcols(c0, 64))]
                                else:
                                    c0 = OFF_KV + (0 if kind == "kc" else 1) * 256 + ix * 128
                                    lds = [(wall, wcols(c0, 128))]
                                wfetch(tb, wall, wc16_d, wc16_d.h.ap()[jn], jn, lds)
                                wvw = wall
                                pz = nb()
                                for c in range(16):
                                    P.mm(pz(pz.h[:, :]), V(w.h[:, c, :], wvw.trks), hv(c), start=(c == 0), stop=(c == 15))
                                if kind in ("kc", "vc"):
                                    dst = kcT if kind == "kc" else vcT
                                    P.copy("scalar", dst(dst.h[:, ix, t0:t0 + 512], tb), pz(pz.h[:, :]))
                                    continue
                                s = sq[wi % 2]
                                P.act(s(s.h[:, :]), pz(pz.h[:, :]), AF.Square)
                                pm = nb()
                                P.mm(pm(pm.h[:, :]), blk_bf(blk_bf.h[:, :]), s(s.h[:, :]))
                                P.act(rs(rs.h[:, :]), pm(pm.h[:, :]), AF.Sqrt, bias=epsc(epsc.h[:, 0:1]), scale=1.0 / 64)
                                P.recip(rs(rs.h[:, :]), rs(rs.h[:, :]))
                                if kind == "q":
                                    dstv = qT(qT.h[:, ix, t0:t0 + 512], tb)
                                    gc = gsc(gsc.h[:, l, 49:50])
                                    P.stt("vector", dstv, pz(pz.h[:, :]), gc, rs(rs.h[:, :]), ALU.mult, ALU.mult)
                                else:
                                    dst = ksd if kind == "ks" else kwd
                                    kn = 1 if kind == "ks" else 2
                                    gc = prm(prm.h[:, l, PRM["kn"] + kn:PRM["kn"] + kn + 1])
                                    P.stt("vector", dst(dst.h[:, ix, t0:t0 + 512], tb), pz(pz.h[:, :]), gc, rs(rs.h[:, :]), ALU.mult, ALU.mult)
                            wfetch(tb, wv(wv.h[:, :, :]), wcv_d, wcv_d.h.ap(), 0,
                                   [(wv(wv.h[:, :, 0:256]), wcols(OFF_KV + 3 * 256, 256)),
                                    (wv(wv.h[:, :, 256:512]), wcols(OFF_KV + 5 * 256, 256)),
                                    (wv(wv.h[:, :, 512:560]), wcols(OFF_NG, 48))])
                            wvt = wv(wv.h[:, :, :]).trks
                            for sub in range(4):
                                ti = tb * 4 + sub
                                p1, p2 = nb(), nb()
                                for c in range(16):
                                    P.mm(p1(p1.h[:, :]), ht(ht.h[:, c, sub * 128:(sub + 1) * 128], ("h", 0)), V(wv.h[:, c, 0:512], wvt),
                                         start=(c == 0), stop=(c == 15))
                                for c in range(16):
                                    P.mm(p2(p2.h[:, 0:48]), ht(ht.h[:, c, sub * 128:(sub + 1) * 128], ("h", 0)), V(wv.h[:, c, 512:560], wvt),
                                         start=(c == 0), stop=(c == 15))
                                P.copy("vector", vsa(vsa.h[:, ti, :, 0:64], ti), p1(p1.h[:, 0:256].rearrange("p (g d) -> p g d", g=4)))
                                P.copy("vector", vwa(vwa.h[:, ti, :, 0:64], ti), p1(p1.h[:, 256:512].rearrange("p (g d) -> p g d", g=4)))
                                P.act(gat(gat.h[:, ti, :], ti), p2(p2.h[:, 0:48]), AF.Sigmoid)
                    P.barrier()
                    if mstop != "A":
                        dall = build_tables(st)
                    if mstop in ("A", "T"):
                        with ExitStack() as sx:
                            xc = sbt(sx, "passx", [128, 16, 512], F32)
                            for tb in range(4):
                                P.dma("sync", xc(xc.h[:, :, :]), xsrc(xsrc.h.ap()[:, :, tb * 512:(tb + 1) * 512].rearrange("c p t -> p c t"), tb))
                                P.dma("sync", xdst(xdst.h.ap()[:, :, tb * 512:(tb + 1) * 512].rearrange("c p t -> p c t"), tb), xc(xc.h[:, :, :]))
                            P.barrier()
                        return
                    attention(l, st, qT, ksd, kwd, kcT, vcT, vsa, vwa, gat, dall)
                    bank_n[0] = 8
                P.barrier()
                if mstop == "X":
                    with ExitStack() as sx:
                        xc = sbt(sx, "passx2", [128, 16, 512], F32)
                        for tb in range(4):
                            P.dma("sync", xc(xc.h[:, :, :]), xsrc(xsrc.h.ap()[:, :, tb * 512:(tb + 1) * 512].rearrange("c p t -> p c t"), tb))
                            P.dma("sync", xdst(xdst.h.ap()[:, :, tb * 512:(tb + 1) * 512].rearrange("c p t -> p c t"), tb), xc(xc.h[:, :, :]))
                        P.barrier()
                    return
                stage_d(l, qT, xsrc, xdst, wcols)
            P.barrier()

        def dump(name, view_fn_list):
            pass

        def dump_attn_inputs(qT, ksd, kwd, kcT, vcT, vsa, vwa, gat, dall):
            with ExitStack() as sd_:
                tmp = sbt(sd_, "dbg_tmp", [128, 8 * 512], F32)
                if "qT" in dbg_d:
                    o = dbg_d["qT"]
                    P.copy("vector", tmp(tmp.h[:, :].rearrange("p (c t) -> p c t", c=8)), qT(qT.h[:, :, 0:512], 0))
                    P.dma("sync", o(o.h.ap()), tmp(tmp.h[:, :]))
                if "ksd" in dbg_d:
                    o = dbg_d["ksd"]
                    t2 = sbt(sd_, "dbg_t2", [128, 4 * 512], F32)
                    P.copy("vector", t2(t2.h[:, :].rearrange("p (c t) -> p c t", c=4)), ksd(ksd.h[:, :, 0:512], 0))
                    P.dma("sync", o(o.h.ap()), t2(t2.h[:, :]))
                if "gat" in dbg_d:
                    o = dbg_d["gat"]
                    P.dma("sync", o(o.h.ap()), gat(gat.h[:, :, :].rearrange("p a b -> p (a b)") if False else gat.h[:, 0, :], 0))
                if "dall" in dbg_d:
                    o = dbg_d["dall"]
                    t3 = sbt(sd_, "dbg_t3", [128, 7 * 128], F32)
                    P.copy("vector", t3(t3.h[:, :].rearrange("p (c t) -> p c t", c=7)), dall(dall.h[:, :, 5, :]))
                    P.dma("sync", o(o.h.ap()), t3(t3.h[:, :]))

        def attention(l, st, qT, ksd, kwd, kcT, vcT, vsa, vwa, gat, dall):
            with ExitStack() as sc:
                kcn = sbt(sc, "c_kcn", [128, 4, 128], BF16)
                vcx = sbt(sc, "c_vcx", [128, 4, 97], BF16)
                P.memset("vector", vcx(vcx.h[:, :, 64:65]), 1.0)
                for g in range(4):
                    P.copy("vector", vcx(vcx.h[0:127, g, 65:97], "ov"), cv("ov", rows=127))
                expb = sbt(sc, "c_expb", [32, 16, 128], BF16)
                P.dma("gpsimd", expb(expb.h[:, :, :]), expm_d(expm_d.h.ap().rearrange("p (a b) -> p a b", a=16)))
                j127 = sbt(sc, "c_j127", [128, 128], BF16)
                P.copy("vector", j127(j127.h[:, :]), cv("j127"))
                with ExitStack() as s2:
                    w1 = sbt(s2, "c_w1", [128, 32, 256], BF16)
                    w2 = sbt(s2, "c_w2", [128, 2, 128], BF16)
                    posb = sbt(s2, "c_posb", [128, 32], BF16)
                    pbias = sbt(s2, "c_pbias", [128, 2], F32)
                    hid = sbt(s2, "c_hid", [128, 2, 128], BF16)
                    sqc = sbt(s2, "c_sq", [128, 128], BF16)
                    rsc = sbt(s2, "c_rs", [128, 128], F32)
                    for kind in ("k", "v"):
                        w1d = dr["cmp_%s_w1" % kind]
                        w2d = dr["cmp_%s_w2" % kind]
                        src1 = w1d.h.ap()[l].rearrange("(l d) f -> d l f", d=64)
                        P.dma("gpsimd", w1(w1.h[0:64, :, :], "a"), w1d(src1))
                        P.dma("gpsimd", w1(w1.h[64:128, :, :], "b"), w1d(src1))
                        src2 = w2d.h.ap()[l].rearrange("(fh p) d -> p fh d", p=128)
                        P.dma("gpsimd", w2(w2.h[:, :, 0:64], "a"), w2d(src2))
                        P.dma("gpsimd", w2(w2.h[:, :, 64:128], "b"), w2d(src2))
                        w1t = w1(w1.h[:, :, :], "a", "b").trks
                        w2t = w2(w2.h[:, :, :], "a", "b").trks
                        pn = "pk" if kind == "k" else "pv"
                        P.copy("vector", posb(posb.h[:, :]), prm(prm.h[:, l, PRM[pn]:PRM[pn] + 32]))
                        for fh in range(2):
                            pb = nb()
                            for ll in range(32):
                                P.mm(pb(pb.h[:, 0:1]), V(w1.h[0:64, ll, fh * 128:(fh + 1) * 128], w1t), posb(posb.h[0:64, ll:ll + 1]),
                                     start=(ll == 0), stop=(ll == 31))
                            P.copy("vector", pbias(pbias.h[:, fh:fh + 1]), pb(pb.h[:, 0:1]))
                        srcT = kcT if kind == "k" else vcT
                        for g in range(4):
                            hf, ch = g % 2, g // 2
                            for fh in range(2):
                                pb = nb()
                                for ll in range(32):
                                    r16 = srcT.h[hf * 64:(hf + 1) * 64, ch, :].rearrange("p (c s) -> p c s", s=16)
                                    rhs = r16[:, 0:127, ll] if ll < 16 else r16[:, 1:128, ll - 16]
                                    P.mm(pb(pb.h[:, 0:127]), V(w1.h[hf * 64:(hf + 1) * 64, ll, fh * 128:(fh + 1) * 128], w1t),
                                         srcT(rhs, 0, 1, 2, 3), start=(ll == 0), stop=(ll == 31))
                                P.act(hid(hid.h[:, fh, 0:127], fh), pb(pb.h[:, 0:127]), AF.Silu, bias=pbias(pbias.h[:, fh:fh + 1]))
                            if kind == "k":
                                pb = nb()
                                for fh in range(2):
                                    P.mm(pb(pb.h[:, 0:127]), V(w2.h[:, fh, :], w2t), hid(hid.h[:, fh, 0:127], fh), start=(fh == 0), stop=(fh == 1))
                                P.act(sqc(sqc.h[:, 0:127]), pb(pb.h[:, 0:127]), AF.Square)
                                pm = nb()
                                P.mm(pm(pm.h[:, 0:127]), blk_bf(blk_bf.h[:, :]), sqc(sqc.h[:, 0:127]))
                                P.act(rsc(rsc.h[:, 0:127]), pm(pm.h[:, 0:127]), AF.Sqrt, bias=epsc(epsc.h[:, 0:1]), scale=1.0 / 64)
                                P.recip(rsc(rsc.h[:, 0:127]), rsc(rsc.h[:, 0:127]))
                                gc = prm(prm.h[:, l, PRM["kn"]:PRM["kn"] + 1])
                                P.stt("vector", kcn(kcn.h[:, g, 0:127], g), pb(pb.h[:, 0:127]), gc, rsc(rsc.h[:, 0:127]), ALU.mult, ALU.mult)
                            else:
                                pb = nb()
                                for fh in range(2):
                                    P.mm(pb(pb.h[0:127, 0:64]), hid(hid.h[:, fh, 0:127], fh), V(w2.h[:, fh, 0:64], w2t), start=(fh == 0), stop=(fh == 1))
                                P.copy("vector", vcx(vcx.h[0:127, g, 0:64], ("v", g)), pb(pb.h[0:127, 0:64]))
                P.barrier()
                bct = [sbt(sc, "t_bch%d" % i, [128, 4, 128], BF16) for i in range(4)]
                bcf = [sbt(sc, "t_bcf%d" % i, [128, 4, 128], BF16) for i in range(4)]
                pT = [sbt(sc, "t_pT%d" % i, [128, 512], BF16) for i in range(5)]
                aaccs = [sbt(sc, "t_aacc%d" % k, [128, 16, 64], F32) for k in range(2)]
                abf = sbt(sc, "t_abf", [128, 1024], BF16)
                rden = sbt(sc, "t_rden", [128, 4], F32)
                coef = sbt(sc, "t_coef", [128, 4], F32)
                imp = sbt(sc, "t_imp", [128, 32], F32)
                scr = sbt(sc, "t_scr", [128, 32], F32)
                wrk = sbt(sc, "t_wrk", [128, 32], F32)
                m8 = sbt(sc, "t_m8", [128, 8], F32)
                selnss = [[sbt(sc, "t_seln%d_%d" % (k, g), [128, 32], F32) for g in range(4)] for k in range(2)]
                selTs = [sbt(sc, "t_selT%d" % g, [32, 2, 128], BF16) for g in range(4)]
                bank_n[0] = 6
                cnt = {"p": 0, "b": 0, "a": 0}
                accb = [banks[6], banks[7]]

                def nacc():
                    b = accb[cnt["a"] % 2]
                    cnt["a"] += 1
                    return b

                o1, o2, o3 = CST["notf"][0], CST["addv"][0], CST["vneg"][0]

                def hd(g, s):
                    return 4 * g + 2 * (s % 2) + s // 2

                def tile_fns(i):
                    q0 = i * 128
                    aacc = aaccs[i % 2]
                    selns = selnss[i % 2]

                    def score(g, klhs, rows, bias_fn, mask_j):
                        p = pT[cnt["p"] % 5]
                        cnt["p"] += 1
                        pss = (nb(), nb())
                        idv = ident_bf(ident_bf.h[0:rows, 0:rows])
                        for hf in range(2):
                            rhs = qT(qT.h[hf * 64:(hf + 1) * 64, 2 * g:2 * g + 2, q0:q0 + 128], i // 4)
                            P.mm(pss[hf](pss[hf].h[0:rows, 0:256]), klhs(hf), rhs, start=True, stop=False)
                        for hf in range(2):
                            P.mm(pss[hf](pss[hf].h[0:rows, 0:256]), idv, bias_fn(hf), start=False, stop=(mask_j is None))
                        if mask_j is not None:
                            sT = selTs[g]
                            for hf in range(2):
                                P.mm(pss[hf](pss[hf].h[0:rows, 0:256]), expb(expb.h[:, mask_j, :]), sT(sT.h[:, :, :]), start=False, stop=True)
                        for hf in range(2):
                            P.act(p(p.h[0:rows, hf * 256:(hf + 1) * 256], hf), pss[hf](pss[hf].h[0:rows, 0:256]), AF.Exp)
                        return p

                    def combine(g, br, pbk, first):
                        w = 97 if br == 0 else 65
                        den = V(pbk.h[:, 0:4 * w].rearrange("p (s c) -> p s c", s=4)[:, :, 64], pbk(pbk.h[:, :]).trks)
                        P.ts("vector", rden(rden.h[:, :]), den, 1e-30, None, ALU.max)
                        P.recip(rden(rden.h[:, :]), rden(rden.h[:, :]))
                        for s in range(4):
                            h = hd(g, s)
                            if br == 0:
                                if s == 0:
                                    P.ts("vector", imp(imp.h[:, :]), pbk(pbk.h[:, s * 97 + 65:s * 97 + 97]), rden(rden.h[:, s:s + 1]), None, ALU.mult)
                                else:
                                    P.stt("vector", imp(imp.h[:, :]), pbk(pbk.h[:, s * 97 + 65:s * 97 + 97]), rden(rden.h[:, s:s + 1]),
                                          imp(imp.h[:, :]), ALU.mult, ALU.add)
                            P.tt("vector", coef(coef.h[:, s:s + 1]), rden(rden.h[:, s:s + 1]), gat(gat.h[:, i, 3 * h + br:3 * h + br + 1], i), ALU.mult)
                            if first:
                                P.ts("vector", aacc(aacc.h[:, h, :], h), pbk(pbk.h[:, s * w:s * w + 64]), coef(coef.h[:, s:s + 1]), None, ALU.mult)
                            else:
                                P.stt("vector", aacc(aacc.h[:, h, :], h), pbk(pbk.h[:, s * w:s * w + 64]), coef(coef.h[:, s:s + 1]),
                                      aacc(aacc.h[:, h, :], h), ALU.mult, ALU.add)

                    def branch(g, steps, pbk, vbuf, mask):
                        LA = 2
                        ps_ = [None] * len(steps)
                        for n in range(min(LA, len(steps))):
                            ps_[n] = steps[n][1]()
                        for n, (j, _) in enumerate(steps):
                            if n + LA < len(steps):
                                ps_[n + LA] = steps[n + LA][1]()
                            p = ps_[n]
                            for s in range(4):
                                P.mm(pbk(pbk.h[:, s * 65:(s + 1) * 65]), p(p.h[:, s * 128:(s + 1) * 128], s // 2), vbuf(vbuf.h[:, j, g, :], j),
                                     start=(n == 0 and s == 0), stop=(n == len(steps) - 1 and s == 3))


                    def phase1():
                        for g in range(4):
                            bh, bf_ = bct[cnt["b"] % 4], bcf[cnt["b"] % 4]
                            cnt["b"] += 1
                            src = bass.AP(fd_d.h, 4 * g * 4096 + 2048 + q0 - 16 * 126 - 31, [[16, 128], [4096, 4], [1, 128]])
                            P.dma("sync", bh(bh.h[:, :, :]), fd_d(src))
                            pb = nb()
                            P.mm(pb(pb.h[0:127, :]), j127(j127.h[0:127, 0:127]), bh(bh.h[0:127, :, :]))
                            P.copy("scalar", bf_(bf_.h[0:127, :, :]), pb(pb.h[0:127, :]))
                            p = score(g, lambda hf: kcn(kcn.h[hf * 64:(hf + 1) * 64, g, 0:127], g), 127,
                                      lambda hf: bf_(bf_.h[0:127, 2 * hf:2 * hf + 2, :]), None)
                            po = nacc()
                            for s in range(4):
                                P.mm(po(po.h[:, s * 97:(s + 1) * 97]), p(p.h[0:127, s * 128:(s + 1) * 128], s // 2),
                                     vcx(vcx.h[0:127, g, :], "ov", ("v", g), 0), start=(s == 0), stop=(s == 3))
                            combine(g, 0, po, True)
                            P.tt("vector", scr(scr.h[:, :]), imp(imp.h[:, :]), cst(cst.h[:, o1 + i * 32:o1 + (i + 1) * 32]), ALU.mult)
                            P.tt("vector", scr(scr.h[:, :]), scr(scr.h[:, :]), cst(cst.h[:, o2 + i * 32:o2 + (i + 1) * 32]), ALU.add)
                            P.op("vector", lambda e: e.max(m8.h[:, :], scr.h[:, :]), [scr(scr.h[:, :])], [m8(m8.h[:, :])])
                            P.op("vector", lambda e: e.match_replace(wrk.h[:, :], m8.h[:, :], scr.h[:, :], -3e30),
                                 [scr(scr.h[:, :]), m8(m8.h[:, :])], [wrk(wrk.h[:, :])])
                            P.op("vector", lambda e: e.max(m8.h[:, :], wrk.h[:, :]), [wrk(wrk.h[:, :])], [m8(m8.h[:, :])])
                            P.op("vector", lambda e: e.match_replace(wrk.h[:, :], m8.h[:, :], wrk.h[:, :], -3e30),
                                 [wrk(wrk.h[:, :]), m8(m8.h[:, :])], [wrk(wrk.h[:, :])])
                            seln = selns[g]
                            P.tt("vector", seln(seln.h[:, :]), scr(scr.h[:, :]), wrk(wrk.h[:, :]), ALU.subtract)
                            P.ts("vector", seln(seln.h[:, :]), seln(seln.h[:, :]), 1.0, -1.0, ALU.min, ALU.add)
                            P.stt("vector", seln(seln.h[:, :]), seln(seln.h[:, :]), 30000.0, cst(cst.h[:, o3 + i * 32:o3 + (i + 1) * 32]),
                                  ALU.mult, ALU.add)

                    def phase23():
                        for g in range(4):
                            pt = nb()
                            P.transpose(pt(pt.h[0:32, 0:128]), selns[g](selns[g].h[:, :]), cv("ident"))
                            sT = selTs[g]
                            for s in range(2):
                                P.copy("vector", sT(sT.h[:, s, :]), pt(pt.h[0:32, 0:128]))
                        jl = max(0, i - 4)
                        segs = []
                        for g in range(4):
                            steps = []
                            for j in range(jl, i + 1):
                                dw = 6 if i - j == 4 else i - j
                                steps.append((j, (lambda j=j, dw=dw, g=g: score(
                                    g, lambda hf: kwd(kwd.h[hf * 64:(hf + 1) * 64, g, j * 128:(j + 1) * 128], j // 4), 128,
                                    lambda hf: dall(dall.h[:, dw, 4 * g + 2 * hf:4 * g + 2 * hf + 2, :]), None))))
                            segs.append((g, steps, vwa, 2))
                        for g in range(4):
                            steps = []
                            for j in range(i + 1):
                                dl = min(i - j, 5)
                                steps.append((j, (lambda j=j, dl=dl, g=g: score(
                                    g, lambda hf: ksd(ksd.h[hf * 64:(hf + 1) * 64, g, j * 128:(j + 1) * 128], j // 4), 128,
                                    lambda hf: dall(dall.h[:, dl, 4 * g + 2 * hf:4 * g + 2 * hf + 2, :]), j))))
                            segs.append((g, steps, vsa, 1))
                        flat = [(si, n, j, fn) for si, (g_, steps_, vb_, br_) in enumerate(segs) for n, (j, fn) in enumerate(steps_)]
                        LA = 2
                        ps_ = {}
                        accs = {}
                        for n in range(min(LA, len(flat))):
                            ps_[n] = flat[n][3]()
                        for n, (si, sn, j, fn) in enumerate(flat):
                            if n + LA < len(flat):
                                ps_[n + LA] = flat[n + LA][3]()
                            g_, steps_, vb_, br_ = segs[si]
                            if sn == 0:
                                accs[si] = nacc()
                            pbk = accs[si]
                            p = ps_.pop(n)
                            lastn = (sn == len(steps_) - 1)
                            for s in range(4):
                                P.mm(pbk(pbk.h[:, s * 65:(s + 1) * 65]), p(p.h[:, s * 128:(s + 1) * 128], s // 2), vb_(vb_.h[:, j, g_, :], j),
                                     start=(sn == 0 and s == 0), stop=(lastn and s == 3))
                            if lastn:
                                combine(g_, br_, pbk, False)
                        for c4 in range(2):
                            pt = nb()
                            for cc in range(4):
                                c = c4 * 4 + cc
                                P.transpose(pt(pt.h[:, cc * 128:(cc + 1) * 128]),
                                            aacc(aacc.h[:, 2 * c:2 * c + 2, :].rearrange("p h d -> p (h d)"), 2 * c, 2 * c + 1), cv("ident"))
                            P.copy("scalar", qT(qT.h[:, c4 * 4:(c4 + 1) * 4, q0:q0 + 128], ("a", i), i // 4),
                                   pt(pt.h[:, :].rearrange("p (c t) -> p c t", c=4)))

                    return phase1, phase23

                fns = [tile_fns(i) for i in range(ntiles)]
                if ntiles:
                    fns[0][0]()
                for i in range(ntiles):
                    if i + 1 < ntiles:
                        fns[i + 1][0]()
                    fns[i][1]()


        def stage_d(l, aT, xsrc, xdst, wcols):
            wpn, wps, wo = dr["w_proj_nsa"], dr["w_proj_sgu"], dr["w_out"]
            with ExitStack() as st:
                xt = sbt(st, "d_x", [128, 16, 512], F32)
                ht = sbt(st, "d_h", [128, 16, 512], BF16)
                sq = [sbt(st, "d_sq%d" % i, [128, 512], BF16) for i in range(2)]
                sd = sbt(st, "d_sd", [128, 512], F32)
                uT = sbt(st, "d_uT", [128, 8, 512], BF16)
                sgT = sbt(st, "d_sgT", [128, 8, 512], BF16)
                mrg = sbt(st, "d_mrg", [128, 16, 512], BF16)
                w16 = [sbt(st, "d_w16_%d" % i, [128, 16, 128], BF16) for i in range(4)]
                w8 = [sbt(st, "d_w8_%d" % i, [128, 8, 128], BF16) for i in range(4)]
                wv2 = sbt(st, "d_wv2", [128, 16, 512], BF16)
                vg = sbt(st, "d_vg", [128, 1024], F32)
                vsq = sbt(st, "d_vsq", [128, 1024], F32)
                vln = sbt(st, "d_vln", [128, 1024], BF16)
                stat = sbt(st, "d_stat", [128, 8], F32)
                lng = sbt(st, "d_lng", [128, 1024], F32)
                lnb = sbt(st, "d_lnb", [128, 1024], F32)
                wsT = sbt(st, "d_wsT", [128, 8, 128], BF16)
                wsf = sbt(st, "d_wsf", [128, 8, 128], F32)
                sbb = sbt(st, "d_sbb", [1, 1024], BF16)
                sg1 = sbt(st, "d_sg1", [128, 512], F32)
                sg2 = sbt(st, "d_sg2", [128, 512], F32)
                t1 = sbt(st, "d_t1", [128, 512], F32)
                P.dma("sync", lng(lng.h[:, :]), sgb_d(sgb_d.h.ap()[l, 0]))
                P.dma("sync", lnb(lnb.h[:, :]), sgb_d(sgb_d.h.ap()[l, 1]))
                P.dma("sync", wsf(wsf.h[:, :, :]), sgw_d(sgw_d.h.ap()[l]))
                for g in range(8):
                    P.tt("vector", wsT(wsT.h[:, g, :]), wsf(wsf.h[:, g, :]), cv("triu"), ALU.mult)
                P.dma("gpsimd", sbb(sbb.h[:, :]), sgub_d(sgub_d.h.ap()[l]))
                k16 = 0
                k8 = 0
                for tb in range(4):
                    t0 = tb * 512
                    make_h((xt, 0), (ht, 0), xsrc, l, "nm", t0, 0, sq, sd)
                    for c in range(8):
                        w = w16[k16 % 4]
                        k16 += 1
                        wfetch(tb, w(w.h[:, :, :]), wc16_d, wc16_d.h.ap()[20 + c], 20 + c, [(w(w.h[:, :, :]), wcols(OFF_UV + c * 128, 128))])
                        pz = nb()
                        for k in range(16):
                            P.mm(pz(pz.h[:, :]), w(w.h[:, k, :]), ht(ht.h[:, k, :], ("h", 0)), start=(k == 0), stop=(k == 15))
                        P.act(uT(uT.h[:, c, :], c), pz(pz.h[:, :]), AF.Gelu_apprx_tanh)
                    for sub in range(4):
                        for half in range(2):
                            wfetch(tb if sub == 0 else 1, wv2(wv2.h[:, :, :]), wcv2_d, wcv2_d.h.ap()[half], half,
                                   [(wv2(wv2.h[:, :, :]), wcols(OFF_UV + 1024 + half * 512, 512))])
                            pz = nb()
                            for k in range(16):
                                P.mm(pz(pz.h[:, :]), ht(ht.h[:, k, sub * 128:(sub + 1) * 128], ("h", 0)), wv2(wv2.h[:, k, :]),
                                     start=(k == 0), stop=(k == 15))
                            P.act(vg(vg.h[:, half * 512:(half + 1) * 512], half), pz(pz.h[:, :]), AF.Gelu_apprx_tanh)
                        vga = vg(vg.h[:, :], 0, 1)
                        P.op("vector", lambda e: e.tensor_reduce(stat.h[:, 0:1], vg.h[:, :], AX.X, ALU.add), [vga], [stat(stat.h[:, 0:1], 0)])
                        P.act(vsq(vsq.h[:, :]), vga, AF.Square)
                        P.op("vector", lambda e: e.tensor_reduce(stat.h[:, 1:2], vsq.h[:, :], AX.X, ALU.add), [vsq(vsq.h[:, :])], [stat(stat.h[:, 1:2], 1)])
                        P.ts("vector", stat(stat.h[:, 2:3], 2), stat(stat.h[:, 0:1], 0), 1.0 / 1024, None, ALU.mult)
                        P.tt("vector", stat(stat.h[:, 3:4], 3), stat(stat.h[:, 2:3], 2), stat(stat.h[:, 2:3], 2), ALU.mult)
                        P.stt("vector", stat(stat.h[:, 4:5], 4), stat(stat.h[:, 1:2], 1), 1.0 / 1024, stat(stat.h[:, 3:4], 3), ALU.mult, ALU.subtract)
                        P.act(stat(stat.h[:, 5:6], 5), stat(stat.h[:, 4:5], 4), AF.Sqrt, bias=epsc(epsc.h[:, 0:1]), scale=1.0)
                        P.recip(stat(stat.h[:, 6:7], 6), stat(stat.h[:, 5:6], 5))
                        P.ts("vector", vsq(vsq.h[:, :]), vga, stat(stat.h[:, 2:3], 2), stat(stat.h[:, 6:7], 6), ALU.subtract, ALU.mult)
                        P.tt("vector", vsq(vsq.h[:, :]), vsq(vsq.h[:, :]), lng(lng.h[:, :]), ALU.mult)
                        P.tt("vector", vln(vln.h[:, :]), vsq(vsq.h[:, :]), lnb(lnb.h[:, :]), ALU.add)
                        for gh in range(2):
                            pz = nb()
                            for gg in range(4):
                                g = gh * 4 + gg
                                P.mm(pz(pz.h[:, gg * 128:(gg + 1) * 128]), vln(vln.h[:, g * 128:(g + 1) * 128]), wsT(wsT.h[:, g, :]), start=(gg == 0), stop=False)
                                P.mm(pz(pz.h[:, gg * 128:(gg + 1) * 128]), ones_bf(ones_bf.h[0:1, :]), sbb(sbb.h[0:1, g * 128:(g + 1) * 128]), start=False, stop=(gg == 3))
                            P.tt("vector", sgT(sgT.h[:, gh * 4:(gh + 1) * 4, sub * 128:(sub + 1) * 128], (gh, sub)),
                                 pz(pz.h[:, :].rearrange("p (g t) -> p g t", g=4)),
                                 uT(uT.h[:, gh * 4:(gh + 1) * 4, sub * 128:(sub + 1) * 128], *range(gh * 4, gh * 4 + 4)), ALU.mult)
                    sgk = [(gh, sub) for gh in range(2) for sub in range(4)]
                    for j in range(16):
                        wa, wb_ = w8[k8 % 4], w8[(k8 + 1) % 4]
                        k8 += 2
                        wg1, wg2 = w16[k16 % 4], w16[(k16 + 1) % 4]
                        k16 += 2
                        wfetch(tb, wa(wa.h[:, :, :]), wc8_d, wc8_d.h.ap()[2 * j], 2 * j,
                               [(wa(wa.h[:, :, :]), wpn(wpn.h.ap()[l, :, j * 128:(j + 1) * 128].rearrange("(c p) f -> p c f", p=128)))])
                        wfetch(tb, wb_(wb_.h[:, :, :]), wc8_d, wc8_d.h.ap()[2 * j + 1], 2 * j + 1,
                               [(wb_(wb_.h[:, :, :]), wps(wps.h.ap()[l, :, j * 128:(j + 1) * 128].rearrange("(c p) f -> p c f", p=128)))])
                        wfetch(tb, wg1(wg1.h[:, :, :]), wc16_d, wc16_d.h.ap()[28 + 2 * j], 28 + 2 * j, [(wg1(wg1.h[:, :, :]), wcols(OFF_MG + j * 128, 128))])
                        wfetch(tb, wg2(wg2.h[:, :, :]), wc16_d, wc16_d.h.ap()[29 + 2 * j], 29 + 2 * j, [(wg2(wg2.h[:, :, :]), wcols(OFF_MG + D + j * 128, 128))])
                        pa, pb, pg1, pg2 = nb(), nb(), nb(), nb()
                        for k in range(8):
                            P.mm(pa(pa.h[:, :]), wa(wa.h[:, k, :]), aT(aT.h[:, k, t0:t0 + 512], *[("a", i) for i in range(tb * 4, tb * 4 + 4)]),
                                 start=(k == 0), stop=(k == 7))
                        for k in range(8):
                            P.mm(pb(pb.h[:, :]), wb_(wb_.h[:, k, :]), sgT(sgT.h[:, k, :], *sgk), start=(k == 0), stop=(k == 7))
                        for k in range(16):
                            P.mm(pg1(pg1.h[:, :]), wg1(wg1.h[:, k, :]), ht(ht.h[:, k, :], ("h", 0)), start=(k == 0), stop=(k == 15))
                        for k in range(16):
                            P.mm(pg2(pg2.h[:, :]), wg2(wg2.h[:, k, :]), ht(ht.h[:, k, :], ("h", 0)), start=(k == 0), stop=(k == 15))
                        P.act(sg1(sg1.h[:, :]), pg1(pg1.h[:, :]), AF.Sigmoid)
                        P.act(sg2(sg2.h[:, :]), pg2(pg2.h[:, :]), AF.Sigmoid)
                        P.tt("vector", t1(t1.h[:, :]), sg1(sg1.h[:, :]), pa(pa.h[:, :]), ALU.mult)
                        P.tt("vector", sg2(sg2.h[:, :]), sg2(sg2.h[:, :]), pb(pb.h[:, :]), ALU.mult)
                        P.tt("vector", mrg(mrg.h[:, j, :], j), t1(t1.h[:, :]), sg2(sg2.h[:, :]), ALU.add)
                    for j in range(16):
                        w = w16[k16 % 4]
                        k16 += 1
                        wfetch(tb, w(w.h[:, :, :]), wc16_d, wc16_d.h.ap()[60 + j], 60 + j,
                               [(w(w.h[:, :, :]), wo(wo.h.ap()[l, :, j * 128:(j + 1) * 128].rearrange("(c p) f -> p c f", p=128)))])
                        pz = nb()
                        for k in range(16):
                            P.mm(pz(pz.h[:, :]), w(w.h[:, k, :]), mrg(mrg.h[:, k, :], *range(16)), start=(k == 0), stop=(k == 15))
                        xv = xt(xt.h[:, j, :], ("x", 0))
                        P.tt("vector", xv, xv, pz(pz.h[:, :]), ALU.add)
                    P.dma("sync", xdst(xdst.h.ap()[:, :, t0:t0 + 512].rearrange("c p t -> p c t"), tb), xt(xt.h[:, :, :], ("x", 0)))

        cur = x_in
        for l in range(depth):
            last = (l == depth - 1)
            todo = [p for p in "1m2" if p in phases]
            for p in todo:
                dst = out_d if (last and p == todo[-1]) else xs_d
                if p == "1":
                    ffn(l, "ffn1", cur, dst)
                elif p == "m":
                    mixer(l, cur, dst)
                else:
                    ffn(l, "ffn2", cur, dst)
                cur = xs_d
        P.barrier()
        P.emit(top)
    return nc


_NC_CACHE = {}


def _prep_shared(inputs):
    f = lambda a: np.ascontiguousarray(np.asarray(a, np.float32))
    sh = {}
    for k in ("ffn1_w_gate", "ffn1_w_up", "ffn1_w_down", "ffn2_w_gate", "ffn2_w_up", "ffn2_w_down", "w_in",
              "cmp_k_w1", "cmp_k_w2", "cmp_v_w1", "cmp_v_w2", "w_proj_nsa", "w_proj_sgu", "w_out"):
        sh[k] = f(inputs[k])
    sh["cst"] = CST_ARR
    prm = np.zeros((L, 128, NPRM), np.float32)
    for l in range(L):
        for nm, key in (("n1", "ffn1_norm"), ("nm", "mix_norm"), ("n2", "ffn2_norm")):
            prm[l, :, PRM[nm]:PRM[nm] + 16] = f(inputs[key])[l].reshape(16, 128).T
        prm[l, :, PRM["qn"]] = np.tile(f(inputs["q_norm"])[l], 2)
        for i in range(3):
            prm[l, :, PRM["kn"] + i] = np.tile(f(inputs["k_norm"])[l, i], 2)
        prm[l, :, PRM["pk"]:PRM["pk"] + 32] = np.tile(f(inputs["cmp_pos_k"])[l].T, (2, 1))
        prm[l, :, PRM["pv"]:PRM["pv"] + 32] = np.tile(f(inputs["cmp_pos_v"])[l].T, (2, 1))
    sh["prm"] = prm
    rb = f(inputs["rel_bias"])
    rbp = np.zeros((32, 16), np.float32)
    for g in range(4):
        for s in range(4):
            rbp[:, 4 * g + s] = rb[:, 4 * g + 2 * (s % 2) + s // 2]
    sh["rbp"] = rbp
    sh["oh"] = OH_ARR
    sh["expm"] = EXPM_ARR
    sh["sgub"] = np.ascontiguousarray(f(inputs["sgu_b"]).reshape(L, 1, 1024))
    sgb = np.zeros((L, 2, 128, 1024), np.float32)
    sgb[:, 0] = f(inputs["sgu_norm_g"])[:, None, :]
    sgb[:, 1] = f(inputs["sgu_norm_b"])[:, None, :]
    sh["sgb"] = sgb
    sh["sgwT"] = np.ascontiguousarray(f(inputs["sgu_w"]).transpose(0, 3, 1, 2))
    return sh


def kernel(**inputs):
    x = np.asarray(inputs["x"], np.float32)
    sh = _prep_shared(inputs)
    if "nc" not in _NC_CACHE:
        _NC_CACHE["nc"] = build_program()
    nc = _NC_CACHE["nc"]
    in_maps = []
    for b in range(NCORES):
        m = dict(sh)
        m["x_fm"] = np.ascontiguousarray(x[b].T.reshape(16, 128, S))
        in_maps.append(m)
    res = run_bass_kernel_spmd(nc, in_maps, core_ids=list(range(NCORES)))
    out = np.empty((NCORES, S, D), np.float32)
    for b in range(NCORES):
        out[b] = res.results[b]["out_fm"].reshape(D, S).T
    return out
```

```python
import math
from contextlib import ExitStack
import numpy as np
import concourse.bass as bass
import concourse.mybir as mybir
from concourse.bass_utils import run_bass_kernel_spmd

F32 = mybir.dt.float32
BF16 = mybir.dt.bfloat16
ALU = mybir.AluOpType
AF = mybir.ActivationFunctionType
AX = mybir.AxisListType

D = 2048
S = 2048
L = 2
DFF = 5504
NFC = 43
INW = 8752
OFF_KV, OFF_NG, OFF_UV, OFF_MG = 1024, 2560, 2608, 4656
EPS = 1e-6
NEGB = -30000.0
NCORES = 4

ENGS = ["sync", "gpsimd", "scalar", "vector", "tensor"]
NDSEM = 24


class Trk:
    __slots__ = ("w", "r", "x")

    def __init__(self, x=False):
        self.w = None
        self.r = {}
        self.x = x


class V:
    __slots__ = ("ap", "trks")

    def __init__(self, ap, trks):
        self.ap = ap
        self.trks = trks


class Buf:
    def __init__(self, h, excl=False):
        self.h = h
        self.t = {}
        self.excl = excl

    def __call__(self, ap, *keys):
        if not keys:
            keys = (0,)
        out = []
        for k in keys:
            t = self.t.get(k)
            if t is None:
                t = self.t[k] = Trk(self.excl)
            out.append(t)
        return V(ap, out)


class Prog:
    def __init__(self, nc):
        self.nc = nc
        self.ops = {e: [] for e in ENGS}
        self.seen = {e: {} for e in ENGS}
        self.seen_dma = {e: set() for e in ENGS}
        self.ndma = {e: 0 for e in ENGS}

    def _dep(self, eng, waits, d):
        if d[0] == "dma":
            if d in self.seen_dma[eng]:
                return
            self.seen_dma[eng].add(d)
            waits.append(d)
        else:
            e, i = d
            if e == eng and eng in ("tensor", "sync"):
                return
            if self.seen[eng].get(e, -1) >= i:
                return
            self.seen[eng][e] = i
            self.ops[e][i]["inc"] = True
            waits.append(d)

    def op(self, eng, fn, reads=(), writes=(), dma=False):
        idx = len(self.ops[eng])
        waits = []
        deps = []
        rd, wr = [], []
        for v in reads:
            for t in v.trks:
                (wr if t.x else rd).append(t)
        for v in writes:
            wr.extend(v.trks)
        for t in rd:
            if t.w is not None:
                deps.append(t.w)
        for t in wr:
            if t.w is not None:
                deps.append(t.w)
            deps.extend(t.r.values())
        if dma:
            n = self.ndma[eng]
            self.ndma[eng] += 1
            ev = ("dma", eng, n)
            if n >= NDSEM:
                deps.append(("dma", eng, n - NDSEM))
        else:
            ev = (eng, idx)
        for d in deps:
            self._dep(eng, waits, d)
        self.ops[eng].append({"fn": fn, "waits": waits, "inc": False, "dma": ev if dma else None})
        for t in rd:
            t.r[ev if dma else eng] = ev
        for t in wr:
            t.w = ev
            t.r = {}
        return ev

    def barrier(self):
        evs = []
        for e in ENGS:
            if self.ops[e]:
                last = len(self.ops[e]) - 1
                while last >= 0 and self.ops[e][last]["fn"] is None:
                    last -= 1
                if last >= 0 and self.ops[e][last]["dma"] is None:
                    evs.append((e, last))
            n = self.ndma[e]
            for k in range(max(0, n - NDSEM), n):
                evs.append(("dma", e, k))
        for e in ENGS:
            waits = []
            for d in evs:
                if d[0] != "dma" and d[0] == e:
                    continue
                self._dep(e, waits, d)
            self.ops[e].append({"fn": None, "waits": waits, "inc": False, "dma": None})

    def emit(self, stack):
        nc = self.nc
        esem = {e: stack.enter_context(nc.semaphore("es_" + e)) for e in ENGS}
        dsem = {e: [stack.enter_context(nc.semaphore("ds_%s_%d" % (e, i))) for i in range(NDSEM)]
                for e in ENGS if self.ndma[e] > 0}
        for e in ENGS:
            c = 0
            for o in self.ops[e]:
                if o["inc"]:
                    c += 1
                o["cnt"] = c
        block = stack.enter_context(nc.Block())
        ops = self.ops

        def run(engobj, e):
            for o in ops[e]:
                for d in o["waits"]:
                    if d[0] == "dma":
                        _, q, n = d
                        engobj.wait_ge(dsem[q][n % NDSEM], 16 * (n // NDSEM + 1))
                    else:
                        engobj.wait_ge(esem[d[0]], ops[d[0]][d[1]]["cnt"])
                if o["fn"] is None:
                    continue
                ins = o["fn"](engobj)
                if o["dma"] is not None:
                    _, q, n = o["dma"]
                    ins.then_inc(dsem[q][n % NDSEM], 16)
                elif o["inc"]:
                    ins.then_inc(esem[e], 1)

        @block.sync
        def _(x):
            run(x, "sync")

        @block.gpsimd
        def _(x):
            run(x, "gpsimd")

        @block.scalar
        def _(x):
            run(x, "scalar")

        @block.vector
        def _(x):
            run(x, "vector")

        @block.tensor
        def _(x):
            run(x, "tensor")

    def dma(self, q, out, in_):
        return self.op(q, lambda e: e.dma_start(out=out.ap, in_=in_.ap), [in_], [out], dma=True)

    def mm(self, out, lhsT, rhs, start=True, stop=True):
        return self.op("tensor", lambda e: e.matmul(out.ap, lhsT.ap, rhs.ap, start=start, stop=stop),
                       [lhsT, rhs], [out])

    def transpose(self, out, in_, ident):
        return self.op("tensor", lambda e: e.transpose(out.ap, in_.ap, ident.ap), [in_, ident], [out])

    def act(self, out, in_, func, bias=None, scale=None):
        reads = [in_]
        kw = {}
        if bias is not None:
            reads.append(bias)
            kw["bias"] = bias.ap
        if scale is not None:
            kw["scale"] = scale
        return self.op("scalar", lambda e: e.activation(out.ap, in_.ap, func, **kw), reads, [out])

    def tt(self, eng, out, in0, in1, op):
        return self.op(eng, lambda e: e.tensor_tensor(out.ap, in0.ap, in1.ap, op), [in0, in1], [out])

    def ts(self, eng, out, in0, s1, s2, op0, op1=None):
        reads = [in0]
        a1, a2 = s1, s2
        if isinstance(s1, V):
            reads.append(s1)
            a1 = s1.ap
        if isinstance(s2, V):
            reads.append(s2)
            a2 = s2.ap
        kw = {}
        if op1 is not None:
            kw["op1"] = op1
        return self.op(eng, lambda e: e.tensor_scalar(out.ap, in0.ap, a1, a2, op0, **kw), reads, [out])

    def stt(self, eng, out, in0, scalar, in1, op0, op1):
        reads = [in0, in1]
        a = scalar
        if isinstance(scalar, V):
            reads.append(scalar)
            a = scalar.ap
        return self.op(eng, lambda e: e.scalar_tensor_tensor(out.ap, in0.ap, a, in1.ap, op0, op1), reads, [out])

    def copy(self, eng, out, in_):
        if eng == "scalar":
            return self.op(eng, lambda e: e.copy(out.ap, in_.ap), [in_], [out])
        return self.op(eng, lambda e: e.tensor_copy(out.ap, in_.ap), [in_], [out])

    def memset(self, eng, out, val):
        return self.op(eng, lambda e: e.memset(out.ap, val), [], [out])

    def recip(self, out, in_):
        return self.op("vector", lambda e: e.reciprocal(out.ap, in_.ap), [in_], [out])


def _t5_bucket(dist):
    n = np.maximum(dist, 0)
    nf = np.maximum(n, 1).astype(np.float32)
    large = 16 + (np.log(nf / np.float32(16)) / np.float32(math.log(128 / 16)) * np.float32(16)).astype(np.int32)
    large = np.minimum(large, 31)
    return np.where(n < 16, n, large)


CST = {}


def _build_consts():
    cols = []
    off = [0]

    def add(name, arr):
        arr = np.asarray(arr, np.float32)
        a = np.zeros((128, arr.shape[1]), np.float32)
        a[:arr.shape[0]] = arr
        CST[name] = (off[0], arr.shape[1])
        off[0] += arr.shape[1]
        cols.append(a)

    idx = np.arange(4096)
    dist = idx - 2048
    oh = np.zeros((33, 4096), np.float32)
    bk = _t5_bucket(dist)
    for i in range(4096):
        if dist[i] < 0:
            oh[32, i] = 1
        else:
            oh[bk[i], i] = 1
    CST["_oh"] = oh
    add("ident", np.eye(128))
    add("j128", np.eye(128)[::-1])
    j127 = np.zeros((128, 128))
    j127[:127, :127] = np.eye(127)[::-1]
    add("j127", j127)
    s_ = np.arange(128)[:, None]
    t_ = np.arange(128)[None, :]
    add("triu", (s_ <= t_))
    expm = np.zeros((32, 16, 128))
    for jc in range(16):
        for k in range(128):
            expm[2 * jc + k // 64, jc, k] = 1
    CST["_expm"] = expm.reshape(32, -1).astype(np.float32)
    ci = np.arange(127)[:, None] * 16
    sj = np.arange(32)[None, :] * 64
    ov = np.clip(np.minimum(ci + 32, sj + 64) - np.maximum(ci, sj), 0, None).astype(np.float32) / 32
    add("ov", ov)
    notf = np.zeros((128, 16, 32))
    addv = np.zeros((128, 16, 32))
    vneg = np.zeros((128, 16, 32))
    for i in range(16):
        for p in range(128):
            cur = (128 * i + p) // 64
            for j in range(32):
                if j > cur:
                    addv[p, i, j] = -1e30
                    vneg[p, i, j] = NEGB
                elif j == 0:
                    addv[p, i, j] = 1e4
                elif j == cur:
                    addv[p, i, j] = 2e4
                elif j == cur - 1:
                    addv[p, i, j] = 3e4
                else:
                    notf[p, i, j] = 1
    add("notf", notf.reshape(128, -1))
    add("addv", addv.reshape(128, -1))
    add("vneg", vneg.reshape(128, -1))
    m6 = (t_ < s_).astype(np.float32)
    add("m6", m6)
    add("n6", (m6 - 1) * 30000.0)
    return np.concatenate(cols, axis=1)


CST_ARR = _build_consts()
NCST = CST_ARR.shape[1]
PRM = {"n1": 0, "nm": 16, "n2": 32, "qn": 48, "kn": 49, "pk": 52, "pv": 84}
NPRM = 116
OH_ARR = CST["_oh"]
EXPM_ARR = CST["_expm"]


def build_program(depth=L, dbg=None, phases="1m2", mstop=None, ntiles=16, tstop=8):
    nc = bass.Bass("TRN2", target_bir_lowering=False)
    dr = {}

    def din(name, shape, dt=F32):
        dr[name] = Buf(nc.dram_tensor(name, list(shape), dt, kind="ExternalInput"))
        return dr[name]

    x_in = din("x_fm", [16, 128, S])
    cst_d = din("cst", [128, NCST])
    prm_d = din("prm", [L, 128, NPRM])
    rb_d = din("rbp", [32, 16])
    oh_d = din("oh", [33, 4096])
    expm_d = din("expm", [32, 2048])
    sgub_d = din("sgub", [L, 1, 1024])
    sgb_d = din("sgb", [L, 2, 128, 1024])
    sgw_d = din("sgwT", [L, 128, 8, 128])
    wnames = {"ffn1_w_gate": [L, D, DFF], "ffn1_w_up": [L, D, DFF], "ffn1_w_down": [L, DFF, D],
              "ffn2_w_gate": [L, D, DFF], "ffn2_w_up": [L, D, DFF], "ffn2_w_down": [L, DFF, D],
              "w_in": [L, D, INW], "cmp_k_w1": [L, 2048, 256], "cmp_k_w2": [L, 256, 64],
              "cmp_v_w1": [L, 2048, 256], "cmp_v_w2": [L, 256, 64],
              "w_proj_nsa": [L, 1024, D], "w_proj_sgu": [L, 1024, D], "w_out": [L, D, D]}
    for k, shp in wnames.items():
        din(k, shp)
    out_d = Buf(nc.dram_tensor("out_fm", [16, 128, S], F32, kind="ExternalOutput"))
    xs_d = Buf(nc.dram_tensor("xs", [16, 128, S], F32))
    fd_d = Buf(nc.dram_tensor("fdt", [16, 4096], BF16))
    wc16_d = Buf(nc.dram_tensor("wc16", [76, 128, 16, 128], BF16))
    wc8_d = Buf(nc.dram_tensor("wc8", [32, 128, 8, 128], BF16))
    wcv_d = Buf(nc.dram_tensor("wcv", [128, 16, 560], BF16))
    wcv2_d = Buf(nc.dram_tensor("wcv2", [2, 128, 16, 512], BF16))
    dbg_d = {}
    if dbg:
        for name, shp in dbg.items():
            dbg_d[name] = Buf(nc.dram_tensor("dbg_" + name, list(shp), F32, kind="ExternalOutput"))

    P = Prog(nc)
    _NC_CACHE['P'] = P
    out_events = []

    with ExitStack() as top:
        top.enter_context(nc.allow_low_precision("bf16 matmul operands, fp32 accumulation"))

        uniq = [0]

        def sbt(st, name, shape, dt):
            uniq[0] += 1
            return Buf(st.enter_context(nc.sbuf_tensor("%s_%d" % (name, uniq[0]), list(shape), dt)))

        banks = [Buf(top.enter_context(nc.psum_tensor("pb%d" % i, [128, 512], F32)), excl=True) for i in range(8)]
        bank_i = [0]
        bank_n = [8]

        def nb():
            b = banks[bank_i[0] % bank_n[0]]
            bank_i[0] += 1
            return b

        cst = sbt(top, "cst_sb", [128, NCST], F32)
        P.dma("sync", cst(cst.h[:, :]), cst_d(cst_d.h.ap()))

        def cv(name, rows=128, lo=0, n=None):
            o, w = CST[name]
            if n is None:
                n = w - lo
            return cst(cst.h[0:rows, o + lo:o + lo + n])

        prm = sbt(top, "prm_sb", [128, L, NPRM], F32)
        P.dma("sync", prm(prm.h[:, :, :]), prm_d(prm_d.h.ap().rearrange("l p n -> p l n")))
        ones_bf = sbt(top, "ones_bf", [128, 128], BF16)
        P.memset("vector", ones_bf(ones_bf.h[:, :]), 1.0)
        blk_bf = sbt(top, "blk_bf", [128, 128], BF16)
        P.memset("vector", blk_bf(blk_bf.h[:, :]), 0.0)
        P.memset("vector", blk_bf(blk_bf.h[0:64, 0:64]), 1.0)
        P.memset("vector", blk_bf(blk_bf.h[64:128, 64:128]), 1.0)
        ident_bf = sbt(top, "ident_bf", [128, 128], BF16)
        P.copy("vector", ident_bf(ident_bf.h[:, :]), cv("ident"))
        epsc = sbt(top, "epsc", [128, 2], F32)
        P.memset("vector", epsc(epsc.h[:, 0:1]), EPS)
        P.memset("vector", epsc(epsc.h[:, 1:2]), 0.0)
        gsc = sbt(top, "gsc", [128, L, 52], F32)
        for l in range(L):
            P.copy("vector", gsc(gsc.h[:, l, 0:49]), prm(prm.h[:, l, 0:49]))
            P.ts("vector", gsc(gsc.h[:, l, 49:50]), prm(prm.h[:, l, 48:49]), 0.125, None, ALU.mult)

        def pcol(l, name, c=0):
            return gsc(gsc.h[:, l, PRM[name] + c:PRM[name] + c + 1])

        def make_h(st_x, st_h, xsrc, l, nname, tok0, hcol0, sq, sd):
            xt, xo = st_x
            ht, ho = st_h
            P.dma("sync", xt(xt.h[:, :, xo:xo + 512], ("x", xo)),
                  xsrc(xsrc.h.ap()[:, :, tok0:tok0 + 512].rearrange("c p t -> p c t"), tok0 // 512))
            psn = nb()
            for c in range(16):
                s = sq[c % 2]
                P.act(s(s.h[:, :]), xt(xt.h[:, c, xo:xo + 512], ("x", xo)), AF.Square)
                P.mm(psn(psn.h[:, :]), ones_bf(ones_bf.h[:, :]), s(s.h[:, :]), start=(c == 0), stop=(c == 15))
            P.act(sd(sd.h[:, :]), psn(psn.h[:, :]), AF.Sqrt, bias=epsc(epsc.h[:, 0:1]), scale=1.0 / D)
            P.recip(sd(sd.h[:, :]), sd(sd.h[:, :]))
            for c in range(16):
                P.stt("vector", ht(ht.h[:, c, ho:ho + 512], ("h", ho)), xt(xt.h[:, c, xo:xo + 512], ("x", xo)),
                      pcol(l, nname, c), sd(sd.h[:, :]), ALU.mult, ALU.mult)

        def wfetch(tb, wt_view, cbuf, cap, ckey, loaders):
            if tb == 0:
                for d_, s_ in loaders:
                    P.dma("gpsimd", d_, s_)
                P.dma("sync", cbuf(cap, ckey), wt_view)
            else:
                P.dma("sync", wt_view, cbuf(cap, ckey))

        def wload(wt, src_ap):
            P.dma("gpsimd", wt(wt.h[:, :, :]) if len(wt.h.shape) == 3 else wt(wt.h[:, :]), src_ap)

        def ffn(l, which, xsrc, xdst):
            wg_d, wu_d, wd_d = dr[which + "_w_gate"], dr[which + "_w_up"], dr[which + "_w_down"]
            nname = "n1" if which == "ffn1" else "n2"
            with ExitStack() as st:
                xt = sbt(st, "f_x", [128, 16, 1024], F32)
                ht = sbt(st, "f_h", [128, 16, 1024], BF16)
                act = sbt(st, "f_act", [128, 22, 1024], BF16)
                sq = [sbt(st, "f_sq%d" % i, [128, 512], BF16) for i in range(2)]
                sd = sbt(st, "f_sd", [128, 512], F32)
                wg = [sbt(st, "f_wg%d" % i, [128, 16, 128], BF16) for i in range(3)]
                wu = [sbt(st, "f_wu%d" % i, [128, 16, 128], BF16) for i in range(3)]
                wd = [sbt(st, "f_wd%d" % i, [128, 22, 128], BF16) for i in range(4)]
                sl = [sbt(st, "f_sl%d" % i, [128, 512], BF16) for i in range(2)]
                wi = 0
                di = 0
                for tt in range(2):
                    for tb in range(2):
                        make_h((xt, tb * 512), (ht, tb * 512), xsrc, l, nname, tt * 1024 + tb * 512, 0, sq, sd)
                    for fh in range(2):
                        fcs = list(range(0, 22)) if fh == 0 else list(range(22, 43))
                        for fi, fc in enumerate(fcs):
                            g, u = wg[wi % 3], wu[wi % 3]
                            wi += 1
                            P.dma("gpsimd", g(g.h[:, :, :]),
                                  wg_d(wg_d.h.ap()[l, :, fc * 128:(fc + 1) * 128].rearrange("(c p) f -> p c f", p=128)))
                            P.dma("gpsimd", u(u.h[:, :, :]),
                                  wu_d(wu_d.h.ap()[l, :, fc * 128:(fc + 1) * 128].rearrange("(c p) f -> p c f", p=128)))
                            for tb in range(2):
                                pg, pu = nb(), nb()
                                for c in range(16):
                                    P.mm(pg(pg.h[:, :]), g(g.h[:, c, :]), ht(ht.h[:, c, tb * 512:(tb + 1) * 512], ("h", tb * 512)),
                                         start=(c == 0), stop=(c == 15))
                                for c in range(16):
                                    P.mm(pu(pu.h[:, :]), u(u.h[:, c, :]), ht(ht.h[:, c, tb * 512:(tb + 1) * 512], ("h", tb * 512)),
                                         start=(c == 0), stop=(c == 15))
                                s = sl[(fi * 2 + tb) % 2]
                                P.act(s(s.h[:, :]), pg(pg.h[:, :]), AF.Silu)
                                P.tt("vector", act(act.h[:, fi, tb * 512:(tb + 1) * 512], (fi, tb)), s(s.h[:, :]), pu(pu.h[:, :]), ALU.mult)
                        nf = len(fcs)
                        for dcp in range(16):
                            w = wd[di % 4]
                            di += 1
                            P.dma("gpsimd", w(w.h[:, 0:nf, :]),
                                  wd_d(wd_d.h.ap()[l, fcs[0] * 128:(fcs[-1] + 1) * 128, dcp * 128:(dcp + 1) * 128]
                                       .rearrange("(f p) d -> p f d", p=128)))
                            for ds in range(1):
                                dc = dcp
                                for tb in range(2):
                                    pd = nb()
                                    for fi in range(nf):
                                        P.mm(pd(pd.h[:, :]), w(w.h[:, fi, ds * 128:(ds + 1) * 128]),
                                             act(act.h[:, fi, tb * 512:(tb + 1) * 512], (fi, tb)), start=(fi == 0), stop=(fi == nf - 1))
                                    xv = xt(xt.h[:, dc, tb * 512:(tb + 1) * 512], ("x", tb * 512))
                                    P.stt("vector", xv, pd(pd.h[:, :]), 0.5, xv, ALU.mult, ALU.add)
                    ev = P.dma("sync", xdst(xdst.h.ap()[:, :, tt * 1024:(tt + 1) * 1024].rearrange("c p t -> p c t"), 2 * tt, 2 * tt + 1),
                               xt(xt.h[:, :, :], ("x", 0), ("x", 512)))
                    if xdst is out_d:
                        out_events.append(ev)
            P.barrier()

        def build_tables(st):
            dall = sbt(st, "dall", [128, 7, 16, 128], BF16)
            with ExitStack() as s2:
                rbx = sbt(s2, "rbx", [33, 16], F32)
                rbb = sbt(s2, "rbb", [33, 16], BF16)
                ohb = sbt(s2, "ohb", [33, 4096], BF16)
                fsb = sbt(s2, "fsb", [16, 4096], BF16)
                hk = sbt(s2, "hk", [128, 16, 128], BF16)
                P.memset("vector", rbx(rbx.h[:, :]), NEGB)
                P.dma("sync", rbx(rbx.h[0:32, :]), rb_d(rb_d.h.ap()))
                P.copy("vector", rbb(rbb.h[:, :]), rbx(rbx.h[:, :]))
                P.dma("gpsimd", ohb(ohb.h[:, :]), oh_d(oh_d.h.ap()))
                for n in range(8):
                    pb = nb()
                    P.mm(pb(pb.h[0:16, :]), rbb(rbb.h[:, :]), ohb(ohb.h[:, n * 512:(n + 1) * 512]))
                    P.copy("vector", fsb(fsb.h[:, n * 512:(n + 1) * 512]), pb(pb.h[0:16, :]))
                P.dma("sync", fd_d(fd_d.h.ap()), fsb(fsb.h[:, :]))
                j128 = sbt(s2, "j128b", [128, 128], BF16)
                P.copy("vector", j128(j128.h[:, :]), cv("j128"))
                for dl in range(6):
                    src = bass.AP(fd_d.h, 2048 + dl * 128 - 127, [[1, 128], [4096, 16], [1, 128]])
                    P.dma("sync", hk(hk.h[:, :, :]), fd_d(src))
                    for n in range(4):
                        pb = nb()
                        P.mm(pb(pb.h[:, :]), j128(j128.h[:, :]), hk(hk.h[:, n * 4:(n + 1) * 4, :]))
                        P.copy("vector", dall(dall.h[:, dl, n * 4:(n + 1) * 4, :]), pb(pb.h[:, :]))
                for h in range(16):
                    P.tt("vector", dall(dall.h[:, 6, h, :]), dall(dall.h[:, 5, h, :]), cv("m6"), ALU.mult)
                    P.tt("vector", dall(dall.h[:, 6, h, :]), dall(dall.h[:, 6, h, :]), cv("n6"), ALU.add)
            P.barrier()
            return dall

        def mixer(l, xsrc, xdst):
            w_in = dr["w_in"]

            def wcols(c0, n):
                return w_in(w_in.h.ap()[l, :, c0:c0 + n].rearrange("(c p) f -> p c f", p=128))

            with ExitStack() as sq_:
                qT = sbt(sq_, "m_qT", [128, 8, S], BF16)
                with ExitStack() as st:
                    ksd = sbt(st, "m_ksd", [128, 4, S], BF16)
                    kwd = sbt(st, "m_kwd", [128, 4, S], BF16)
                    kcT = sbt(st, "m_kcT", [128, 2, S], BF16)
                    vcT = sbt(st, "m_vcT", [128, 2, S], BF16)
                    vsa = sbt(st, "m_vsa", [128, 16, 4, 65], BF16)
                    vwa = sbt(st, "m_vwa", [128, 16, 4, 65], BF16)
                    gat = sbt(st, "m_gat", [128, 16, 48], F32)
                    P.memset("vector", vsa(vsa.h[:, :, :, 64:65]), 1.0)
                    P.memset("vector", vwa(vwa.h[:, :, :, 64:65]), 1.0)
                    with ExitStack() as sa:
                        xt = sbt(sa, "a_x", [128, 16, 512], F32)
                        ht = sbt(sa, "a_h", [128, 16, 512], BF16)
                        sq = [sbt(sa, "a_sq%d" % i, [128, 512], BF16) for i in range(2)]
                        sd = sbt(sa, "a_sd", [128, 512], F32)
                        rs = sbt(sa, "a_rs", [128, 512], F32)
                        wq = [sbt(sa, "a_wq%d" % i, [128, 16, 128], BF16) for i in range(3)]
                        wv = sbt(sa, "a_wv", [128, 16, 560], BF16)
                        wi = 0
                        for tb in range(4):
                            t0 = tb * 512
                            make_h((xt, 0), (ht, 0), xsrc, l, "nm", t0, 0, sq, sd)
                            hv = lambda c: ht(ht.h[:, c, :], ("h", 0))
                            jobs = [("q", c) for c in range(8)] + [("ks", g) for g in range(4)] + \
                                   [("kw", g) for g in range(4)] + [("kc", c) for c in range(2)] + [("vc", c) for c in range(2)]
                            for jn, (kind, ix) in enumerate(jobs):
                                w = wq[wi % 3]
                                wi += 1
                                wall = w(w.h[:, :, :])
                                if kind == "q":
                                    lds = [(wall, wcols(ix * 128, 128))]
                                elif kind in ("ks", "kw"):
                                    c0 = OFF_KV + (2 if kind == "ks" else 4) * 256 + ix * 64
                                    lds = [(w(w.h[:, :, 0:64]), wcols(c0, 64)), (w(w.h[:, :, 64:128]), wcols(c0, 64))]
                                else:
                                    c0 = OFF_KV + (0 if kind == "kc" else 1) * 256 + ix * 128
                                    lds = [(wall, wcols(c0, 128))]
                                wfetch(tb, wall, wc16_d, wc16_d.h.ap()[jn], jn, lds)
                                wvw = wall
                                pz = nb()
                                for c in range(16):
                                    P.mm(pz(pz.h[:, :]), V(w.h[:, c, :], wvw.trks), hv(c), start=(c == 0), stop=(c == 15))
                                if kind in ("kc", "vc"):
                                    dst = kcT if kind == "kc" else vcT
                                    P.copy("scalar", dst(dst.h[:, ix, t0:t0 + 512], tb), pz(pz.h[:, :]))
                                    continue
                                s = sq[wi % 2]
                                P.act(s(s.h[:, :]), pz(pz.h[:, :]), AF.Square)
                                pm = nb()
                                P.mm(pm(pm.h[:, :]), blk_bf(blk_bf.h[:, :]), s(s.h[:, :]))
                                P.act(rs(rs.h[:, :]), pm(pm.h[:, :]), AF.Sqrt, bias=epsc(epsc.h[:, 0:1]), scale=1.0 / 64)
                                P.recip(rs(rs.h[:, :]), rs(rs.h[:, :]))
                                if kind == "q":
                                    dstv = qT(qT.h[:, ix, t0:t0 + 512], tb)
                                    gc = gsc(gsc.h[:, l, 49:50])
                                    P.stt("vector", dstv, pz(pz.h[:, :]), gc, rs(rs.h[:, :]), ALU.mult, ALU.mult)
                                else:
                                    dst = ksd if kind == "ks" else kwd
                                    kn = 1 if kind == "ks" else 2
                                    gc = prm(prm.h[:, l, PRM["kn"] + kn:PRM["kn"] + kn + 1])
                                    P.stt("vector", dst(dst.h[:, ix, t0:t0 + 512], tb), pz(pz.h[:, :]), gc, rs(rs.h[:, :]), ALU.mult, ALU.mult)
                            wfetch(tb, wv(wv.h[:, :, :]), wcv_d, wcv_d.h.ap(), 0,
                                   [(wv(wv.h[:, :, 0:256]), wcols(OFF_KV + 3 * 256, 256)),
                                    (wv(wv.h[:, :, 256:512]), wcols(OFF_KV + 5 * 256, 256)),
                                    (wv(wv.h[:, :, 512:560]), wcols(OFF_NG, 48))])
                            wvt = wv(wv.h[:, :, :]).trks
                            for sub in range(4):
                                ti = tb * 4 + sub
                                p1, p2 = nb(), nb()
                                for c in range(16):
                                    P.mm(p1(p1.h[:, :]), ht(ht.h[:, c, sub * 128:(sub + 1) * 128], ("h", 0)), V(wv.h[:, c, 0:512], wvt),
                                         start=(c == 0), stop=(c == 15))
                                for c in range(16):
                                    P.mm(p2(p2.h[:, 0:48]), ht(ht.h[:, c, sub * 128:(sub + 1) * 128], ("h", 0)), V(wv.h[:, c, 512:560], wvt),
                                         start=(c == 0), stop=(c == 15))
                                P.copy("vector", vsa(vsa.h[:, ti, :, 0:64], ti), p1(p1.h[:, 0:256].rearrange("p (g d) -> p g d", g=4)))
                                P.copy("vector", vwa(vwa.h[:, ti, :, 0:64], ti), p1(p1.h[:, 256:512].rearrange("p (g d) -> p g d", g=4)))
                                P.act(gat(gat.h[:, ti, :], ti), p2(p2.h[:, 0:48]), AF.Sigmoid)
                    P.barrier()
                    if mstop != "A":
                        dall = build_tables(st)
                    if mstop in ("A", "T"):
                        with ExitStack() as sx:
                            xc = sbt(sx, "passx", [128, 16, 512], F32)
                            for tb in range(4):
                                P.dma("sync", xc(xc.h[:, :, :]), xsrc(xsrc.h.ap()[:, :, tb * 512:(tb + 1) * 512].rearrange("c p t -> p c t"), tb))
                                P.dma("sync", xdst(xdst.h.ap()[:, :, tb * 512:(tb + 1) * 512].rearrange("c p t -> p c t"), tb), xc(xc.h[:, :, :]))
                            P.barrier()
                        return
                    attention(l, st, qT, ksd, kwd, kcT, vcT, vsa, vwa, gat, dall)
                    bank_n[0] = 8
                P.barrier()
                if mstop == "X":
                    with ExitStack() as sx:
                        xc = sbt(sx, "passx2", [128, 16, 512], F32)
                        for tb in range(4):
                            P.dma("sync", xc(xc.h[:, :, :]), xsrc(xsrc.h.ap()[:, :, tb * 512:(tb + 1) * 512].rearrange("c p t -> p c t"), tb))
                            P.dma("sync", xdst(xdst.h.ap()[:, :, tb * 512:(tb + 1) * 512].rearrange("c p t -> p c t"), tb), xc(xc.h[:, :, :]))
                        P.barrier()
                    return
                stage_d(l, qT, xsrc, xdst, wcols)
            P.barrier()

        def dump(name, view_fn_list):
            pass

        def dump_attn_inputs(qT, ksd, kwd, kcT, vcT, vsa, vwa, gat, dall):
            with ExitStack() as sd_:
                tmp = sbt(sd_, "dbg_tmp", [128, 8 * 512], F32)
                if "qT" in dbg_d:
                    o = dbg_d["qT"]
                    P.copy("vector", tmp(tmp.h[:, :].rearrange("p (c t) -> p c t", c=8)), qT(qT.h[:, :, 0:512], 0))
                    P.dma("sync", o(o.h.ap()), tmp(tmp.h[:, :]))
                if "ksd" in dbg_d:
                    o = dbg_d["ksd"]
                    t2 = sbt(sd_, "dbg_t2", [128, 4 * 512], F32)
                    P.copy("vector", t2(t2.h[:, :].rearrange("p (c t) -> p c t", c=4)), ksd(ksd.h[:, :, 0:512], 0))
                    P.dma("sync", o(o.h.ap()), t2(t2.h[:, :]))
                if "gat" in dbg_d:
                    o = dbg_d["gat"]
                    P.dma("sync", o(o.h.ap()), gat(gat.h[:, :, :].rearrange("p a b -> p (a b)") if False else gat.h[:, 0, :], 0))
                if "dall" in dbg_d:
                    o = dbg_d["dall"]
                    t3 = sbt(sd_, "dbg_t3", [128, 7 * 128], F32)
                    P.copy("vector", t3(t3.h[:, :].rearrange("p (c t) -> p c t", c=7)), dall(dall.h[:, :, 5, :]))
                    P.dma("sync", o(o.h.ap()), t3(t3.h[:, :]))

        def attention(l, st, qT, ksd, kwd, kcT, vcT, vsa, vwa, gat, dall):
            with ExitStack() as sc:
                kcn = sbt(sc, "c_kcn", [128, 4, 128], BF16)
                vcx = sbt(sc, "c_vcx", [128, 4, 97], BF16)
                P.memset("vector", vcx(vcx.h[:, :, 64:65]), 1.0)
                for g in range(4):
                    P.copy("vector", vcx(vcx.h[0:127, g, 65:97], "ov"), cv("ov", rows=127))
                expb = sbt(sc, "c_expb", [32, 16, 128], BF16)
                P.dma("gpsimd", expb(expb.h[:, :, :]), expm_d(expm_d.h.ap().rearrange("p (a b) -> p a b", a=16)))
                j127 = sbt(sc, "c_j127", [128, 128], BF16)
                P.copy("vector", j127(j127.h[:, :]), cv("j127"))
                with ExitStack() as s2:
                    w1 = sbt(s2, "c_w1", [128, 32, 256], BF16)
                    w2 = sbt(s2, "c_w2", [128, 2, 128], BF16)
                    posb = sbt(s2, "c_posb", [128, 32], BF16)
                    pbias = sbt(s2, "c_pbias", [128, 2], F32)
                    hid = sbt(s2, "c_hid", [128, 2, 128], BF16)
                    sqc = sbt(s2, "c_sq", [128, 128], BF16)
                    rsc = sbt(s2, "c_rs", [128, 128], F32)
                    for kind in ("k", "v"):
                        w1d = dr["cmp_%s_w1" % kind]
                        w2d = dr["cmp_%s_w2" % kind]
                        src1 = w1d.h.ap()[l].rearrange("(l d) f -> d l f", d=64)
                        P.dma("gpsimd", w1(w1.h[0:64, :, :], "a"), w1d(src1))
                        P.dma("gpsimd", w1(w1.h[64:128, :, :], "b"), w1d(src1))
                        src2 = w2d.h.ap()[l].rearrange("(fh p) d -> p fh d", p=128)
                        P.dma("gpsimd", w2(w2.h[:, :, 0:64], "a"), w2d(src2))
                        P.dma("gpsimd", w2(w2.h[:, :, 64:128], "b"), w2d(src2))
                        w1t = w1(w1.h[:, :, :], "a", "b").trks
                        w2t = w2(w2.h[:, :, :], "a", "b").trks
                        pn = "pk" if kind == "k" else "pv"
                        P.copy("vector", posb(posb.h[:, :]), prm(prm.h[:, l, PRM[pn]:PRM[pn] + 32]))
                        for fh in range(2):
                            pb = nb()
                            for ll in range(32):
                                P.mm(pb(pb.h[:, 0:1]), V(w1.h[0:64, ll, fh * 128:(fh + 1) * 128], w1t), posb(posb.h[0:64, ll:ll + 1]),
                                     start=(ll == 0), stop=(ll == 31))
                            P.copy("vector", pbias(pbias.h[:, fh:fh + 1]), pb(pb.h[:, 0:1]))
                        srcT = kcT if kind == "k" else vcT
                        for g in range(4):
                            hf, ch = g % 2, g // 2
                            for fh in range(2):
                                pb = nb()
                                for ll in range(32):
                                    r16 = srcT.h[hf * 64:(hf + 1) * 64, ch, :].rearrange("p (c s) -> p c s", s=16)
                                    rhs = r16[:, 0:127, ll] if ll < 16 else r16[:, 1:128, ll - 16]
                                    P.mm(pb(pb.h[:, 0:127]), V(w1.h[hf * 64:(hf + 1) * 64, ll, fh * 128:(fh + 1) * 128], w1t),
                                         srcT(rhs, 0, 1, 2, 3), start=(ll == 0), stop=(ll == 31))
                                P.act(hid(hid.h[:, fh, 0:127], fh), pb(pb.h[:, 0:127]), AF.Silu, bias=pbias(pbias.h[:, fh:fh + 1]))
                            if kind == "k":
                                pb = nb()
                                for fh in range(2):
                                    P.mm(pb(pb.h[:, 0:127]), V(w2.h[:, fh, :], w2t), hid(hid.h[:, fh, 0:127], fh), start=(fh == 0), stop=(fh == 1))
                                P.act(sqc(sqc.h[:, 0:127]), pb(pb.h[:, 0:127]), AF.Square)
                                pm = nb()
                                P.mm(pm(pm.h[:, 0:127]), blk_bf(blk_bf.h[:, :]), sqc(sqc.h[:, 0:127]))
                                P.act(rsc(rsc.h[:, 0:127]), pm(pm.h[:, 0:127]), AF.Sqrt, bias=epsc(epsc.h[:, 0:1]), scale=1.0 / 64)
                                P.recip(rsc(rsc.h[:, 0:127]), rsc(rsc.h[:, 0:127]))
                                gc = prm(prm.h[:, l, PRM["kn"]:PRM["kn"] + 1])
                                P.stt("vector", kcn(kcn.h[:, g, 0:127], g), pb(pb.h[:, 0:127]), gc, rsc(rsc.h[:, 0:127]), ALU.mult, ALU.mult)
                            else:
                                pb = nb()
                                for fh in range(2):
                                    P.mm(pb(pb.h[0:127, 0:64]), hid(hid.h[:, fh, 0:127], fh), V(w2.h[:, fh, 0:64], w2t), start=(fh == 0), stop=(fh == 1))
                                P.copy("vector", vcx(vcx.h[0:127, g, 0:64], ("v", g)), pb(pb.h[0:127, 0:64]))
                P.barrier()
                bct = [sbt(sc, "t_bch%d" % i, [128, 4, 128], BF16) for i in range(4)]
                bcf = [sbt(sc, "t_bcf%d" % i, [128, 4, 128], BF16) for i in range(4)]
                pT = [sbt(sc, "t_pT%d" % i, [128, 512], BF16) for i in range(5)]
                aaccs = [sbt(sc, "t_aacc%d" % k, [128, 16, 64], F32) for k in range(2)]
                abf = sbt(sc, "t_abf", [128, 1024], BF16)
                rden = sbt(sc, "t_rden", [128, 4], F32)
                coef = sbt(sc, "t_coef", [128, 4], F32)
                imp = sbt(sc, "t_imp", [128, 32], F32)
                scr = sbt(sc, "t_scr", [128, 32], F32)
                wrk = sbt(sc, "t_wrk", [128, 32], F32)
                m8 = sbt(sc, "t_m8", [128, 8], F32)
                selnss = [[sbt(sc, "t_seln%d_%d" % (k, g), [128, 32], F32) for g in range(4)] for k in range(2)]
                selTs = [sbt(sc, "t_selT%d" % g, [32, 2, 128], BF16) for g in range(4)]
                bank_n[0] = 6
                cnt = {"p": 0, "b": 0, "a": 0}
                accb = [banks[6], banks[7]]

                def nacc():
                    b = accb[cnt["a"] % 2]
                    cnt["a"] += 1
                    return b

                o1, o2, o3 = CST["notf"][0], CST["addv"][0], CST["vneg"][0]

                def hd(g, s):
                    return 4 * g + 2 * (s % 2) + s // 2

                def tile_fns(i):
                    q0 = i * 128
                    aacc = aaccs[i % 2]
                    selns = selnss[i % 2]

                    def score(g, klhs, rows, bias_fn, mask_j):
                        p = pT[cnt["p"] % 5]
                        cnt["p"] += 1
                        pss = (nb(), nb())
                        idv = ident_bf(ident_bf.h[0:rows, 0:rows])
                        for hf in range(2):
                            rhs = qT(qT.h[hf * 64:(hf + 1) * 64, 2 * g:2 * g + 2, q0:q0 + 128], i // 4)
                            P.mm(pss[hf](pss[hf].h[0:rows, 0:256]), klhs(hf), rhs, start=True, stop=False)
                        for hf in range(2):
                            P.mm(pss[hf](pss[hf].h[0:rows, 0:256]), idv, bias_fn(hf), start=False, stop=(mask_j is None))
                        if mask_j is not None:
                            sT = selTs[g]
                            for hf in range(2):
                                P.mm(pss[hf](pss[hf].h[0:rows, 0:256]), expb(expb.h[:, mask_j, :]), sT(sT.h[:, :, :]), start=False, stop=True)
                        for hf in range(2):
                            P.act(p(p.h[0:rows, hf * 256:(hf + 1) * 256], hf), pss[hf](pss[hf].h[0:rows, 0:256]), AF.Exp)
                        return p

                    def combine(g, br, pbk, first):
                        w = 97 if br == 0 else 65
                        den = V(pbk.h[:, 0:4 * w].rearrange("p (s c) -> p s c", s=4)[:, :, 64], pbk(pbk.h[:, :]).trks)
                        P.ts("vector", rden(rden.h[:, :]), den, 1e-30, None, ALU.max)
                        P.recip(rden(rden.h[:, :]), rden(rden.h[:, :]))
                        for s in range(4):
                            h = hd(g, s)
                            if br == 0:
                                if s == 0:
                                    P.ts("vector", imp(imp.h[:, :]), pbk(pbk.h[:, s * 97 + 65:s * 97 + 97]), rden(rden.h[:, s:s + 1]), None, ALU.mult)
                                else:
                                    P.stt("vector", imp(imp.h[:, :]), pbk(pbk.h[:, s * 97 + 65:s * 97 + 97]), rden(rden.h[:, s:s + 1]),
                                          imp(imp.h[:, :]), ALU.mult, ALU.add)
                            P.tt("vector", coef(coef.h[:, s:s + 1]), rden(rden.h[:, s:s + 1]), gat(gat.h[:, i, 3 * h + br:3 * h + br + 1], i), ALU.mult)
                            if first:
                                P.ts("vector", aacc(aacc.h[:, h, :], h), pbk(pbk.h[:, s * w:s * w + 64]), coef(coef.h[:, s:s + 1]), None, ALU.mult)
                            else:
                                P.stt("vector", aacc(aacc.h[:, h, :], h), pbk(pbk.h[:, s * w:s * w + 64]), coef(coef.h[:, s:s + 1]),
                                      aacc(aacc.h[:, h, :], h), ALU.mult, ALU.add)

                    def branch(g, steps, pbk, vbuf, mask):
                        LA = 2
                        ps_ = [None] * len(steps)
                        for n in range(min(LA, len(steps))):
                            ps_[n] = steps[n][1]()
                        for n, (j, _) in enumerate(steps):
                            if n + LA < len(steps):
                                ps_[n + LA] = steps[n + LA][1]()
                            p = ps_[n]
                            for s in range(4):
                                P.mm(pbk(pbk.h[:, s * 65:(s + 1) * 65]), p(p.h[:, s * 128:(s + 1) * 128], s // 2), vbuf(vbuf.h[:, j, g, :], j),
                                     start=(n == 0 and s == 0), stop=(n == len(steps) - 1 and s == 3))


                    def phase1():
                        for g in range(4):
                            bh, bf_ = bct[cnt["b"] % 4], bcf[cnt["b"] % 4]
                            cnt["b"] += 1
                            src = bass.AP(fd_d.h, 4 * g * 4096 + 2048 + q0 - 16 * 126 - 31, [[16, 128], [4096, 4], [1, 128]])
                            P.dma("sync", bh(bh.h[:, :, :]), fd_d(src))
                            pb = nb()
                            P.mm(pb(pb.h[0:127, :]), j127(j127.h[0:127, 0:127]), bh(bh.h[0:127, :, :]))
                            P.copy("scalar", bf_(bf_.h[0:127, :, :]), pb(pb.h[0:127, :]))
                            p = score(g, lambda hf: kcn(kcn.h[hf * 64:(hf + 1) * 64, g, 0:127], g), 127,
                                      lambda hf: bf_(bf_.h[0:127, 2 * hf:2 * hf + 2, :]), None)
                            po = nacc()
                            for s in range(4):
                                P.mm(po(po.h[:, s * 97:(s + 1) * 97]), p(p.h[0:127, s * 128:(s + 1) * 128], s // 2),
                                     vcx(vcx.h[0:127, g, :], "ov", ("v", g), 0), start=(s == 0), stop=(s == 3))
                            combine(g, 0, po, True)
                            P.tt("vector", scr(scr.h[:, :]), imp(imp.h[:, :]), cst(cst.h[:, o1 + i * 32:o1 + (i + 1) * 32]), ALU.mult)
                            P.tt("vector", scr(scr.h[:, :]), scr(scr.h[:, :]), cst(cst.h[:, o2 + i * 32:o2 + (i + 1) * 32]), ALU.add)
                            P.op("vector", lambda e: e.max(m8.h[:, :], scr.h[:, :]), [scr(scr.h[:, :])], [m8(m8.h[:, :])])
                            P.op("vector", lambda e: e.match_replace(wrk.h[:, :], m8.h[:, :], scr.h[:, :], -3e30),
                                 [scr(scr.h[:, :]), m8(m8.h[:, :])], [wrk(wrk.h[:, :])])
                            P.op("vector", lambda e: e.max(m8.h[:, :], wrk.h[:, :]), [wrk(wrk.h[:, :])], [m8(m8.h[:, :])])
                            P.op("vector", lambda e: e.match_replace(wrk.h[:, :], m8.h[:, :], wrk.h[:, :], -3e30),
                                 [wrk(wrk.h[:, :]), m8(m8.h[:, :])], [wrk(wrk.h[:, :])])
                            seln = selns[g]
                            P.tt("vector", seln(seln.h[:, :]), scr(scr.h[:, :]), wrk(wrk.h[:, :]), ALU.subtract)
                            P.ts("vector", seln(seln.h[:, :]), seln(seln.h[:, :]), 1.0, -1.0, ALU.min, ALU.add)
                            P.stt("vector", seln(seln.h[:, :]), seln(seln.h[:, :]), 30000.0, cst(cst.h[:, o3 + i * 32:o3 + (i + 1) * 32]),
                                  ALU.mult, ALU.add)

                    def phase23():
                        for g in range(4):
                            pt = nb()
                            P.transpose(pt(pt.h[0:32, 0:128]), selns[g](selns[g].h[:, :]), cv("ident"))
                            sT = selTs[g]
                            for s in range(2):
                                P.copy("vector", sT(sT.h[:, s, :]), pt(pt.h[0:32, 0:128]))
                        jl = max(0, i - 4)
                        segs = []
                        for g in range(4):
                            steps = []
                            for j in range(jl, i + 1):
                                dw = 6 if i - j == 4 else i - j
                                steps.append((j, (lambda j=j, dw=dw, g=g: score(
                                    g, lambda hf: kwd(kwd.h[hf * 64:(hf + 1) * 64, g, j * 128:(j + 1) * 128], j // 4), 128,
                                    lambda hf: dall(dall.h[:, dw, 4 * g + 2 * hf:4 * g + 2 * hf + 2, :]), None))))
                            segs.append((g, steps, vwa, 2))
                        for g in range(4):
                            steps = []
                            for j in range(i + 1):
                                dl = min(i - j, 5)
                                steps.append((j, (lambda j=j, dl=dl, g=g: score(
                                    g, lambda hf: ksd(ksd.h[hf * 64:(hf + 1) * 64, g, j * 128:(j + 1) * 128], j // 4), 128,
                                    lambda hf: dall(dall.h[:, dl, 4 * g + 2 * hf:4 * g + 2 * hf + 2, :]), j))))
                            segs.append((g, steps, vsa, 1))
                        flat = [(si, n, j, fn) for si, (g_, steps_, vb_, br_) in enumerate(segs) for n, (j, fn) in enumerate(steps_)]
                        LA = 2
                        ps_ = {}
                        accs = {}
                        for n in range(min(LA, len(flat))):
                            ps_[n] = flat[n][3]()
                        for n, (si, sn, j, fn) in enumerate(flat):
                            if n + LA < len(flat):
                                ps_[n + LA] = flat[n + LA][3]()
                            g_, steps_, vb_, br_ = segs[si]
                            if sn == 0:
                                accs[si] = nacc()
                            pbk = accs[si]
                            p = ps_.pop(n)
                            lastn = (sn == len(steps_) - 1)
                            for s in range(4):
                                P.mm(pbk(pbk.h[:, s * 65:(s + 1) * 65]), p(p.h[:, s * 128:(s + 1) * 128], s // 2), vb_(vb_.h[:, j, g_, :], j),
                                     start=(sn == 0 and s == 0), stop=(lastn and s == 3))
                            if lastn:
                                combine(g_, br_, pbk, False)
                        for c4 in range(2):
                            pt = nb()
                            for cc in range(4):
                                c = c4 * 4 + cc
                                P.transpose(pt(pt.h[:, cc * 128:(cc + 1) * 128]),
                                            aacc(aacc.h[:, 2 * c:2 * c + 2, :].rearrange("p h d -> p (h d)"), 2 * c, 2 * c + 1), cv("ident"))
                            P.copy("scalar", qT(qT.h[:, c4 * 4:(c4 + 1) * 4, q0:q0 + 128], ("a", i), i // 4),
                                   pt(pt.h[:, :].rearrange("p (c t) -> p c t", c=4)))

                    return phase1, phase23

                fns = [tile_fns(i) for i in range(ntiles)]
                if ntiles:
                    fns[0][0]()
                for i in range(ntiles):
                    if i + 1 < ntiles:
                        fns[i + 1][0]()
                    fns[i][1]()


        def stage_d(l, aT, xsrc, xdst, wcols):
            wpn, wps, wo = dr["w_proj_nsa"], dr["w_proj_sgu"], dr["w_out"]
            with ExitStack() as st:
                xt = sbt(st, "d_x", [128, 16, 512], F32)
                ht = sbt(st, "d_h", [128, 16, 512], BF16)
                sq = [sbt(st, "d_sq%d" % i, [128, 512], BF16) for i in range(2)]
                sd = sbt(st, "d_sd", [128, 512], F32)
                uT = sbt(st, "d_uT", [128, 8, 512], BF16)
                sgT = sbt(st, "d_sgT", [128, 8, 512], BF16)
                mrg = sbt(st, "d_mrg", [128, 16, 512], BF16)
                w16 = [sbt(st, "d_w16_%d" % i, [128, 16, 128], BF16) for i in range(4)]
                w8 = [sbt(st, "d_w8_%d" % i, [128, 8, 128], BF16) for i in range(4)]
                wv2 = sbt(st, "d_wv2", [128, 16, 512], BF16)
                vg = sbt(st, "d_vg", [128, 1024], F32)
                vsq = sbt(st, "d_vsq", [128, 1024], F32)
                vln = sbt(st, "d_vln", [128, 1024], BF16)
                stat = sbt(st, "d_stat", [128, 8], F32)
                lng = sbt(st, "d_lng", [128, 1024], F32)
                lnb = sbt(st, "d_lnb", [128, 1024], F32)
                wsT = sbt(st, "d_wsT", [128, 8, 128], BF16)
                wsf = sbt(st, "d_wsf", [128, 8, 128], F32)
                sbb = sbt(st, "d_sbb", [1, 1024], BF16)
                sg1 = sbt(st, "d_sg1", [128, 512], F32)
                sg2 = sbt(st, "d_sg2", [128, 512], F32)
                t1 = sbt(st, "d_t1", [128, 512], F32)
                P.dma("sync", lng(lng.h[:, :]), sgb_d(sgb_d.h.ap()[l, 0]))
                P.dma("sync", lnb(lnb.h[:, :]), sgb_d(sgb_d.h.ap()[l, 1]))
                P.dma("sync", wsf(wsf.h[:, :, :]), sgw_d(sgw_d.h.ap()[l]))
                for g in range(8):
                    P.tt("vector", wsT(wsT.h[:, g, :]), wsf(wsf.h[:, g, :]), cv("triu"), ALU.mult)
                P.dma("gpsimd", sbb(sbb.h[:, :]), sgub_d(sgub_d.h.ap()[l]))
                k16 = 0
                k8 = 0
                for tb in range(4):
                    t0 = tb * 512
                    make_h((xt, 0), (ht, 0), xsrc, l, "nm", t0, 0, sq, sd)
                    for c in range(8):
                        w = w16[k16 % 4]
                        k16 += 1
                        wfetch(tb, w(w.h[:, :, :]), wc16_d, wc16_d.h.ap()[20 + c], 20 + c, [(w(w.h[:, :, :]), wcols(OFF_UV + c * 128, 128))])
                        pz = nb()
                        for k in range(16):
                            P.mm(pz(pz.h[:, :]), w(w.h[:, k, :]), ht(ht.h[:, k, :], ("h", 0)), start=(k == 0), stop=(k == 15))
                        P.act(uT(uT.h[:, c, :], c), pz(pz.h[:, :]), AF.Gelu_apprx_tanh)
                    for sub in range(4):
                        for half in range(2):
                            wfetch(tb if sub == 0 else 1, wv2(wv2.h[:, :, :]), wcv2_d, wcv2_d.h.ap()[half], half,
                                   [(wv2(wv2.h[:, :, :]), wcols(OFF_UV + 1024 + half * 512, 512))])
                            pz = nb()
                            for k in range(16):
                                P.mm(pz(pz.h[:, :]), ht(ht.h[:, k, sub * 128:(sub + 1) * 128], ("h", 0)), wv2(wv2.h[:, k, :]),
                                     start=(k == 0), stop=(k == 15))
                            P.act(vg(vg.h[:, half * 512:(half + 1) * 512], half), pz(pz.h[:, :]), AF.Gelu_apprx_tanh)
                        vga = vg(vg.h[:, :], 0, 1)
                        P.op("vector", lambda e: e.tensor_reduce(stat.h[:, 0:1], vg.h[:, :], AX.X, ALU.add), [vga], [stat(stat.h[:, 0:1], 0)])
                        P.act(vsq(vsq.h[:, :]), vga, AF.Square)
                        P.op("vector", lambda e: e.tensor_reduce(stat.h[:, 1:2], vsq.h[:, :], AX.X, ALU.add), [vsq(vsq.h[:, :])], [stat(stat.h[:, 1:2], 1)])
                        P.ts("vector", stat(stat.h[:, 2:3], 2), stat(stat.h[:, 0:1], 0), 1.0 / 1024, None, ALU.mult)
                        P.tt("vector", stat(stat.h[:, 3:4], 3), stat(stat.h[:, 2:3], 2), stat(stat.h[:, 2:3], 2), ALU.mult)
                        P.stt("vector", stat(stat.h[:, 4:5], 4), stat(stat.h[:, 1:2], 1), 1.0 / 1024, stat(stat.h[:, 3:4], 3), ALU.mult, ALU.subtract)
                        P.act(stat(stat.h[:, 5:6], 5), stat(stat.h[:, 4:5], 4), AF.Sqrt, bias=epsc(epsc.h[:, 0:1]), scale=1.0)
                        P.recip(stat(stat.h[:, 6:7], 6), stat(stat.h[:, 5:6], 5))
                        P.ts("vector", vsq(vsq.h[:, :]), vga, stat(stat.h[:, 2:3], 2), stat(stat.h[:, 6:7], 6), ALU.subtract, ALU.mult)
                        P.tt("vector", vsq(vsq.h[:, :]), vsq(vsq.h[:, :]), lng(lng.h[:, :]), ALU.mult)
                        P.tt("vector", vln(vln.h[:, :]), vsq(vsq.h[:, :]), lnb(lnb.h[:, :]), ALU.add)
                        for gh in range(2):
                            pz = nb()
                            for gg in range(4):
                                g = gh * 4 + gg
                                P.mm(pz(pz.h[:, gg * 128:(gg + 1) * 128]), vln(vln.h[:, g * 128:(g + 1) * 128]), wsT(wsT.h[:, g, :]), start=(gg == 0), stop=False)
                                P.mm(pz(pz.h[:, gg * 128:(gg + 1) * 128]), ones_bf(ones_bf.h[0:1, :]), sbb(sbb.h[0:1, g * 128:(g + 1) * 128]), start=False, stop=(gg == 3))
                            P.tt("vector", sgT(sgT.h[:, gh * 4:(gh + 1) * 4, sub * 128:(sub + 1) * 128], (gh, sub)),
                                 pz(pz.h[:, :].rearrange("p (g t) -> p g t", g=4)),
                                 uT(uT.h[:, gh * 4:(gh + 1) * 4, sub * 128:(sub + 1) * 128], *range(gh * 4, gh * 4 + 4)), ALU.mult)
                    sgk = [(gh, sub) for gh in range(2) for sub in range(4)]
                    for j in range(16):
                        wa, wb_ = w8[k8 % 4], w8[(k8 + 1) % 4]
                        k8 += 2
                        wg1, wg2 = w16[k16 % 4], w16[(k16 + 1) % 4]
                        k16 += 2
                        wfetch(tb, wa(wa.h[:, :, :]), wc8_d, wc8_d.h.ap()[2 * j], 2 * j,
                               [(wa(wa.h[:, :, :]), wpn(wpn.h.ap()[l, :, j * 128:(j + 1) * 128].rearrange("(c p) f -> p c f", p=128)))])
                        wfetch(tb, wb_(wb_.h[:, :, :]), wc8_d, wc8_d.h.ap()[2 * j + 1], 2 * j + 1,
                               [(wb_(wb_.h[:, :, :]), wps(wps.h.ap()[l, :, j * 128:(j + 1) * 128].rearrange("(c p) f -> p c f", p=128)))])
                        wfetch(tb, wg1(wg1.h[:, :, :]), wc16_d, wc16_d.h.ap()[28 + 2 * j], 28 + 2 * j, [(wg1(wg1.h[:, :, :]), wcols(OFF_MG + j * 128, 128))])
                        wfetch(tb, wg2(wg2.h[:, :, :]), wc16_d, wc16_d.h.ap()[29 + 2 * j], 29 + 2 * j, [(wg2(wg2.h[:, :, :]), wcols(OFF_MG + D + j * 128, 128))])
                        pa, pb, pg1, pg2 = nb(), nb(), nb(), nb()
                        for k in range(8):
                            P.mm(pa(pa.h[:, :]), wa(wa.h[:, k, :]), aT(aT.h[:, k, t0:t0 + 512], *[("a", i) for i in range(tb * 4, tb * 4 + 4)]),
                                 start=(k == 0), stop=(k == 7))
                        for k in range(8):
                            P.mm(pb(pb.h[:, :]), wb_(wb_.h[:, k, :]), sgT(sgT.h[:, k, :], *sgk), start=(k == 0), stop=(k == 7))
                        for k in range(16):
                            P.mm(pg1(pg1.h[:, :]), wg1(wg1.h[:, k, :]), ht(ht.h[:, k, :], ("h", 0)), start=(k == 0), stop=(k == 15))
                        for k in range(16):
                            P.mm(pg2(pg2.h[:, :]), wg2(wg2.h[:, k, :]), ht(ht.h[:, k, :], ("h", 0)), start=(k == 0), stop=(k == 15))
                        P.act(sg1(sg1.h[:, :]), pg1(pg1.h[:, :]), AF.Sigmoid)
                        P.act(sg2(sg2.h[:, :]), pg2(pg2.h[:, :]), AF.Sigmoid)
                        P.tt("vector", t1(t1.h[:, :]), sg1(sg1.h[:, :]), pa(pa.h[:, :]), ALU.mult)
                        P.tt("vector", sg2(sg2.h[:, :]), sg2(sg2.h[:, :]), pb(pb.h[:, :]), ALU.mult)
                        P.tt("vector", mrg(mrg.h[:, j, :], j), t1(t1.h[:, :]), sg2(sg2.h[:, :]), ALU.add)
                    for j in range(16):
                        w = w16[k16 % 4]
                        k16 += 1
                        wfetch(tb, w(w.h[:, :, :]), wc16_d, wc16_d.h.ap()[60 + j], 60 + j,
                               [(w(w.h[:, :, :]), wo(wo.h.ap()[l, :, j * 128:(j + 1) * 128].rearrange("(c p) f -> p c f", p=128)))])
                        pz = nb()
                        for k in range(16):
                            P.mm(pz(pz.h[:, :]), w(w.h[:, k, :]), mrg(mrg.h[:, k, :], *range(16)), start=(k == 0), stop=(k == 15))
                        xv = xt(xt.h[:, j, :], ("x", 0))
                        P.tt("vector", xv, xv, pz(pz.h[:, :]), ALU.add)
                    P.dma("sync", xdst(xdst.h.ap()[:, :, t0:t0 + 512].rearrange("c p t -> p c t"), tb), xt(xt.h[:, :, :], ("x", 0)))

        cur = x_in
        for l in range(depth):
            last = (l == depth - 1)
            todo = [p for p in "1m2" if p in phases]
            for p in todo:
                dst = out_d if (last and p == todo[-1]) else xs_d
                if p == "1":
                    ffn(l, "ffn1", cur, dst)
                elif p == "m":
                    mixer(l, cur, dst)
                else:
                    ffn(l, "ffn2", cur, dst)
                cur = xs_d
        P.barrier()
        P.emit(top)
    return nc


_NC_CACHE = {}


def _prep_shared(inputs):
    f = lambda a: np.ascontiguousarray(np.asarray(a, np.float32))
    sh = {}
    for k in ("ffn1_w_gate", "ffn1_w_up", "ffn1_w_down", "ffn2_w_gate", "ffn2_w_up", "ffn2_w_down", "w_in",
              "cmp_k_w1", "cmp_k_w2", "cmp_v_w1", "cmp_v_w2", "w_proj_nsa", "w_proj_sgu", "w_out"):
        sh[k] = f(inputs[k])
    sh["cst"] = CST_ARR
    prm = np.zeros((L, 128, NPRM), np.float32)
    for l in range(L):
        for nm, key in (("n1", "ffn1_norm"), ("nm", "mix_norm"), ("n2", "ffn2_norm")):
            prm[l, :, PRM[nm]:PRM[nm] + 16] = f(inputs[key])[l].reshape(16, 128).T
        prm[l, :, PRM["qn"]] = np.tile(f(inputs["q_norm"])[l], 2)
        for i in range(3):
            prm[l, :, PRM["kn"] + i] = np.tile(f(inputs["k_norm"])[l, i], 2)
        prm[l, :, PRM["pk"]:PRM["pk"] + 32] = np.tile(f(inputs["cmp_pos_k"])[l].T, (2, 1))
        prm[l, :, PRM["pv"]:PRM["pv"] + 32] = np.tile(f(inputs["cmp_pos_v"])[l].T, (2, 1))
    sh["prm"] = prm
    rb = f(inputs["rel_bias"])
    rbp = np.zeros((32, 16), np.float32)
    for g in range(4):
        for s in range(4):
            rbp[:, 4 * g + s] = rb[:, 4 * g + 2 * (s % 2) + s // 2]
    sh["rbp"] = rbp
    sh["oh"] = OH_ARR
    sh["expm"] = EXPM_ARR
    sh["sgub"] = np.ascontiguousarray(f(inputs["sgu_b"]).reshape(L, 1, 1024))
    sgb = np.zeros((L, 2, 128, 1024), np.float32)
    sgb[:, 0] = f(inputs["sgu_norm_g"])[:, None, :]
    sgb[:, 1] = f(inputs["sgu_norm_b"])[:, None, :]
    sh["sgb"] = sgb
    sh["sgwT"] = np.ascontiguousarray(f(inputs["sgu_w"]).transpose(0, 3, 1, 2))
    return sh


def kernel(**inputs):
    x = np.asarray(inputs["x"], np.float32)
    sh = _prep_shared(inputs)
    if "nc" not in _NC_CACHE:
        _NC_CACHE["nc"] = build_program()
    nc = _NC_CACHE["nc"]
    in_maps = []
    for b in range(NCORES):
        m = dict(sh)
        m["x_fm"] = np.ascontiguousarray(x[b].T.reshape(16, 128, S))
        in_maps.append(m)
    res = run_bass_kernel_spmd(nc, in_maps, core_ids=list(range(NCORES)))
    out = np.empty((NCORES, S, D), np.float32)
    for b in range(NCORES):
        out[b] = res.results[b]["out_fm"].reshape(D, S).T
    return out
```
